# Optimizing a Trainium2 kernel written in Bass

```python
import math
import jax
import jax.numpy as jnp
from jax import lax
import numpy as np

D_MODEL = 1024
BATCH = 8
SEQ = 4096
DEPTH = 2
DEC_BATCH = 32
DEC_SEQ = 2048
PAST_LEN = 128

MEM_LEN = 256
N_MIXERS = 2
N_SSD_LAYERS = (DEPTH + N_MIXERS - 1) // N_MIXERS
N_DIFF_LAYERS = DEPTH // N_MIXERS
EPS = 1e-6

SSD_D_INNER = 2 * D_MODEL
SSD_HEAD_DIM = 64
SSD_N_HEADS = SSD_D_INNER // SSD_HEAD_DIM
SSD_N_GROUPS = 8
SSD_D_STATE = 128
SSD_BC = SSD_N_GROUPS * SSD_D_STATE
SSD_XBC = SSD_D_INNER + 2 * SSD_BC
SSD_CONV_WIDTH = 5
SSD_CONV_PAD = SSD_CONV_WIDTH // 2
SSD_CHUNK = 128
SSD_MIX_COLS = SSD_D_INNER + SSD_XBC + 2 * SSD_N_HEADS

DIFF_N_HEADS = 8
DIFF_HEAD_DIM = D_MODEL // (2 * DIFF_N_HEADS)
DIFF_V_DIM = 2 * DIFF_HEAD_DIM
DIFF_QK_COLS = DIFF_N_HEADS * 2 * DIFF_HEAD_DIM
DIFF_WIDTH = DIFF_N_HEADS * DIFF_V_DIM
DIFF_MIX_COLS = 2 * DIFF_QK_COLS + DIFF_WIDTH
Q_BLOCK = 128

X_N_HEADS = 4
X_HEAD_DIM = 256
X_WIDTH = X_N_HEADS * X_HEAD_DIM

REL_BUCKETS = 32
REL_MAX_DIST = 128

D_FF = 4 * D_MODEL

SSD_IN_COLS = SSD_MIX_COLS + X_WIDTH
DIFF_IN_COLS = DIFF_MIX_COLS + X_WIDTH

kernel_name = 'hybrid_ssd_diffattn_memory_encoder'


def rms(x):
    xf = x.astype(jnp.float32)
    return (xf * lax.rsqrt(jnp.mean(xf * xf, axis=-1, keepdims=True) + EPS)).astype(x.dtype)


def rmsnorm(x, g):
    return rms(x) * g


def t5_bucket(rel):
    nb = REL_BUCKETS // 2
    max_exact = nb // 2
    ret = jnp.where(rel > 0, nb, 0)
    n = jnp.abs(rel)
    nf = jnp.maximum(n, 1).astype(jnp.float32)
    large = max_exact + (jnp.log(nf / max_exact) / math.log(REL_MAX_DIST / max_exact)
                         * (nb - max_exact)).astype(jnp.int32)
    large = jnp.minimum(large, nb - 1)
    return ret + jnp.where(n < max_exact, n, large)


def rel_bias(table, q_pos, k_pos):
    bucket = t5_bucket(k_pos[None, :] - q_pos[:, None])
    return jnp.transpose(table[bucket].astype(jnp.float32), (2, 0, 1))


def ssd_chunked_scan(x, dt, a, bm, cm):
    b, l, h, p = x.shape
    g, n = bm.shape[2], bm.shape[3]
    r = h // g
    nc = l // SSD_CHUNK

    def chunks(t):
        return jnp.moveaxis(t.reshape((b, nc, SSD_CHUNK) + t.shape[2:]), 1, 0)

    xdt = chunks((x.astype(jnp.float32) * dt[..., None]).reshape(b, l, g, r, p))
    da = chunks((dt * a).reshape(b, l, g, r))
    bc = chunks(bm.astype(jnp.float32))
    cc = chunks(cm.astype(jnp.float32))
    lower = jnp.tril(jnp.ones((SSD_CHUNK, SSD_CHUNK), dtype=bool))

    def step(state, inp):
        xk, ak, bk, ck = inp
        cum = jnp.cumsum(ak, axis=1)
        seg = cum[:, :, None] - cum[:, None, :]
        decay = jnp.exp(jnp.where(lower[None, :, :, None, None], seg, -jnp.inf))
        w = jnp.einsum('btgn,bsgn->btsg', ck, bk)[..., None] * decay
        y = jnp.einsum('btsgr,bsgrp->btgrp', w, xk)
        y = y + jnp.einsum('btgn,bgrpn->btgrp', ck, state) * jnp.exp(cum)[..., None]
        to_end = jnp.exp(cum[:, -1:] - cum)
        state = (state * jnp.exp(cum[:, -1])[..., None, None]
                 + jnp.einsum('bsgn,bsgrp->bgrpn', bk, xk * to_end[..., None]))
        return state, y

    init = jnp.zeros((b, g, r, p, n), jnp.float32)
    _, ys = lax.scan(step, init, (xdt, da, bc, cc))
    return jnp.moveaxis(ys, 0, 1).reshape(b, l, h, p)


def _flip(t):
    return jnp.flip(t, axis=1)


def ssd_mixer(u, conv_w, conv_b, dt_bias, a_log, d_skip, norm_g):
    b, s, _ = u.shape
    z = u[..., :SSD_D_INNER]
    xbc = u[..., SSD_D_INNER:SSD_D_INNER + SSD_XBC]
    dt_raw = u[..., SSD_D_INNER + SSD_XBC:].reshape(b, s, 2, SSD_N_HEADS).astype(jnp.float32)
    xbc = lax.conv_general_dilated(
        xbc, conv_w[:, None, :], window_strides=(1,),
        padding=[(SSD_CONV_PAD, SSD_CONV_PAD)],
        dimension_numbers=('NWC', 'WIO', 'NWC'), feature_group_count=SSD_XBC)
    xbc = jax.nn.silu(xbc + conv_b)
    xs = xbc[..., :SSD_D_INNER].reshape(b, s, SSD_N_HEADS, SSD_HEAD_DIM)
    bm = xbc[..., SSD_D_INNER:SSD_D_INNER + SSD_BC].reshape(b, s, SSD_N_GROUPS, SSD_D_STATE)
    cm = xbc[..., SSD_D_INNER + SSD_BC:].reshape(b, s, SSD_N_GROUPS, SSD_D_STATE)
    dt = jax.nn.softplus(dt_raw + dt_bias.astype(jnp.float32))
    a = -jnp.exp(a_log.astype(jnp.float32))
    y_f = ssd_chunked_scan(xs, dt[:, :, 0], a[0], bm, cm)
    y_b = _flip(ssd_chunked_scan(_flip(xs), _flip(dt[:, :, 1]), a[1], _flip(bm), _flip(cm)))
    y = y_f + y_b + xs.astype(jnp.float32) * d_skip.astype(jnp.float32)[:, None]
    y = y.reshape(b, s, SSD_D_INNER).astype(u.dtype) * jax.nn.silu(z)
    y = rms(y.reshape(b, s, SSD_N_GROUPS, SSD_D_INNER // SSD_N_GROUPS)).reshape(b, s, SSD_D_INNER)
    return y * norm_g


def lambda_init(layer_idx):
    return 0.8 - 0.6 * math.exp(-0.3 * layer_idx)


def diff_attention(u, lam_params, subln_g, table, lam_init):
    b, s, _ = u.shape
    q = u[..., :DIFF_QK_COLS].reshape(b, s, DIFF_N_HEADS, 2, DIFF_HEAD_DIM)
    k = u[..., DIFF_QK_COLS:2 * DIFF_QK_COLS].reshape(b, s, DIFF_N_HEADS, 2, DIFF_HEAD_DIM)
    v = u[..., 2 * DIFF_QK_COLS:].reshape(b, s, DIFF_N_HEADS, DIFF_V_DIM)
    lp = lam_params.astype(jnp.float32)
    lam = jnp.exp(jnp.sum(lp[0] * lp[1])) - jnp.exp(jnp.sum(lp[2] * lp[3])) + lam_init
    k1, k2 = k[..., 0, :], k[..., 1, :]
    scale = DIFF_HEAD_DIM ** -0.5
    nblk = s // Q_BLOCK
    k_pos = jnp.arange(s, dtype=jnp.int32)

    def blocks(t):
        return jnp.moveaxis(t.reshape(b, nblk, Q_BLOCK, DIFF_N_HEADS, DIFF_HEAD_DIM), 1, 0)

    def attend(args):
        q1b, q2b, start = args
        q_pos = start + jnp.arange(Q_BLOCK, dtype=jnp.int32)
        bias = rel_bias(table, q_pos, k_pos)[None]
        s1 = jnp.einsum('bqhd,bkhd->bhqk', q1b, k1).astype(jnp.float32) * scale + bias
        s2 = jnp.einsum('bqhd,bkhd->bhqk', q2b, k2).astype(jnp.float32) * scale + bias
        pr = (jax.nn.softmax(s1, axis=-1) - lam * jax.nn.softmax(s2, axis=-1)).astype(v.dtype)
        return jnp.einsum('bhqk,bkhe->bqhe', pr, v)

    starts = jnp.arange(nblk, dtype=jnp.int32) * Q_BLOCK
    out = lax.map(attend, (blocks(q[..., 0, :]), blocks(q[..., 1, :]), starts))
    out = jnp.moveaxis(out, 0, 1).reshape(b, s, DIFF_N_HEADS, DIFF_V_DIM)
    out = rmsnorm(out, subln_g) * (1.0 - lam_init)
    return out.reshape(b, s, DIFF_WIDTH).astype(u.dtype)


def memory_cross_attention(q, mem, mem_g, w_kv):
    b, s, _ = q.shape
    kv = rmsnorm(mem, mem_g) @ w_kv
    km = kv[..., :X_WIDTH].reshape(b, MEM_LEN, X_N_HEADS, X_HEAD_DIM)
    vm = kv[..., X_WIDTH:].reshape(b, MEM_LEN, X_N_HEADS, X_HEAD_DIM)
    qh = q.reshape(b, s, X_N_HEADS, X_HEAD_DIM)
    logits = jnp.einsum('bshd,bmhd->bhsm', qh, km).astype(jnp.float32) * (X_HEAD_DIM ** -0.5)
    pr = jax.nn.softmax(logits, axis=-1).astype(vm.dtype)
    return jnp.einsum('bhsm,bmhd->bshd', pr, vm).reshape(b, s, X_WIDTH)


def trunk(x, mem, p):
    for i in range(DEPTH):
        j = i // N_MIXERS
        h = rmsnorm(x, p['norm_pre_mix'][i])
        if i % N_MIXERS == 0:
            u = h @ p['ssd_w_in'][j]
            mix = ssd_mixer(u[..., :SSD_MIX_COLS], p['ssd_conv_w'][j], p['ssd_conv_b'][j],
                            p['ssd_dt_bias'][j], p['ssd_a_log'][j], p['ssd_d'][j], p['ssd_norm'][j])
            w_out = p['ssd_w_out'][j]
        else:
            u = h @ p['diff_w_in'][j]
            mix = diff_attention(u[..., :DIFF_MIX_COLS], p['diff_lambda'][j], p['diff_subln'][j],
                                 p['rel_bias_table'], lambda_init(i))
            w_out = p['diff_w_out'][j]
        mem_out = memory_cross_attention(u[..., -X_WIDTH:], mem, p['x_mem_norm'][i], p['x_w_kv'][i])
        o = jnp.concatenate([mix.astype(x.dtype), mem_out.astype(x.dtype)], axis=-1) @ w_out
        x = x + rmsnorm(o, p['norm_post_mix'][i])
        h = rmsnorm(x, p['norm_pre_mlp'][i])
        f = jnp.square(jax.nn.relu(h @ p['mlp_w1'][i])) @ p['mlp_w2'][i]
        x = x + rmsnorm(f, p['norm_post_mlp'][i])
    return x


def setup_inputs(seed: int = 0) -> dict:
    key = jax.random.key(seed)
    ks = jax.random.split(key, 32)
    f32 = jnp.float32

    def nrm(k, shape, scale):
        return jax.random.normal(k, shape, f32) * scale

    def gain(k, shape):
        return 1.0 + 0.02 * jax.random.normal(k, shape, f32)

    dt0 = jnp.exp(jax.random.uniform(ks[10], (N_SSD_LAYERS, 2, SSD_N_HEADS), f32,
                                     math.log(1e-3), math.log(1e-1)))
    dt_bias = dt0 + jnp.log(-jnp.expm1(-dt0))
    a_log = jnp.log(jax.random.uniform(ks[11], (N_SSD_LAYERS, 2, SSD_N_HEADS), f32, 1.0, 16.0))
    return {
        'x_prompt': nrm(ks[0], (BATCH, SEQ, D_MODEL), 1.0),
        'x_sample': nrm(ks[1], (DEC_BATCH, DEC_SEQ, D_MODEL), 1.0),
        'mem_prompt': nrm(ks[2], (BATCH, MEM_LEN, D_MODEL), 1.0),
        'mem_sample': nrm(ks[3], (DEC_BATCH, MEM_LEN, D_MODEL), 1.0),
        'rel_bias_table': nrm(ks[4], (REL_BUCKETS, DIFF_N_HEADS), 0.5),
        'norm_pre_mix': gain(ks[5], (DEPTH, D_MODEL)),
        'norm_post_mix': gain(ks[6], (DEPTH, D_MODEL)),
        'norm_pre_mlp': gain(ks[7], (DEPTH, D_MODEL)),
        'norm_post_mlp': gain(ks[8], (DEPTH, D_MODEL)),
        'ssd_w_in': nrm(ks[9], (N_SSD_LAYERS, D_MODEL, SSD_IN_COLS), D_MODEL ** -0.5),
        'ssd_conv_w': nrm(ks[12], (N_SSD_LAYERS, SSD_CONV_WIDTH, SSD_XBC), SSD_CONV_WIDTH ** -0.5),
        'ssd_conv_b': nrm(ks[13], (N_SSD_LAYERS, SSD_XBC), 0.01),
        'ssd_dt_bias': dt_bias,
        'ssd_a_log': a_log,
        'ssd_d': gain(ks[14], (N_SSD_LAYERS, SSD_N_HEADS)),
        'ssd_norm': gain(ks[15], (N_SSD_LAYERS, SSD_D_INNER)),
        'ssd_w_out': nrm(ks[16], (N_SSD_LAYERS, SSD_D_INNER + X_WIDTH, D_MODEL),
                         (SSD_D_INNER + X_WIDTH) ** -0.5),
        'diff_w_in': nrm(ks[17], (N_DIFF_LAYERS, D_MODEL, DIFF_IN_COLS), D_MODEL ** -0.5),
        'diff_lambda': nrm(ks[18], (N_DIFF_LAYERS, 4, DIFF_HEAD_DIM), 0.1),
        'diff_subln': gain(ks[19], (N_DIFF_LAYERS, DIFF_V_DIM)),
        'diff_w_out': nrm(ks[20], (N_DIFF_LAYERS, DIFF_WIDTH + X_WIDTH, D_MODEL),
                          (DIFF_WIDTH + X_WIDTH) ** -0.5),
        'x_mem_norm': gain(ks[21], (DEPTH, D_MODEL)),
        'x_w_kv': nrm(ks[22], (DEPTH, D_MODEL, 2 * X_WIDTH), D_MODEL ** -0.5),
        'mlp_w1': nrm(ks[23], (DEPTH, D_MODEL, D_FF), D_MODEL ** -0.5),
        'mlp_w2': nrm(ks[24], (DEPTH, D_FF, D_MODEL), D_FF ** -0.5),
    }


def reference(x_prompt, x_sample, mem_prompt, mem_sample, rel_bias_table,
              norm_pre_mix, norm_post_mix, norm_pre_mlp, norm_post_mlp,
              ssd_w_in, ssd_conv_w, ssd_conv_b, ssd_dt_bias, ssd_a_log, ssd_d, ssd_norm, ssd_w_out,
              diff_w_in, diff_lambda, diff_subln, diff_w_out,
              x_mem_norm, x_w_kv, mlp_w1, mlp_w2):
    params = {
        'rel_bias_table': rel_bias_table,
        'norm_pre_mix': norm_pre_mix, 'norm_post_mix': norm_post_mix,
        'norm_pre_mlp': norm_pre_mlp, 'norm_post_mlp': norm_post_mlp,
        'ssd_w_in': ssd_w_in, 'ssd_conv_w': ssd_conv_w, 'ssd_conv_b': ssd_conv_b,
        'ssd_dt_bias': ssd_dt_bias, 'ssd_a_log': ssd_a_log, 'ssd_d': ssd_d,
        'ssd_norm': ssd_norm, 'ssd_w_out': ssd_w_out,
        'diff_w_in': diff_w_in, 'diff_lambda': diff_lambda, 'diff_subln': diff_subln,
        'diff_w_out': diff_w_out,
        'x_mem_norm': x_mem_norm, 'x_w_kv': x_w_kv,
        'mlp_w1': mlp_w1, 'mlp_w2': mlp_w2,
    }
    y_prompt = trunk(x_prompt, mem_prompt, params)
    y_sample = trunk(x_sample, mem_sample, params)
    return (y_prompt, y_sample)
```

```python
import numpy as np
import concourse.bass as bass
import concourse.mybir as mybir
from concourse.bass_utils import run_bass_kernel_spmd
from contextlib import ExitStack

F32 = mybir.dt.float32
BF16 = mybir.dt.bfloat16
AF = mybir.ActivationFunctionType
ALU = mybir.AluOpType
AX = mybir.AxisListType


class Buf:
    __slots__ = ("w", "rs", "name", "excl")

    def __init__(self, name="", excl=False):
        self.w = None
        self.rs = []
        self.name = name
        self.excl = excl


class Sched:
    ENGS = ("pe", "act", "dve", "pool", "sp")

    def __init__(self, nc, stack, n_dma_sems=40):
        self.nc = nc
        self.ops = {e: [] for e in self.ENGS}
        self.esem = {e: stack.enter_context(nc.semaphore("s_" + e)) for e in self.ENGS}
        self.dsem = [stack.enter_context(nc.semaphore("d%d" % i)) for i in range(n_dma_sems)]
        self.duse = [0] * n_dma_sems
        self.dnext = 0
        self.seen_c = {e: {} for e in self.ENGS}
        self.seen_d = {e: {} for e in self.ENGS}
        self.signal = {e: set() for e in self.ENGS}

    def _need(self, eng, tok, waits, kind):
        if tok is None:
            return
        if tok[0] == "c":
            _, te, idx = tok
            if te == eng and (kind != "raw" or eng == "pe"):
                return
            if self.seen_c[eng].get(te, -1) >= idx:
                return
            self.seen_c[eng][te] = idx
            self.signal[te].add(idx)
            waits[:] = [w for w in waits if not (w[0] == "c" and w[1] == te)]
            waits.append(tok)
        else:
            _, k, val = tok
            if self.seen_d[eng].get(k, 0) >= val:
                return
            self.seen_d[eng][k] = val
            waits[:] = [w for w in waits if not (w[0] == "d" and w[1] == k)]
            waits.append(tok)

    def emit(self, eng, fn, reads=(), writes=(), dma=False):
        waits = []
        if any(b.excl for b in reads):
            writes = list(writes) + [b for b in reads if b.excl]
            reads = [b for b in reads if not b.excl]
        for b in reads:
            for t in (b.w or ()):
                self._need(eng, t, waits, "raw")
        for b in writes:
            for t in (b.w or ()):
                self._need(eng, t, waits, "waw")
            for r in b.rs:
                self._need(eng, r, waits, "war")
        idx = len(self.ops[eng])
        if dma:
            k = self.dnext
            self.dnext = (self.dnext + 1) % len(self.dsem)
            if self.duse[k] > 0:
                self._need(eng, ("d", k, 16 * self.duse[k]), waits, "raw")
            self.duse[k] += 1
            tok = ("d", k, 16 * self.duse[k])
        else:
            tok = ("c", eng, idx)
        self.ops[eng].append((fn, waits, tok))
        for b in reads:
            b.rs.append(tok)
        for b in writes:
            if dma and b.w and not b.rs and all(t[0] == "d" for t in b.w):
                b.w = b.w + [tok]
            else:
                b.w = [tok]
            b.rs = []
        return tok

    def finish(self):
        waits = []
        for k, u in enumerate(self.duse):
            if u > 0 and self.seen_d["sp"].get(k, 0) < 16 * u:
                waits.append(("d", k, 16 * u))
        self.ops["sp"].append((None, waits, None))

    def barrier(self):
        waits = []
        for k, u in enumerate(self.duse):
            if u > 0:
                self._need("sp", ("d", k, 16 * u), waits, "raw")
        for e in ("pe", "act", "dve", "pool"):
            for i in range(len(self.ops[e]) - 1, -1, -1):
                fn, w, tok = self.ops[e][i]
                if fn is not None and tok is not None and tok[0] == "c":
                    self._need("sp", tok, waits, "raw")
                    break
        tok_sp = ("c", "sp", len(self.ops["sp"]))
        self.ops["sp"].append((lambda h: h.nop(), waits, tok_sp))
        for e in ("pe", "act", "dve", "pool"):
            w = []
            self._need(e, tok_sp, w, "raw")
            if w:
                self.ops[e].append((None, w, None))

    def materialize(self):
        nc = self.nc
        cnt = {}
        for e in self.ENGS:
            c = 0
            m = {}
            for i in sorted(self.signal[e]):
                c += 1
                m[i] = c
            cnt[e] = m
        with nc.Block() as block:
            def run(e, h):
                for i, (fn, waits, tok) in enumerate(self.ops[e]):
                    for w in waits:
                        if w[0] == "c":
                            h.wait_ge(self.esem[w[1]], cnt[w[1]][w[2]])
                        else:
                            h.wait_ge(self.dsem[w[1]], w[2])
                    if fn is None:
                        continue
                    ins = fn(h)
                    if tok[0] == "d":
                        ins.then_inc(self.dsem[tok[1]], 16)
                    elif i in cnt[e]:
                        ins.then_inc(self.esem[e], 1)

            @block.tensor
            def _(h):
                run("pe", h)

            @block.scalar
            def _(h):
                run("act", h)

            @block.vector
            def _(h):
                run("dve", h)

            @block.gpsimd
            def _(h):
                run("pool", h)

            @block.sync
            def _(h):
                run("sp", h)

    def mm(self, out, lhsT, rhs, start=True, stop=True, reads=(), writes=(), **kw):
        return self.emit("pe", lambda e: e.matmul(out, lhsT, rhs, start=start, stop=stop, **kw), reads, writes)

    def tr(self, out, in_, ident, reads=(), writes=()):
        return self.emit("pe", lambda e: e.transpose(out, in_, ident), reads, writes)

    def act(self, out, in_, func, reads=(), writes=(), **kw):
        return self.emit("act", lambda e: e.activation(out, in_, func, **kw), reads, writes)

    def dma(self, eng, out, in_, reads=(), writes=(), **kw):
        return self.emit(eng, lambda e: e.dma_start(out=out, in_=in_, **kw), reads, writes, dma=True)

    def tt(self, eng, out, in0, in1, op, reads=(), writes=()):
        return self.emit(eng, lambda e: e.tensor_tensor(out, in0, in1, op), reads, writes)

    def ts(self, eng, out, in0, s1, s2, op0, op1=None, reads=(), writes=(), **kw):
        if op1 is None:
            return self.emit(eng, lambda e: e.tensor_scalar(out, in0, s1, s2, op0, **kw), reads, writes)
        return self.emit(eng, lambda e: e.tensor_scalar(out, in0, s1, s2, op0, op1, **kw), reads, writes)

    def stt(self, eng, out, in0, scalar, in1, op0, op1, reads=(), writes=()):
        return self.emit(eng, lambda e: e.scalar_tensor_tensor(out, in0, scalar, in1, op0, op1), reads, writes)

    def copy(self, eng, out, in_, reads=(), writes=()):
        if eng == "act":
            return self.emit(eng, lambda e: e.activation(out, in_, AF.Copy), reads, writes)
        return self.emit(eng, lambda e: e.tensor_copy(out, in_), reads, writes)

    def memset(self, eng, ap, val, writes=()):
        return self.emit(eng, lambda e: e.memset(ap, val), (), writes)
import math


def _prod(s):
    r = 1
    for v in s:
        r *= v
    return r


class T:
    __slots__ = ("ap", "b")

    def __init__(self, ap, name=""):
        self.ap = ap
        self.b = Buf(name)

    def __getitem__(self, k):
        return self.ap[k]


class Arena:
    def __init__(self, tensor, width):
        self.t = tensor
        self.W = width
        self.p = 0

    def reset(self, p=0):
        self.p = p

    def alloc(self, free_shape, dtype, name=""):
        n = _prod(free_shape)
        words = n if dtype == F32 else (n + 1) // 2
        words = (words + 7) // 8 * 8
        assert self.p + words <= self.W, ("arena overflow", name, self.p, words, self.W)
        ap = self.t[:, self.p:self.p + words]
        self.p += words
        if dtype != F32:
            ap = ap.bitcast(dtype)
        ap = ap[:, 0:n]
        if len(free_shape) == 2:
            ap = ap.rearrange("p (a b) -> p a b", a=free_shape[0])
        elif len(free_shape) == 3:
            ap = ap.rearrange("p (a b c) -> p a b c", a=free_shape[0], b=free_shape[1])
        return T(ap, name)


class Ring:
    def __init__(self, items):
        self.items = items
        self.i = 0

    def next(self):
        it = self.items[self.i % len(self.items)]
        self.i += 1
        return it
D = 1024
KT = 8
TB = 512
EPS = 1e-6
LAM_INIT1 = 0.8 - 0.6 * math.exp(-0.3 * 1)
NEG = -30000.0
RL = 1280

PARAMS = [
    ("rel_bias_table", [32, 8]), ("norm_pre_mix", [2, 1024]), ("norm_post_mix", [2, 1024]),
    ("norm_pre_mlp", [2, 1024]), ("norm_post_mlp", [2, 1024]), ("ssd_w_in", [1024, 7232]),
    ("ssd_conv_w", [5, 4096]), ("ssd_conv_b", [1, 4096]), ("ssd_dt_bias", [1, 64]),
    ("ssd_a_log", [1, 64]), ("ssd_d", [1, 32]), ("ssd_norm", [1, 2048]), ("ssd_w_out", [3072, 1024]),
    ("diff_w_in", [1024, 4096]), ("diff_lambda", [1, 256]), ("diff_subln", [1, 128]),
    ("diff_w_out", [2048, 1024]), ("x_mem_norm", [2, 1024]), ("x_w_kv", [2048, 2048]),
    ("mlp_w1", [2048, 4096]), ("mlp_w2", [8192, 1024]),
]


def build(seqs, dbg=False, upto="END"):
    nc = bass.Bass("TRN2", target_bir_lowering=False)
    NS = len(seqs)
    NT = sum(seqs)
    SM = max(seqs)
    ORDER = ["W", "KV", "A0", "B0", "C0", "D0", "B1", "D1", "END"]
    lvl = ORDER.index(upto)

    def din(name, shape):
        return nc.dram_tensor(name, shape, F32, kind="ExternalInput").ap()

    x_in = din("x", [NT, D])
    mem_in = din("mem", [NS * 256, D])
    P = {n: din(n, s) for n, s in PARAMS}
    oh_in = din("onehot", [32, RL])
    y_out = nc.dram_tensor("y", [NT, D], F32, kind="ExternalOutput").ap()
    skind = "ExternalOutput" if dbg else "Internal"

    def dscr(name, shape, dt, k=None):
        return nc.dram_tensor(name, shape, dt, kind=(k or skind)).ap()

    WB = {
        "in0": dscr("wb_in0", [1024, 7232], BF16, "Internal"), "out0": dscr("wb_out0", [3072, 1024], BF16, "Internal"),
        "in1": dscr("wb_in1", [1024, 4096], BF16, "Internal"), "out1": dscr("wb_out1", [2048, 1024], BF16, "Internal"),
        "kv": dscr("wb_kv", [2048, 2048], BF16, "Internal"), "w1": dscr("wb_w1", [2048, 4096], BF16, "Internal"),
        "w2": dscr("wb_w2", [8192, 1024], BF16, "Internal"),
    }
    WSRC = {"in0": "ssd_w_in", "out0": "ssd_w_out", "in1": "diff_w_in", "out1": "diff_w_out",
            "kv": "x_w_kv", "w1": "mlp_w1", "w2": "mlp_w2"}
    z_s = dscr("z_s", [SM, 2048], BF16)
    xbc_s = dscr("xbc_s", [4096, SM], BF16)
    dt_s = dscr("dt_s", [SM, 64], F32)
    mo_s = [dscr("mo_s%d" % l, [1024, SM], BF16) for l in range(2)]
    ct_s = dscr("ct_s", [1024, SM], BF16)
    bt_s = dscr("bt_s", [1024, SM], BF16)
    btm_s = dscr("btm_s", [SM, 1024], BF16)
    xtm_s = dscr("xtm_s", [SM, 2048], BF16)
    yf_s = dscr("yf_s", [SM, 2048], F32)
    mix_s = dscr("mix_s", [2048, SM], BF16)
    atm_s = dscr("atm_s", [SM, 1024], BF16)
    x1_s = dscr("x1_s", [SM, 1024], F32)
    q_s = dscr("q_s", [1024, SM], BF16)
    k_s = dscr("k_s", [1024, SM], BF16)
    v_s = dscr("v_s", [SM, 1024], BF16)
    vecd = dscr("vecd", [8, RL], F32)
    rep_t = nc.dram_tensor("rep", [8 * 128, RL], F32, kind=skind)
    rep = rep_t.ap()

    DBUF = {}

    def db(name, blk=0):
        k = (name, blk)
        if k not in DBUF:
            DBUF[k] = Buf(name)
        return DBUF[k]

    with ExitStack() as st:
        S = Sched(nc, st, n_dma_sems=48)
        AW = 50000
        arena_t = st.enter_context(nc.sbuf_tensor("arena", [128, AW], F32))
        AR = Arena(arena_t, AW)
        pbanks = []
        pall = st.enter_context(nc.psum_tensor("pall", [128, 4096], F32))
        for i in range(8):
            pbanks.append(T(pall[:, i * 512:(i + 1) * 512], "pb%d" % i))
            pbanks[-1].b.excl = True
        PR = Ring(pbanks[0:6])
        trs = []
        for i in (6, 7):
            t_ = T(pbanks[i].ap.bitcast(BF16)[:, 0:512], "tr%d" % i)
            t_.b = pbanks[i].b
            trs.append(t_)
        TRR = Ring(trs)

        ident_f = AR.alloc([128], F32, "ident_f")
        ident = AR.alloc([128], BF16, "ident")
        ones_b = AR.alloc([128], BF16, "ones_b")
        ones_f = AR.alloc([128], F32, "ones_f")
        tri_f = AR.alloc([128], F32, "tri_f")
        triu_f = AR.alloc([128], F32, "triu_f")
        tris_f = AR.alloc([128], F32, "tris_f")
        trisl_f = AR.alloc([128], F32, "trisl_f")
        maskf = AR.alloc([4, 128], F32, "maskf")
        maskb = AR.alloc([4, 128], F32, "maskb")
        maskf_b = AR.alloc([4, 128], BF16, "maskf_b")
        maskb_b = AR.alloc([4, 128], BF16, "maskb_b")
        ssdn = AR.alloc([2048], F32, "ssdn")
        dtb = AR.alloc([64], F32, "dtb")
        avec = AR.alloc([64], F32, "avec")
        dsk = AR.alloc([32], F32, "dsk")
        subln = AR.alloc([128], F32, "subln")
        lamt = AR.alloc([8], F32, "lamt")
        tab15 = AR.alloc([8], F32, "tab15")
        tab31 = AR.alloc([8], F32, "tab31")
        cw = AR.alloc([32, 5], F32, "cw")
        cb = AR.alloc([32], F32, "cb")
        kmT = [AR.alloc([8, 256], BF16, "kmT%d" % l) for l in range(2)]
        vm = [AR.alloc([2, 1024], BF16, "vm%d" % l) for l in range(2)]
        PERSIST = AR.p

        def A_(e, o, i, f, r=(), w=(), **kw):
            return S.emit("act", lambda h: h.activation(o, i, f, **kw), [t.b for t in r], [t.b for t in w])

        def bs(ts):
            return [t if isinstance(t, Buf) else t.b for t in ts]

        def MM(out, lhsT, rhs, start, stop, r, w):
            return S.mm(out, lhsT, rhs, start=start, stop=stop, reads=bs(r), writes=bs(w))

        def ACT(o, i, f, r, w, **kw):
            return S.emit("act", lambda h: h.activation(o, i, f, **kw), bs(r), bs(w))

        def TT(e, o, a, b, op, r, w):
            return S.tt(e, o, a, b, op, reads=bs(r), writes=bs(w))

        def TS(e, o, a, s1, s2, op0, op1, r, w):
            return S.ts(e, o, a, s1, s2, op0, op1, reads=bs(r), writes=bs(w))

        def STT(e, o, a, sc, b, op0, op1, r, w):
            return S.stt(e, o, a, sc, b, op0, op1, reads=bs(r), writes=bs(w))

        def CP(e, o, i, r, w):
            return S.copy(e, o, i, reads=bs(r), writes=bs(w))

        def DMA(o, i, r, w, eng="sp"):
            return S.dma(eng, o, i, reads=bs(r), writes=bs(w))

        def MS(e, ap, val, w):
            return S.memset(e, ap, val, writes=bs(w))

        def AFS(t, pattern, op, fill, base, cm):
            S.emit("pool", lambda h: h.affine_select(t.ap, t.ap, pattern, op, fill, base=base, channel_multiplier=cm),
                   bs([t]), bs([t]))

        MS("pool", ident_f.ap, 1.0, [ident_f])
        AFS(ident_f, [[-1, 128]], ALU.is_equal, 0.0, 0, 1)
        CP("dve", ident.ap, ident_f.ap, [ident_f], [ident])
        MS("pool", ones_f.ap, 1.0, [ones_f])
        MS("pool", ones_b.ap, 1.0, [ones_b])
        MS("pool", tri_f.ap, 1.0, [tri_f])
        AFS(tri_f, [[1, 128]], ALU.is_ge, 0.0, 0, -1)
        MS("pool", triu_f.ap, 1.0, [triu_f])
        AFS(triu_f, [[-1, 128]], ALU.is_ge, 0.0, 0, 1)
        MS("pool", tris_f.ap, 1.0, [tris_f])
        AFS(tris_f, [[-1, 128]], ALU.is_gt, 0.0, 0, 1)
        MS("pool", trisl_f.ap, 1.0, [trisl_f])
        AFS(trisl_f, [[1, 128]], ALU.is_gt, 0.0, 0, -1)
        MS("pool", maskf.ap, 0.0, [maskf])
        S.emit("pool", lambda h: h.affine_select(maskf.ap, maskf.ap, [[0, 4], [1, 128]], ALU.is_ge, NEG, base=0,
                                                 channel_multiplier=-1), bs([maskf]), bs([maskf]))
        MS("pool", maskb.ap, 0.0, [maskb])
        S.emit("pool", lambda h: h.affine_select(maskb.ap, maskb.ap, [[0, 4], [-1, 128]], ALU.is_ge, NEG, base=0,
                                                 channel_multiplier=1), bs([maskb]), bs([maskb]))
        CP("dve", maskf_b.ap, maskf.ap, [maskf], [maskf_b])
        CP("dve", maskb_b.ap, maskb.ap, [maskb], [maskb_b])

        def load_gain(nm, l):
            g = AR.alloc([1024], F32, "g_" + nm)
            DMA(g.ap, P[nm][l:l + 1, :].partition_broadcast(128), [], [g])
            return g
        DMA(ssdn.ap, P["ssd_norm"][0:1, :].partition_broadcast(128), [], [ssdn])
        DMA(dtb.ap, P["ssd_dt_bias"][0:1, :].partition_broadcast(128), [], [dtb])
        DMA(avec.ap, P["ssd_a_log"][0:1, :].partition_broadcast(128), [], [avec])
        DMA(dsk.ap, P["ssd_d"][0:1, :].partition_broadcast(128), [], [dsk])
        DMA(subln.ap, P["diff_subln"][0:1, :].partition_broadcast(128), [], [subln])
        DMA(tab15.ap, P["rel_bias_table"][15:16, :].partition_broadcast(128), [], [tab15])
        DMA(tab31.ap, P["rel_bias_table"][31:32, :].partition_broadcast(128), [], [tab31])
        ACT(avec.ap, avec.ap, AF.Exp, [avec], [avec])
        TS("dve", avec.ap, avec.ap, -1.0, None, ALU.mult, None, [avec], [avec])
        TS("dve", subln.ap, subln.ap, 1.0 - LAM_INIT1, None, ALU.mult, None, [subln], [subln])
        AR.reset(PERSIST)
        lp = AR.alloc([4, 64], F32, "lp")
        lpp = AR.alloc([2, 64], F32, "lpp")
        lps = AR.alloc([2], F32, "lps")
        DMA(lp.ap.rearrange("p a b -> p (a b)"), P["diff_lambda"][0:1, :].partition_broadcast(128), [], [lp])
        lp4 = lp.ap.rearrange("p (a c) b -> p a c b", c=2)
        TT("dve", lpp.ap, lp4[:, :, 0, :], lp4[:, :, 1, :], ALU.mult, [lp], [lpp])
        S.emit("dve", lambda h: h.tensor_reduce(lps.ap, lpp.ap, AX.X, ALU.add), bs([lpp]), bs([lps]))
        ACT(lps.ap, lps.ap, AF.Exp, [lps], [lps])
        TT("dve", lamt.ap[:, 0:1], lps.ap[:, 0:1], lps.ap[:, 1:2], ALU.subtract, [lps], [lamt])
        TS("dve", lamt.ap[:, 0:1], lamt.ap[:, 0:1], LAM_INIT1, None, ALU.add, None, [lamt], [lamt])
        TS("dve", lamt.ap[:, 1:2], lamt.ap[:, 0:1], -1.0, None, ALU.mult, None, [lamt], [lamt])
        cwr = AR.alloc([2, 128], F32, "cwr")
        cbr = AR.alloc([128], F32, "cbr")
        MS("pool", cwr.ap, 0.0, [cwr])
        MS("pool", cbr.ap, 0.0, [cbr])
        cw_rows = P["ssd_conv_w"].rearrange("k (ct p) -> (k ct) p", p=128)
        DMA(cwr.ap[:, 0, :], cw_rows[0:128, :], [], [cwr])
        DMA(cwr.ap[0:32, 1, :], cw_rows[128:160, :], [], [cwr])
        DMA(cbr.ap[0:32, :], P["ssd_conv_b"].rearrange("o (ct p) -> (o ct) p", p=128), [], [cbr])
        pb = PR.next()
        MM(pb.ap[:, 0:128], cwr.ap[:, 0, :], ident_f.ap, True, True, [cwr, ident_f], [pb])
        MM(pb.ap[:, 128:160], cwr.ap[0:32, 1, :], ident_f.ap[0:32, 0:32], True, True, [cwr, ident_f], [pb])
        MM(pb.ap[:, 160:192], cbr.ap[0:32, :], ident_f.ap[0:32, 0:32], True, True, [cbr, ident_f], [pb])
        CP("dve", cw.ap.rearrange("p ct k -> p k ct"), pb.ap[:, 0:160].rearrange("p (k ct) -> p k ct", k=5), [pb], [cw])
        CP("dve", cb.ap, pb.ap[:, 160:192], [pb], [cb])
        tabs = AR.alloc([8], F32, "tabs")
        ohs = AR.alloc([RL], F32, "ohs")
        rv = AR.alloc([RL], F32, "rv")
        DMA(tabs.ap[0:32, :], P["rel_bias_table"][:, :], [], [tabs])
        DMA(ohs.ap[0:32, :], oh_in[:, :], [], [ohs])
        for c0 in range(0, RL, 512):
            n = min(512, RL - c0)
            pb = PR.next()
            MM(pb.ap[0:8, 0:n], tabs.ap[0:32, :], ohs.ap[0:32, c0:c0 + n], True, True, [tabs, ohs], [pb])
            CP("dve", rv.ap[0:8, c0:c0 + n], pb.ap[0:8, 0:n], [pb], [rv])
        DMA(vecd[:, :], rv.ap[0:8, :], [rv], [db("vecd")])
        S.barrier()
        for h in range(8):
            DMA(rep[h * 128:(h + 1) * 128, :], vecd[h:h + 1, :].partition_broadcast(128), [db("vecd")], [db("rep", h)])

        def emit_cast(key, r0):
            DMA(WB[key][r0:r0 + 256, :], P[WSRC[key]][r0:r0 + 256, :], [], [db("w_" + key, r0)], eng="pool")

        for key in ("kv", "in0"):
            for r0 in range(0, P[WSRC[key]].shape[0], 256):
                emit_cast(key, r0)
        pending_casts = []
        for key, lo, hi in (("out0", 0, 3072), ("w1", 0, 1024), ("w2", 0, 4096), ("in1", 0, 1024),
                            ("out1", 0, 2048), ("w1", 1024, 2048), ("w2", 4096, 8192)):
            for r0 in range(lo, hi, 256):
                pending_casts.append((key, r0))

        def drip(n):
            for _ in range(min(n, len(pending_casts))):
                emit_cast(*pending_casts.pop(0))
        S.barrier()

        def new_phase():
            S.barrier()
            AR.reset(PERSIST)

        WR = [None]

        def wload(key, r0, nkt, c0, ncols):
            t = WR[0].next()
            src = WB[key][r0:r0 + nkt * 128, c0:c0 + ncols].rearrange("(kt p) c -> p kt c", p=128)
            deps = [db("w_" + key, r) for r in range(r0 - r0 % 256, r0 + nkt * 128, 256)]
            assert all(d_.w for d_ in deps), ("weight block not cast yet", key, r0)
            DMA(t.ap[:, 0:nkt, 0:ncols], src, deps, [t])
            return t

        STQ = "pool"
        evac_i = [0]

        def evac(out_ap, in_ap, r, w):
            evac_i[0] += 1
            if evac_i[0] % 2 == 0:
                ACT(out_ap, in_ap, AF.Copy, r, w)
            else:
                CP("dve", out_ap, in_ap, r, w)

        def tm_norm(src, g, dst, ss, sq, r_extra=()):
            MS("pool", ss.ap, 0.0, [ss])
            for j in range(4):
                ACT(sq.ap, src.ap[:, j, :], AF.Square, [src, ss] + list(r_extra), [sq, ss], accum_out=ss.ap[:, j:j + 1])
            ACT(ss.ap[:, 4:8], ss.ap[:, 0:4], AF.Ln, [ss], [ss], bias=EPS, scale=1.0 / D)
            ACT(ss.ap[:, 8:12], ss.ap[:, 4:8], AF.Exp, [ss], [ss], scale=-0.5)
            for j in range(4):
                STT("dve", dst.ap[:, j, :], src.ap[:, j, :], ss.ap[:, 8 + j:9 + j], g.ap, ALU.mult, ALU.mult,
                    [src, ss, g], [dst])

        def transpose_block(hb, hT, nkt=8):
            for kt in range(nkt):
                tr = TRR.next()
                for j in range(4):
                    S.tr(tr.ap[:, j * 128:(j + 1) * 128], hb.ap[:, j, kt * 128:(kt + 1) * 128], ident.ap,
                         reads=bs([hb, ident]), writes=bs([tr]))
                CP("dve", hT.ap[:, kt, :], tr.ap, [tr], [hT])

        def cross_attn(l, qT, moT, E, rden):
            for h in range(4):
                es = []
                for mt in range(2):
                    pb = PR.next()
                    for dt_ in range(2):
                        MM(pb.ap, kmT[l].ap[:, 2 * h + dt_, mt * 128:(mt + 1) * 128], qT.ap[:, 2 * h + dt_, :],
                           dt_ == 0, dt_ == 1, [kmT[l], qT], [pb])
                    e = E.next()
                    ACT(e.ap, pb.ap, AF.Exp, [pb], [e], scale=1.0 / 16.0)
                    es.append(e)
                pden = PR.next()
                for mt in range(2):
                    MM(pden.ap, ones_b.ap, es[mt].ap, mt == 0, mt == 1, [ones_b, es[mt]], [pden])
                S.emit("dve", lambda hh, o=rden.ap, i=pden.ap: hh.reciprocal(o, i), bs([pden]), bs([rden]))
                for dt_ in range(2):
                    pn = PR.next()
                    for mt in range(2):
                        MM(pn.ap, vm[l].ap[:, mt, (2 * h + dt_) * 128:(2 * h + dt_ + 1) * 128], es[mt].ap,
                           mt == 0, mt == 1, [vm[l], es[mt]], [pn])
                    TT("dve", moT.ap[:, 2 * h + dt_, :], pn.ap, rden.ap, ALU.mult, [pn, rden], [moT])

        def phase_KV(si, m0):
            new_phase()
            WR[0] = Ring([AR.alloc([8, 512], BF16, "w%d" % i) for i in range(3)])
            mt_ = AR.alloc([4, 1024], F32, "memt")
            ss = AR.alloc([12], F32, "ss")
            sq = AR.alloc([1024], F32, "sq")
            hb = AR.alloc([4, 1024], BF16, "hb")
            hT = AR.alloc([8, 512], BF16, "hT")
            gm = [load_gain("x_mem_norm", l) for l in range(2)]
            MS("pool", mt_.ap[:, 2:4, :], 0.0, [mt_])
            DMA(mt_.ap[:, 0:2, :], mem_in[m0:m0 + 256, :].rearrange("(j p) d -> p j d", p=128), [], [mt_])
            for l in range(2):
                tm_norm(mt_, gm[l], hb, ss, sq)
                transpose_block(hb, hT)
                for c in range(4):
                    w = wload("kv", l * 1024, 8, c * 512, 512)
                    if c < 2:
                        for i in range(4):
                            pb = PR.next()
                            for kt in range(8):
                                MM(pb.ap[:, 0:256], w.ap[:, kt, i * 128:(i + 1) * 128], hT.ap[:, kt, 0:256],
                                   kt == 0, kt == 7, [w, hT], [pb])
                            evac(kmT[l].ap[:, c * 4 + i, :], pb.ap[:, 0:256], [pb], [kmT[l]])
                    else:
                        for mt in range(2):
                            pb = PR.next()
                            for kt in range(8):
                                MM(pb.ap, hT.ap[:, kt, mt * 128:(mt + 1) * 128], w.ap[:, kt, :],
                                   kt == 0, kt == 7, [w, hT], [pb])
                            evac(vm[l].ap[:, mt, (c - 2) * 512:(c - 1) * 512], pb.ap, [pb], [vm[l]])

        def alloc_front(host=None, with_dt=True):
            d = {}
            d["xt"] = AR.alloc([4, 1024], F32, "xt")
            d["ss"] = AR.alloc([12], F32, "ss")
            d["sq"] = AR.alloc([1024], F32, "sq")
            d["hb"] = AR.alloc([4, 1024], BF16, "hb")
            d["hT"] = AR.alloc([8, 512], BF16, "hT")
            d["E"] = Ring([AR.alloc([512], BF16, "E%d" % i) for i in range(4)])
            d["rden"] = AR.alloc([512], F32, "rden")
            if host is None:
                d["qT"] = AR.alloc([8, 512], BF16, "qT")
                d["moT"] = AR.alloc([8, 512], BF16, "moT")
                d["fst"] = Ring([AR.alloc([4, 512], BF16, "fst%d" % i) for i in range(2)])
            else:
                sub = Arena(arena_t, AW)
                sub.reset(host[0])
                d["qT"] = sub.alloc([8, 512], BF16, "qT")
                d["moT"] = sub.alloc([8, 512], BF16, "moT")
                f0 = sub.alloc([4, 512], BF16, "fst0")
                f1 = sub.alloc([4, 512], BF16, "fst1")
                assert sub.p <= host[0] + host[1]
                for t in (d["qT"], d["moT"], f0, f1):
                    t.b = host[2]
                d["fst"] = Ring([f0, f1])
            if with_dt:
                d["dtt"] = AR.alloc([4, 4, 64], F32, "dtt")
            return d

        def front(l, t0, S_, tb, d):
            xt, hb, hT, qT, moT = d["xt"], d["hb"], d["hT"], d["qT"], d["moT"]
            c0t = tb * TB
            tm_norm(xt, d["g_pre_mix%d" % l], hb, d["ss"], d["sq"])
            transpose_block(hb, hT)
            if l == 0:
                chunks = [("z", 512 * c, 512, c) for c in range(4)] + [("xbc", 2048 + 512 * c, 512, c) for c in range(8)] \
                    + [("dt", 6144, 64, 0)] + [("q", 6208 + 512 * c, 512, c) for c in range(2)]
                key = "in0"
            else:
                chunks = [("qd", 512 * c, 512, c) for c in range(2)] + [("kd", 1024 + 512 * c, 512, c) for c in range(2)] \
                    + [("vd", 2048 + 512 * c, 512, c) for c in range(2)] + [("q", 3072 + 512 * c, 512, c) for c in range(2)]
                key = "in1"
            for kind, col0, ncols, c in chunks:
                w = wload(key, 0, 8, col0, ncols)
                if kind in ("xbc", "q", "qd", "kd"):
                    stg = None if kind == "q" else d["fst"].next()
                    for i in range(4):
                        pb = PR.next()
                        for kt in range(8):
                            MM(pb.ap, w.ap[:, kt, i * 128:(i + 1) * 128], hT.ap[:, kt, :], kt == 0, kt == 7, [w, hT], [pb])
                        if kind == "q":
                            evac(qT.ap[:, c * 4 + i, :], pb.ap, [pb], [qT])
                        else:
                            evac(stg.ap[:, i, :], pb.ap, [pb], [stg])
                    if kind != "q":
                        dst = {"xbc": xbc_s, "qd": q_s, "kd": k_s}[kind]
                        DMA(dst[c * 512:(c + 1) * 512, c0t:c0t + TB].rearrange("(i p) t -> p i t", p=128), stg.ap,
                            [stg], [db(kind, tb)], eng=STQ)
                elif kind in ("z", "vd"):
                    stg = d["fst"].next()
                    for j in range(4):
                        pb = PR.next()
                        for kt in range(8):
                            MM(pb.ap, hT.ap[:, kt, j * 128:(j + 1) * 128], w.ap[:, kt, :], kt == 0, kt == 7, [w, hT], [pb])
                        if kind == "z":
                            ACT(stg.ap[:, j, :], pb.ap, AF.Silu, [pb], [stg])
                        else:
                            evac(stg.ap[:, j, :], pb.ap, [pb], [stg])
                    dst = z_s if kind == "z" else v_s
                    DMA(dst[c0t:c0t + TB, c * 512:(c + 1) * 512].rearrange("(j p) c -> p j c", p=128), stg.ap,
                        [stg], [db(kind, tb)], eng=STQ)
                else:
                    dtt = d["dtt"]
                    pb = PR.next()
                    for j in range(4):
                        for kt in range(8):
                            MM(pb.ap[:, j * 64:(j + 1) * 64], hT.ap[:, kt, j * 128:(j + 1) * 128], w.ap[:, kt, 0:64],
                               kt == 0, kt == 7, [w, hT], [pb])
                    pv = pb.ap[:, 0:256].rearrange("p (j c) -> p j c", j=4)
                    dtb_b = dtb.ap.unsqueeze(1).to_broadcast([128, 4, 64])
                    TT("dve", dtt.ap[:, 0], pv, dtb_b, ALU.add, [pb, dtb], [dtt])
                    STT("dve", dtt.ap[:, 1], dtt.ap[:, 0], -1.0, dtt.ap[:, 0], ALU.mult, ALU.max, [dtt], [dtt])
                    ACT(dtt.ap[:, 2], dtt.ap[:, 1], AF.Exp, [dtt], [dtt], scale=-1.0)
                    ACT(dtt.ap[:, 3], dtt.ap[:, 2], AF.Ln, [dtt], [dtt], bias=1.0)
                    STT("dve", dtt.ap[:, 1], dtt.ap[:, 0], 0.0, dtt.ap[:, 3], ALU.max, ALU.add, [dtt], [dtt])
                    DMA(dt_s[c0t:c0t + TB, :].rearrange("(j p) c -> p j c", p=128), dtt.ap[:, 1], [dtt], [db("dt", tb)], eng=STQ)
            cross_attn(l, qT, moT, d["E"], d["rden"])
            DMA(mo_s[l][:, c0t:c0t + TB].rearrange("(i p) t -> p i t", p=128), moT.ap, [moT], [db("mo%d" % l, tb)], eng=STQ)

        def phase_A0(si, t0, S_):
            new_phase()
            WR[0] = Ring([AR.alloc([8, 512], BF16, "w%d" % i) for i in range(4)])
            d = alloc_front()
            d["g_pre_mix0"] = load_gain("norm_pre_mix", 0)
            for tb in range(S_ // TB):
                drip(8)
                DMA(d["xt"].ap, x_in[t0 + tb * TB:t0 + (tb + 1) * TB, :].rearrange("(j p) d -> p j d", p=128), [], [d["xt"]])
                front(0, t0, S_, tb, d)
            drip(len(pending_casts))

        def phase_B0(si, S_):
            new_phase()
            xin = Ring([AR.alloc([32, 516], BF16, "xin%d" % i) for i in range(1)])
            dgall = AR.alloc([160, 128], BF16, "dgall")
            for ct in range(32):
                for k in range(5):
                    TS("dve", dgall.ap[:, ct * 5 + k, :], ident_f.ap, cw.ap[:, ct, k:k + 1], None,
                       ALU.mult, None, [ident_f, cw], [dgall])
            pc = Ring([AR.alloc([32, 512], BF16, "pc%d" % i) for i in range(1)])
            xtm = Ring([AR.alloc([4, 2048], BF16, "xtm%d" % i) for i in range(2)])
            btm = Ring([AR.alloc([4, 1024], BF16, "btm%d" % i) for i in range(2)])
            nb = S_ // TB
            for tb in range(nb):
                xi = xin.next()
                lo = max(0, tb * TB - 2)
                hi = min(S_, tb * TB + TB + 2)
                o0 = lo - (tb * TB - 2)
                if tb == 0:
                    MS("pool", xi.ap[:, :, 0:2], 0.0, [xi])
                if tb == nb - 1:
                    MS("pool", xi.ap[:, :, 514:516], 0.0, [xi])
                rds = [db("xbc", b) for b in (tb - 1, tb, tb + 1) if 0 <= b < nb]
                for q4 in range(4):
                    DMA(xi.ap[:, q4 * 8:(q4 + 1) * 8, o0:o0 + (hi - lo)],
                        xbc_s[q4 * 1024:(q4 + 1) * 1024, lo:hi].rearrange("(ct p) t -> p ct t", p=128), rds, [xi])
                po = pc.next()
                for ct in range(32):
                    pb = PR.next()
                    for k in range(5):
                        MM(pb.ap, dgall.ap[:, ct * 5 + k, :], xi.ap[:, ct, k:k + 512], k == 0, k == 4, [dgall, xi], [pb])
                    ACT(po.ap[:, ct, :], pb.ap, AF.Silu, [pb, cb], [po], bias=cb.ap[:, ct:ct + 1])
                c0t = tb * TB
                DMA(bt_s[:, c0t:c0t + TB].rearrange("(i p) t -> p i t", p=128), po.ap[:, 16:24, :], [po], [db("bt", tb)])
                DMA(ct_s[:, c0t:c0t + TB].rearrange("(i p) t -> p i t", p=128), po.ap[:, 24:32, :], [po], [db("ct", tb)])
                xt_ = xtm.next()
                bt_ = btm.next()
                for j in range(4):
                    for ct in range(24):
                        if ct % 4 == 0:
                            tr = TRR.next()
                        S.tr(tr.ap[:, (ct % 4) * 128:(ct % 4 + 1) * 128], po.ap[:, ct, j * 128:(j + 1) * 128], ident.ap,
                             reads=bs([po, ident]), writes=bs([tr]))
                        if ct % 4 == 3:
                            c4 = ct // 4
                            if c4 < 4:
                                CP("dve", xt_.ap[:, j, c4 * 512:(c4 + 1) * 512], tr.ap, [tr], [xt_])
                            else:
                                CP("dve", bt_.ap[:, j, (c4 - 4) * 512:(c4 - 3) * 512], tr.ap, [tr], [bt_])
                DMA(xtm_s[c0t:c0t + TB, :].rearrange("(j p) c -> p j c", p=128), xt_.ap, [xt_], [db("xtm", tb)])
                DMA(btm_s[c0t:c0t + TB, :].rearrange("(j p) c -> p j c", p=128), bt_.ap, [bt_], [db("btm", tb)])

        def phase_C0(si, S_):
            new_phase()
            nb = S_ // TB
            xtm = Ring([AR.alloc([2048], BF16, "sxtm%d" % i) for i in range(2)])
            btm = Ring([AR.alloc([1024], BF16, "sbtm%d" % i) for i in range(2)])
            btf = Ring([AR.alloc([8, 512], BF16, "sbt%d" % i) for i in range(2)])
            ctf = Ring([AR.alloc([8, 512], BF16, "sct%d" % i) for i in range(2)])
            dtr = Ring([AR.alloc([4, 64], F32, "sdt%d" % i) for i in range(2)])
            zr = Ring([AR.alloc([2048], BF16, "sz%d" % i) for i in range(2)])
            yfl = Ring([AR.alloc([2048], F32, "syf%d" % i) for i in range(2)])
            xdsr = Ring([AR.alloc([2048], BF16, "xds%d" % i) for i in range(2)])
            ldtr = Ring([AR.alloc([64], F32, "ldt%d" % i) for i in range(2)])
            state = AR.alloc([2048], F32, "state")
            stateb = AR.alloc([2048], BF16, "stateb")
            dar = Ring([AR.alloc([32], F32, "da%d" % i) for i in range(2)])
            xdte = Ring([AR.alloc([2048], BF16, "xdte%d" % i) for i in range(2)])
            ex3 = Ring([AR.alloc([96], F32, "ex3_%d" % i) for i in range(2)])
            ncum = Ring([AR.alloc([32], F32, "ncum%d" % i) for i in range(2)])
            gt = Ring([AR.alloc([128], BF16, "gt%d" % i) for i in range(3)])
            exs = Ring([AR.alloc([4, 128], BF16, "exs%d" % i) for i in range(3)])
            wt = Ring([AR.alloc([4, 128], BF16, "wt%d" % i) for i in range(3)])
            yc = Ring([AR.alloc([2048], F32, "yc%d" % i) for i in range(2)])
            ytmp = Ring([AR.alloc([256], F32, "ytmp%d" % i) for i in range(3)])
            ssgr = Ring([AR.alloc([24], F32, "ssg%d" % i) for i in range(2)])
            sq = AR.alloc([2048], BF16, "sq2")
            ybr = Ring([AR.alloc([2048], BF16, "yb%d" % i) for i in range(2)])
            mixT = Ring([AR.alloc([16, 128], BF16, "mixT%d" % i) for i in range(2)])

            class Ctx:
                pass

            blk = {}

            def prologue(c):
                dirn, tb, j = c.dirn, c.tb, c.j
                c0t = tb * TB
                key = (dirn, tb)
                if key not in blk:
                    Bt = btf.next()
                    Ct = ctf.next()
                    Dt = dtr.next()
                    DMA(Bt.ap, bt_s[:, c0t:c0t + TB].rearrange("(i p) t -> p i t", p=128), [db("bt", tb)], [Bt])
                    DMA(Ct.ap, ct_s[:, c0t:c0t + TB].rearrange("(i p) t -> p i t", p=128), [db("ct", tb)], [Ct])
                    DMA(Dt.ap, dt_s[c0t:c0t + TB, :].rearrange("(j p) c -> p j c", p=128), [db("dt", tb)], [Dt])
                    blk.clear()
                    blk[key] = (Bt, Ct, Dt)
                c.Bt, c.Ct, c.Dt = blk[key]
                c.r0 = c0t + j * 128
                c.ck = tb * 4 + j
                c.X = xtm.next()
                c.Bm = btm.next()
                DMA(c.X.ap, xtm_s[c.r0:c.r0 + 128, :], [db("xtm", tb)], [c.X])
                DMA(c.Bm.ap, btm_s[c.r0:c.r0 + 128, :], [db("btm", tb)], [c.Bm])
                if dirn == 1:
                    c.Z = zr.next()
                    DMA(c.Z.ap, z_s[c.r0:c.r0 + 128, :], [db("z", tb)], [c.Z])
                c.tri_c = tri_f if dirn == 0 else triu_f
                c.tri_e = tris_f if dirn == 0 else trisl_f
                c.mask = maskf_b if dirn == 0 else maskb_b
                dtj = c.Dt.ap[:, j, dirn * 32:(dirn + 1) * 32]
                c.da = dar.next()
                TT("dve", c.da.ap, dtj, avec.ap[:, dirn * 32:(dirn + 1) * 32], ALU.mult, [c.Dt, avec], [c.da])
                pm = PR.next()
                MM(pm.ap[:, 0:32], c.tri_c.ap, c.da.ap, True, True, [c.tri_c, c.da], [pm])
                MM(pm.ap[:, 32:64], c.tri_e.ap, c.da.ap, True, True, [c.tri_e, c.da], [pm])
                MM(pm.ap[:, 64:96], ones_f.ap, c.da.ap, True, True, [ones_f, c.da], [pm])
                c.e3 = ex3.next()
                ACT(c.e3.ap, pm.ap[:, 0:96], AF.Exp, [pm], [c.e3])
                ld = ldtr.next()
                ACT(ld.ap[:, 0:32], dtj, AF.Ln, [c.Dt], [ld])
                c.nc_ = ncum.next()
                STT("dve", c.nc_.ap, pm.ap[:, 0:32], -1.0, ld.ap[:, 0:32], ALU.mult, ALU.add, [pm, ld], [c.nc_])
                TT("dve", ld.ap[:, 32:64], dtj, c.e3.ap[:, 32:64], ALU.mult, [c.Dt, c.e3], [ld])
                c.xe = xdte.next()
                TT("dve", c.xe.ap.rearrange("p (h c) -> p h c", h=32), c.X.ap.rearrange("p (h c) -> p h c", h=32),
                   ld.ap[:, 32:64].unsqueeze(2).to_broadcast([128, 32, 64]), ALU.mult, [c.X, ld], [c.xe])
                if dirn == 0:
                    c.xds = xdsr.next()
                    TT("pool", c.xds.ap.rearrange("p (h c) -> p h c", h=32), c.X.ap.rearrange("p (h c) -> p h c", h=32),
                       dsk.ap.unsqueeze(2).to_broadcast([128, 32, 64]), ALU.mult, [c.X, dsk], [c.xds])
                else:
                    c.yf = yfl.next()
                    DMA(c.yf.ap, yf_s[c.r0:c.r0 + 128, :], [db("yf", c.ck)], [c.yf])
                c.Y = yc.next()
                c.pupd = None

            def front_g(c, g):
                j = c.j
                pg = PR.next()
                MM(pg.ap[:, 0:128], c.Bt.ap[:, g, j * 128:(j + 1) * 128], c.Ct.ap[:, g, j * 128:(j + 1) * 128],
                   True, True, [c.Bt, c.Ct], [pg])
                G = gt.next()
                ACT(G.ap, pg.ap[:, 0:128], AF.Copy, [pg], [G])
                psg = PR.next()
                for hh in range(4):
                    h_ = g * 4 + hh
                    S.mm(psg.ap[:, hh * 128:(hh + 1) * 128], c.da.ap[:, h_:h_ + 1].to_broadcast([128, 128]), c.tri_c.ap,
                         start=(hh == 0), stop=False, reads=bs([c.da, c.tri_c]), writes=bs([psg]), skip_group_check=True)
                S.mm(psg.ap, ident.ap, c.mask.ap.rearrange("p h t -> p (h t)"), start=False, stop=True,
                     reads=bs([ident, c.mask]), writes=bs([psg]), skip_group_check=True)
                ex = exs.next()
                for hh in range(4):
                    ACT(ex.ap[:, hh, :], psg.ap[:, hh * 128:(hh + 1) * 128], AF.Exp, [psg, c.nc_], [ex],
                        bias=c.nc_.ap[:, g * 4 + hh:g * 4 + hh + 1])
                W = wt.next()
                TT("dve", W.ap, ex.ap, G.ap.unsqueeze(1).to_broadcast([128, 4, 128]), ALU.mult, [ex, G], [W])
                return W

            def back_g(c, g, W):
                j = c.j
                py = PR.next()
                for hh in range(4):
                    h_ = g * 4 + hh
                    if c.dirn == 0:
                        S.mm(py.ap[:, hh * 64:(hh + 1) * 64], W.ap[:, hh, :], c.X.ap[:, h_ * 64:(h_ + 1) * 64],
                             start=True, stop=False, reads=bs([W, c.X]), writes=bs([py]), skip_group_check=True)
                        S.mm(py.ap[:, hh * 64:(hh + 1) * 64], ident.ap, c.xds.ap[:, h_ * 64:(h_ + 1) * 64],
                             start=False, stop=True, reads=bs([ident, c.xds]), writes=bs([py]), skip_group_check=True)
                    else:
                        MM(py.ap[:, hh * 64:(hh + 1) * 64], W.ap[:, hh, :], c.X.ap[:, h_ * 64:(h_ + 1) * 64],
                           True, True, [W, c.X], [py])
                MM(py.ap[:, 256:512], c.Ct.ap[:, g, j * 128:(j + 1) * 128], stateb.ap[:, g * 256:(g + 1) * 256],
                   True, True, [c.Ct, stateb], [py])
                yt = ytmp.next()
                TT("dve", yt.ap.rearrange("p (h c) -> p h c", h=4), py.ap[:, 256:512].rearrange("p (h c) -> p h c", h=4),
                   c.e3.ap[:, g * 4:(g + 1) * 4].unsqueeze(2).to_broadcast([128, 4, 64]), ALU.mult, [py, c.e3], [yt])
                TT("dve", c.Y.ap[:, g * 256:(g + 1) * 256], yt.ap, py.ap[:, 0:256], ALU.add, [yt, py], [c.Y])
                if g % 2 == 0:
                    c.pupd = PR.next()
                MM(c.pupd.ap[:, (g % 2) * 256:(g % 2 + 1) * 256], c.Bm.ap[:, g * 128:(g + 1) * 128],
                   c.xe.ap[:, g * 256:(g + 1) * 256], True, True, [c.Bm, c.xe], [c.pupd])
                if g % 2 == 1:
                    g0 = g - 1
                    sl = slice(g0 * 256, (g0 + 2) * 256)
                    TT("pool", state.ap[:, sl].rearrange("p (h c) -> p h c", h=8),
                       state.ap[:, sl].rearrange("p (h c) -> p h c", h=8),
                       c.e3.ap[:, 64 + g0 * 4:64 + g0 * 4 + 8].unsqueeze(2).to_broadcast([128, 8, 64]), ALU.mult,
                       [state, c.e3], [state])
                    TT("dve", state.ap[:, sl], state.ap[:, sl], c.pupd.ap, ALU.add, [state, c.pupd], [state])
                    ACT(stateb.ap[:, sl], state.ap[:, sl], AF.Copy, [state], [stateb])

            def ep_stages(c):
                Y, r0, ck = c.Y, c.r0, c.ck
                if c.dirn == 0:
                    return [lambda: DMA(yf_s[r0:r0 + 128, :], Y.ap, [Y], [db("yf", ck)])]
                Z = c.Z
                st_ = {}

                def s1():
                    TT("pool", Y.ap, Y.ap, c.yf.ap, ALU.add, [Y, c.yf], [Y])

                def s2():
                    TT("dve", Y.ap, Y.ap, Z.ap, ALU.mult, [Y, Z], [Y])

                def s3():
                    ssg = ssgr.next()
                    st_["ssg"] = ssg
                    MS("pool", ssg.ap, 0.0, [ssg])
                    for g in range(8):
                        ACT(sq.ap[:, g * 256:(g + 1) * 256], Y.ap[:, g * 256:(g + 1) * 256], AF.Square, [Y, ssg], [sq, ssg],
                            accum_out=ssg.ap[:, g:g + 1])
                    ACT(ssg.ap[:, 8:16], ssg.ap[:, 0:8], AF.Ln, [ssg], [ssg], bias=EPS, scale=1.0 / 256.0)
                    ACT(ssg.ap[:, 16:24], ssg.ap[:, 8:16], AF.Exp, [ssg], [ssg], scale=-0.5)

                def s4():
                    ssg = st_["ssg"]
                    c.yb = ybr.next()
                    for g in range(8):
                        STT("dve", c.yb.ap[:, g * 256:(g + 1) * 256], Y.ap[:, g * 256:(g + 1) * 256], ssg.ap[:, 16 + g:17 + g],
                            ssdn.ap[:, g * 256:(g + 1) * 256], ALU.mult, ALU.mult, [Y, ssg, ssdn], [c.yb])

                def s5():
                    yb = c.yb
                    mt_ = mixT.next()
                    for c4 in range(4):
                        tr = TRR.next()
                        for i in range(4):
                            ct = c4 * 4 + i
                            S.tr(tr.ap[:, i * 128:(i + 1) * 128], yb.ap[:, ct * 128:(ct + 1) * 128], ident.ap,
                                 reads=bs([yb, ident]), writes=bs([tr]))
                        CP("dve", mt_.ap[:, c4 * 4:(c4 + 1) * 4, :], tr.ap.rearrange("p (i t) -> p i t", i=4), [tr], [mt_])
                    DMA(mix_s[:, r0:r0 + 128].rearrange("(i p) t -> p i t", p=128), mt_.ap, [mt_], [db("mix", ck)])

                return [s1, None, s2, None, s3, None, s4, None, s5]

            for dirn in range(2):
                MS("pool", state.ap, 0.0, [state])
                MS("pool", stateb.ap, 0.0, [stateb])
                blocks = list(range(nb)) if dirn == 0 else list(range(nb - 1, -1, -1))
                chunks = []
                for tb in blocks:
                    for j in (range(4) if dirn == 0 else range(3, -1, -1)):
                        c = Ctx()
                        c.dirn, c.tb, c.j = dirn, tb, j
                        chunks.append(c)
                items = [(c, g) for c in chunks for g in range(8)]
                deferred = []
                prologue(items[0][0])
                Wn = front_g(*items[0])
                for idx, (c, g) in enumerate(items):
                    Wc = Wn
                    if idx + 1 < len(items):
                        c2, g2 = items[idx + 1]
                        if g2 == 0:
                            prologue(c2)
                        Wn = front_g(c2, g2)
                    back_g(c, g, Wc)
                    for q_ in deferred:
                        if q_:
                            f_ = q_.pop(0)
                            if f_ is not None:
                                f_()
                    deferred = [q_ for q_ in deferred if q_]
                    if g == 7:
                        deferred.append(ep_stages(c))
                for q_ in deferred:
                    for f_ in q_:
                        if f_ is not None:
                            f_()

        def phase_D(l, si, t0, S_):
            new_phase()
            WR[0] = Ring([AR.alloc([8, 512], BF16, "w%d" % i) for i in range(5)])
            nk = 24 if l == 0 else 16
            hp0 = AR.p
            hid = AR.alloc([32, 512], BF16, "hid")
            sub = Arena(arena_t, AW)
            sub.reset(hp0)
            actT = sub.alloc([nk, 512], BF16, "actT")
            actT.b = hid.b
            d = alloc_front(host=(hp0, 8192, hid.b), with_dt=False)
            ot = AR.alloc([4, 1024], F32, "ot")
            rl = Ring([AR.alloc([512], F32, "rl%d" % i) for i in range(2)])
            xt, hb, hT = d["xt"], d["hb"], d["hT"]
            atm = hb
            g_post_mix = load_gain("norm_post_mix", l)
            g_pre_mlp = load_gain("norm_pre_mlp", l)
            g_post_mlp = load_gain("norm_post_mlp", l)
            if l == 0:
                d["g_pre_mix1"] = load_gain("norm_pre_mix", 1)
            nmix = nk - 8
            for tb in range(S_ // TB):
                c0t = tb * TB
                if l == 0:
                    DMA(xt.ap, x_in[t0 + c0t:t0 + c0t + TB, :].rearrange("(j p) d -> p j d", p=128), [], [xt])
                    for c4 in range(4):
                        DMA(actT.ap[:, 0:16, c4 * 128:(c4 + 1) * 128],
                            mix_s[:, c0t + c4 * 128:c0t + (c4 + 1) * 128].rearrange("(i p) t -> p i t", p=128),
                            [db("mix", tb * 4 + c4)], [actT])
                else:
                    DMA(xt.ap, x1_s[c0t:c0t + TB, :].rearrange("(j p) d -> p j d", p=128), [db("x1", tb)], [xt])
                    DMA(atm.ap, atm_s[c0t:c0t + TB, :].rearrange("(j p) c -> p j c", p=128),
                        [db("atm", (tb, hh)) for hh in range(8)], [atm])
                    transpose_block(atm, actT)
                DMA(actT.ap[:, nmix:nk, :], mo_s[l][:, c0t:c0t + TB].rearrange("(i p) t -> p i t", p=128),
                    [db("mo%d" % l, tb)], [actT])
                key = "out0" if l == 0 else "out1"
                for ch in range(2):
                    accs = [PR.next() for _ in range(4)]
                    for kc in range(nk // 8):
                        w = wload(key, kc * 1024, 8, ch * 512, 512)
                        for j in range(4):
                            for kt in range(8):
                                MM(accs[j].ap, actT.ap[:, kc * 8 + kt, j * 128:(j + 1) * 128], w.ap[:, kt, :],
                                   kc == 0 and kt == 0, kc == nk // 8 - 1 and kt == 7, [actT, w], [accs[j]])
                    for j in range(4):
                        evac(ot.ap[:, j, ch * 512:(ch + 1) * 512], accs[j].ap, [accs[j]], [ot])
                tm_norm(ot, g_post_mix, ot, d["ss"], d["sq"])
                TT("dve", xt.ap, xt.ap, ot.ap, ALU.add, [xt, ot], [xt])
                tm_norm(xt, g_pre_mlp, hb, d["ss"], d["sq"])
                transpose_block(hb, hT)
                for fc in range(8):
                    w = wload("w1", l * 1024, 8, fc * 512, 512)
                    for i in range(4):
                        pb = PR.next()
                        for kt in range(8):
                            MM(pb.ap, w.ap[:, kt, i * 128:(i + 1) * 128], hT.ap[:, kt, :], kt == 0, kt == 7, [w, hT], [pb])
                        r = rl.next()
                        ACT(r.ap, pb.ap, AF.Relu, [pb], [r])
                        TT("pool", hid.ap[:, fc * 4 + i, :], r.ap, r.ap, ALU.mult, [r], [hid])
                for ch in range(2):
                    accs = [PR.next() for _ in range(4)]
                    for fc in range(4):
                        w = wload("w2", l * 4096 + fc * 1024, 8, ch * 512, 512)
                        for j in range(4):
                            for ft in range(8):
                                MM(accs[j].ap, hid.ap[:, fc * 8 + ft, j * 128:(j + 1) * 128], w.ap[:, ft, :],
                                   fc == 0 and ft == 0, fc == 3 and ft == 7, [hid, w], [accs[j]])
                    for j in range(4):
                        evac(ot.ap[:, j, ch * 512:(ch + 1) * 512], accs[j].ap, [accs[j]], [ot])
                tm_norm(ot, g_post_mlp, ot, d["ss"], d["sq"])
                TT("dve", xt.ap, xt.ap, ot.ap, ALU.add, [xt, ot], [xt])
                if l == 0:
                    DMA(x1_s[c0t:c0t + TB, :].rearrange("(j p) d -> p j d", p=128), xt.ap, [xt], [db("x1", tb)], eng=STQ)
                    if lvl >= ORDER.index("B1"):
                        front(1, t0, S_, tb, d)
                else:
                    DMA(y_out[t0 + c0t:t0 + c0t + TB, :].rearrange("(j p) d -> p j d", p=128), xt.ap, [xt], [db("y", (si, tb))], eng=STQ)

        def phase_B1(si, S_):
            new_phase()
            nkt = S_ // 128
            nqb = S_ // TB
            qh = Ring([AR.alloc([S_], BF16, "qh%d" % i) for i in range(2)])
            kh = Ring([AR.alloc([S_], BF16, "kh%d" % i) for i in range(2)])
            vh = Ring([AR.alloc([nkt, 130], BF16, "vh%d" % i) for i in range(2)])
            bt6 = Ring([AR.alloc([6, 512], F32, "bt6_%d" % i) for i in range(2)])
            Er = Ring([AR.alloc([2, 512], BF16, "ae%d" % i) for i in range(3)])
            tmpr = Ring([AR.alloc([2, 512], F32, "atmp%d" % i) for i in range(2)])
            rr = Ring([AR.alloc([4], F32, "arr%d" % i) for i in range(8)])
            t1 = Ring([AR.alloc([128], F32, "at1_%d" % i) for i in range(4)])
            o_ = Ring([AR.alloc([128], F32, "ao%d" % i) for i in range(4)])
            sqa = AR.alloc([128], F32, "asq")
            ssa = Ring([AR.alloc([4], F32, "assa%d" % i) for i in range(4)])
            ob = Ring([AR.alloc([4, 128], BF16, "aob%d" % i) for i in range(2)])
            accs_r = Ring([AR.alloc([4, 260], F32, "accs%d" % i) for i in range(2)])
            for v in vh.items:
                MS("pool", v.ap[:, :, 128:130], 1.0, [v])
            accb = pbanks[0:4]
            prs = []
            for i in (4, 6):
                t_ = T(pall[:, i * 512:(i + 2) * 512].rearrange("p (c q) -> p c q", c=2), "pair%d" % i)
                t_.b.excl = True
                prs.append(t_)
            scr = Ring(prs)
            heads = {}

            def load_head(h):
                Q = qh.next()
                K = kh.next()
                V = vh.next()
                Bt = bt6.next()
                DMA(Q.ap, q_s[h * 128:(h + 1) * 128, 0:S_], [db("qd", b_) for b_ in range(nqb)], [Q])
                DMA(K.ap, k_s[h * 128:(h + 1) * 128, 0:S_], [db("kd", b_) for b_ in range(nqb)], [K])
                DMA(V.ap[:, :, 0:128], v_s[0:S_, h * 128:(h + 1) * 128].rearrange("(kt p) e -> p kt e", p=128),
                    [db("vd", b_) for b_ in range(nqb)], [V])
                for di in range(6):
                    delta = -128 + 128 * di
                    src = bass.AP(rep_t, h * 128 * RL + 640 - delta, [[RL - 1, 128], [1, 512]])
                    DMA(Bt.ap[:, di, :], src, [db("rep", h)], [Bt])
                heads[h] = (Q, K, V, Bt)

            def emit_scores(h, qb, kt):
                Q, K, V, Bt = heads[h]
                delta = kt * 128 - qb * TB
                pair = scr.next()
                for c in range(2):
                    MM(pair.ap[:, c, :], K.ap[c * 64:(c + 1) * 64, kt * 128:(kt + 1) * 128],
                       Q.ap[c * 64:(c + 1) * 64, qb * TB:(qb + 1) * TB], True, True, [K, Q], [pair])
                e = Er.next()
                if -128 <= delta <= 512:
                    tm = tmpr.next()
                    STT("dve", tm.ap, pair.ap, 0.125, Bt.ap[:, (delta + 128) // 128, :].unsqueeze(1).to_broadcast([128, 2, 512]),
                        ALU.mult, ALU.add, [pair, Bt], [tm])
                    ACT(e.ap, tm.ap, AF.Exp, [tm], [e])
                else:
                    cbias = tab15 if delta < 0 else tab31
                    ACT(e.ap, pair.ap, AF.Exp, [pair, cbias], [e], scale=0.125, bias=cbias.ap[:, h:h + 1])
                return e

            def emit_pv(h, qb, kt, es):
                Q, K, V, Bt = heads[h]
                for jq in range(4):
                    av = accb[jq].ap[:, 0:260].rearrange("p (c e) -> p c e", c=2)
                    for c in range(2):
                        S.mm(av[:, c, :], es.ap[:, c, jq * 128:(jq + 1) * 128], V.ap[:, kt, :],
                             start=(kt == 0 and c == 0), stop=(kt == nkt - 1 and c == 1),
                             reads=bs([es, V]), writes=bs([accb[jq]]), skip_group_check=True)

            def epilogue(h, qb):
                OB = ob.next()
                AS = accs_r.next()
                rs_ = []
                for jq in range(4):
                    CP("dve", AS.ap[:, jq, :], accb[jq].ap[:, 0:260], [accb[jq]], [AS])
                for jq in range(4):
                    av = AS.ap[:, jq, :].rearrange("p (c e) -> p c e", c=2)
                    r = rr.next()
                    S.emit("dve", lambda hh, o=r.ap[:, 0:2], i=av[:, :, 128]: hh.reciprocal(o, i), bs([AS]), bs([r]))
                    rs_.append(r)
                for jq in range(4):
                    av = AS.ap[:, jq, :].rearrange("p (c e) -> p c e", c=2)
                    r = rs_[jq]
                    t_ = t1.next()
                    TS("pool", t_.ap, av[:, 0, 0:128], r.ap[:, 0:1], None, ALU.mult, None, [AS, r], [t_])
                    oo = o_.next()
                    TS("pool", oo.ap, av[:, 1, 0:128], r.ap[:, 1:2], lamt.ap[:, 1:2], ALU.mult, ALU.mult, [AS, r, lamt], [oo])
                    TT("pool", oo.ap, oo.ap, t_.ap, ALU.add, [oo, t_], [oo])
                    sa = ssa.next()
                    MS("pool", sa.ap, 0.0, [sa])
                    ACT(sqa.ap, oo.ap, AF.Square, [oo, sa], [sqa, sa], accum_out=sa.ap[:, 0:1])
                    ACT(sa.ap[:, 1:2], sa.ap[:, 0:1], AF.Ln, [sa], [sa], bias=EPS, scale=1.0 / 128.0)
                    ACT(sa.ap[:, 2:3], sa.ap[:, 1:2], AF.Exp, [sa], [sa], scale=-0.5)
                    TS("pool", oo.ap, oo.ap, sa.ap[:, 2:3], None, ALU.mult, None, [oo, sa], [oo])
                    TT("pool", OB.ap[:, jq, :], oo.ap, subln.ap, ALU.mult, [oo, subln], [OB])
                DMA(atm_s[qb * TB:(qb + 1) * TB, h * 128:(h + 1) * 128].rearrange("(j p) e -> p j e", p=128), OB.ap,
                    [OB], [db("atm", (qb, h))])

            steps = [(h, qb, kt) for h in range(8) for qb in range(nqb) for kt in range(nkt)]
            load_head(0)
            pend = emit_scores(*steps[0])
            for i, (h, qb, kt) in enumerate(steps):
                if qb == 0 and kt == 0 and h + 1 < 8:
                    load_head(h + 1)
                nxt = emit_scores(*steps[i + 1]) if i + 1 < len(steps) else None
                emit_pv(h, qb, kt, pend)
                if kt == nkt - 1:
                    epilogue(h, qb)
                pend = nxt

        t0 = 0
        for si, S_ in enumerate(seqs):
            if lvl >= ORDER.index("KV"):
                phase_KV(si, si * 256)
            if lvl >= ORDER.index("A0"):
                phase_A0(si, t0, S_)
            if lvl >= ORDER.index("B0"):
                phase_B0(si, S_)
            if lvl >= ORDER.index("C0"):
                phase_C0(si, S_)
            if lvl >= ORDER.index("D0"):
                phase_D(0, si, t0, S_)
            if lvl >= ORDER.index("B1"):
                phase_B1(si, S_)
            if lvl >= ORDER.index("D1"):
                phase_D(1, si, t0, S_)
            t0 += S_
        S.finish()
        S.materialize()
        nops = {e: len(S.ops[e]) for e in S.ENGS}
    return nc, nops


def _rel_onehot():
    rel_np = np.arange(640, 640 - RL, -1, dtype=np.int32)
    try:
        import jax
        import jax.numpy as jnp
        with jax.default_device(jax.devices("cpu")[0]):
            rel = jnp.asarray(rel_np)
            nb, max_exact = 16, 8
            ret = jnp.where(rel > 0, nb, 0)
            n = jnp.abs(rel)
            nf = jnp.maximum(n, 1).astype(jnp.float32)
            large = max_exact + (jnp.log(nf / max_exact) / math.log(128 / max_exact) * (nb - max_exact)).astype(jnp.int32)
            large = jnp.minimum(large, nb - 1)
            bucket = np.asarray(ret + jnp.where(n < max_exact, n, large))
    except Exception:
        n = np.abs(rel_np)
        nf = np.maximum(n, 1).astype(np.float32)
        large = 8 + (np.log(nf / np.float32(8)) / np.float32(math.log(16)) * np.float32(8)).astype(np.int32)
        large = np.minimum(large, 15)
        bucket = np.where(rel_np > 0, 16, 0) + np.where(n < 8, n, large)
    oh = np.zeros((32, RL), np.float32)
    oh[bucket, np.arange(RL)] = 1.0
    return oh


_PROG = {}


def _param_maps(inputs):
    m = {}
    for n, shp in PARAMS:
        m[n] = np.ascontiguousarray(np.asarray(inputs[n], dtype=np.float32).reshape(shp))
    m["onehot"] = _rel_onehot()
    return m


def run_cores(seqs, xs, mems, inputs, dbg=False, upto="END"):
    key = (tuple(seqs), dbg, upto)
    if key not in _PROG:
        _PROG[key] = build(list(seqs), dbg=dbg, upto=upto)[0]
    nc = _PROG[key]
    pm = _param_maps(inputs)
    in_maps = []
    for x, mm_ in zip(xs, mems):
        d = dict(pm)
        d["x"] = np.ascontiguousarray(x, dtype=np.float32)
        d["mem"] = np.ascontiguousarray(mm_, dtype=np.float32)
        in_maps.append(d)
    res = run_bass_kernel_spmd(nc, in_maps, core_ids=list(range(len(xs))))
    return res.results


def kernel(**inputs):
    xp = np.asarray(inputs["x_prompt"], dtype=np.float32)
    xs_ = np.asarray(inputs["x_sample"], dtype=np.float32)
    mp = np.asarray(inputs["mem_prompt"], dtype=np.float32)
    ms = np.asarray(inputs["mem_sample"], dtype=np.float32)
    seqs = [4096, 2048, 2048, 2048, 2048]
    xs, mems = [], []
    for c in range(8):
        xs.append(np.concatenate([xp[c], xs_[4 * c:4 * c + 4].reshape(-1, D)], axis=0))
        mems.append(np.concatenate([mp[c], ms[4 * c:4 * c + 4].reshape(-1, D)], axis=0))
    res = run_cores(seqs, xs, mems, inputs)
    y_prompt = np.empty_like(xp)
    y_sample = np.empty_like(xs_)
    for c in range(8):
        y = np.asarray(res[c]["y"], dtype=np.float32)
        y_prompt[c] = y[0:4096]
        y_sample[4 * c:4 * c + 4] = y[4096:].reshape(4, 2048, D)
    return (y_prompt, y_sample)
```

```python
import numpy as np
import concourse.bass as bass
import concourse.mybir as mybir
from concourse.bass_utils import run_bass_kernel_spmd
from contextlib import ExitStack

F32 = mybir.dt.float32
BF16 = mybir.dt.bfloat16
AF = mybir.ActivationFunctionType
ALU = mybir.AluOpType
AX = mybir.AxisListType


class Buf:
    __slots__ = ("w", "rs", "name", "excl")

    def __init__(self, name="", excl=False):
        self.w = None
        self.rs = []
        self.name = name
        self.excl = excl


class Sched:
    ENGS = ("pe", "act", "dve", "pool", "sp")

    def __init__(self, nc, stack, n_dma_sems=40):
        self.nc = nc
        self.ops = {e: [] for e in self.ENGS}
        self.esem = {e: stack.enter_context(nc.semaphore("s_" + e)) for e in self.ENGS}
        self.dsem = [stack.enter_context(nc.semaphore("d%d" % i)) for i in range(n_dma_sems)]
        self.duse = [0] * n_dma_sems
        self.dnext = 0
        self.seen_c = {e: {} for e in self.ENGS}
        self.seen_d = {e: {} for e in self.ENGS}
        self.signal = {e: set() for e in self.ENGS}

    def _need(self, eng, tok, waits, kind):
        if tok is None:
            return
        if tok[0] == "c":
            _, te, idx = tok
            if te == eng and (kind != "raw" or eng == "pe"):
                return
            if self.seen_c[eng].get(te, -1) >= idx:
                return
            self.seen_c[eng][te] = idx
            self.signal[te].add(idx)
            waits[:] = [w for w in waits if not (w[0] == "c" and w[1] == te)]
            waits.append(tok)
        else:
            _, k, val = tok
            if self.seen_d[eng].get(k, 0) >= val:
                return
            self.seen_d[eng][k] = val
            waits[:] = [w for w in waits if not (w[0] == "d" and w[1] == k)]
            waits.append(tok)

    def emit(self, eng, fn, reads=(), writes=(), dma=False):
        waits = []
        if any(b.excl for b in reads):
            writes = list(writes) + [b for b in reads if b.excl]
            reads = [b for b in reads if not b.excl]
        for b in reads:
            for t in (b.w or ()):
                self._need(eng, t, waits, "raw")
        for b in writes:
            for t in (b.w or ()):
                self._need(eng, t, waits, "waw")
            for r in b.rs:
                self._need(eng, r, waits, "war")
        idx = len(self.ops[eng])
        if dma:
            k = self.dnext
            self.dnext = (self.dnext + 1) % len(self.dsem)
            if self.duse[k] > 0:
                self._need(eng, ("d", k, 16 * self.duse[k]), waits, "raw")
            self.duse[k] += 1
            tok = ("d", k, 16 * self.duse[k])
        else:
            tok = ("c", eng, idx)
        self.ops[eng].append((fn, waits, tok))
        for b in reads:
            b.rs.append(tok)
        for b in writes:
            if dma and b.w and not b.rs and all(t[0] == "d" for t in b.w):
                b.w = b.w + [tok]
            else:
                b.w = [tok]
            b.rs = []
        return tok

    def finish(self):
        waits = []
        for k, u in enumerate(self.duse):
            if u > 0 and self.seen_d["sp"].get(k, 0) < 16 * u:
                waits.append(("d", k, 16 * u))
        self.ops["sp"].append((None, waits, None))

    def barrier(self):
        waits = []
        for k, u in enumerate(self.duse):
            if u > 0:
                self._need("sp", ("d", k, 16 * u), waits, "raw")
        for e in ("pe", "act", "dve", "pool"):
            for i in range(len(self.ops[e]) - 1, -1, -1):
                fn, w, tok = self.ops[e][i]
                if fn is not None and tok is not None and tok[0] == "c":
                    self._need("sp", tok, waits, "raw")
                    break
        tok_sp = ("c", "sp", len(self.ops["sp"]))
        self.ops["sp"].append((lambda h: h.nop(), waits, tok_sp))
        for e in ("pe", "act", "dve", "pool"):
            w = []
            self._need(e, tok_sp, w, "raw")
            if w:
                self.ops[e].append((None, w, None))

    def materialize(self):
        nc = self.nc
        cnt = {}
        for e in self.ENGS:
            c = 0
            m = {}
            for i in sorted(self.signal[e]):
                c += 1
                m[i] = c
            cnt[e] = m
        with nc.Block() as block:
            def run(e, h):
                for i, (fn, waits, tok) in enumerate(self.ops[e]):
                    for w in waits:
                        if w[0] == "c":
                            h.wait_ge(self.esem[w[1]], cnt[w[1]][w[2]])
                        else:
                            h.wait_ge(self.dsem[w[1]], w[2])
                    if fn is None:
                        continue
                    ins = fn(h)
                    if tok[0] == "d":
                        ins.then_inc(self.dsem[tok[1]], 16)
                    elif i in cnt[e]:
                        ins.then_inc(self.esem[e], 1)

            @block.tensor
            def _(h):
                run("pe", h)

            @block.scalar
            def _(h):
                run("act", h)

            @block.vector
            def _(h):
                run("dve", h)

            @block.gpsimd
            def _(h):
                run("pool", h)

            @block.sync
            def _(h):
                run("sp", h)

    def mm(self, out, lhsT, rhs, start=True, stop=True, reads=(), writes=(), **kw):
        return self.emit("pe", lambda e: e.matmul(out, lhsT, rhs, start=start, stop=stop, **kw), reads, writes)

    def tr(self, out, in_, ident, reads=(), writes=()):
        return self.emit("pe", lambda e: e.transpose(out, in_, ident), reads, writes)

    def act(self, out, in_, func, reads=(), writes=(), **kw):
        return self.emit("act", lambda e: e.activation(out, in_, func, **kw), reads, writes)

    def dma(self, eng, out, in_, reads=(), writes=(), **kw):
        return self.emit(eng, lambda e: e.dma_start(out=out, in_=in_, **kw), reads, writes, dma=True)

    def tt(self, eng, out, in0, in1, op, reads=(), writes=()):
        return self.emit(eng, lambda e: e.tensor_tensor(out, in0, in1, op), reads, writes)

    def ts(self, eng, out, in0, s1, s2, op0, op1=None, reads=(), writes=(), **kw):
        if op1 is None:
            return self.emit(eng, lambda e: e.tensor_scalar(out, in0, s1, s2, op0, **kw), reads, writes)
        return self.emit(eng, lambda e: e.tensor_scalar(out, in0, s1, s2, op0, op1, **kw), reads, writes)

    def stt(self, eng, out, in0, scalar, in1, op0, op1, reads=(), writes=()):
        return self.emit(eng, lambda e: e.scalar_tensor_tensor(out, in0, scalar, in1, op0, op1), reads, writes)

    def copy(self, eng, out, in_, reads=(), writes=()):
        if eng == "act":
            return self.emit(eng, lambda e: e.activation(out, in_, AF.Copy), reads, writes)
        return self.emit(eng, lambda e: e.tensor_copy(out, in_), reads, writes)

    def memset(self, eng, ap, val, writes=()):
        return self.emit(eng, lambda e: e.memset(ap, val), (), writes)
import math


def _prod(s):
    r = 1
    for v in s:
        r *= v
    return r


class T:
    __slots__ = ("ap", "b")

    def __init__(self, ap, name=""):
        self.ap = ap
        self.b = Buf(name)

    def __getitem__(self, k):
        return self.ap[k]


class Arena:
    def __init__(self, tensor, width):
        self.t = tensor
        self.W = width
        self.p = 0

    def reset(self, p=0):
        self.p = p

    def alloc(self, free_shape, dtype, name=""):
        n = _prod(free_shape)
        words = n if dtype == F32 else (n + 1) // 2
        words = (words + 7) // 8 * 8
        assert self.p + words <= self.W, ("arena overflow", name, self.p, words, self.W)
        ap = self.t[:, self.p:self.p + words]
        self.p += words
        if dtype != F32:
            ap = ap.bitcast(dtype)
        ap = ap[:, 0:n]
        if len(free_shape) == 2:
            ap = ap.rearrange("p (a b) -> p a b", a=free_shape[0])
        elif len(free_shape) == 3:
            ap = ap.rearrange("p (a b c) -> p a b c", a=free_shape[0], b=free_shape[1])
        return T(ap, name)


class Ring:
    def __init__(self, items):
        self.items = items
        self.i = 0

    def next(self):
        it = self.items[self.i % len(self.items)]
        self.i += 1
        return it
D = 1024
KT = 8
TB = 512
EPS = 1e-6
LAM_INIT1 = 0.8 - 0.6 * math.exp(-0.3 * 1)
NEG = -30000.0
RL = 1280

PARAMS = [
    ("rel_bias_table", [32, 8]), ("norm_pre_mix", [2, 1024]), ("norm_post_mix", [2, 1024]),
    ("norm_pre_mlp", [2, 1024]), ("norm_post_mlp", [2, 1024]), ("ssd_w_in", [1024, 7232]),
    ("ssd_conv_w", [5, 4096]), ("ssd_conv_b", [1, 4096]), ("ssd_dt_bias", [1, 64]),
    ("ssd_a_log", [1, 64]), ("ssd_d", [1, 32]), ("ssd_norm", [1, 2048]), ("ssd_w_out", [3072, 1024]),
    ("diff_w_in", [1024, 4096]), ("diff_lambda", [1, 256]), ("diff_subln", [1, 128]),
    ("diff_w_out", [2048, 1024]), ("x_mem_norm", [2, 1024]), ("x_w_kv", [2048, 2048]),
    ("mlp_w1", [2048, 4096]), ("mlp_w2", [8192, 1024]),
]


def build(seqs, dbg=False, upto="END"):
    nc = bass.Bass("TRN2", target_bir_lowering=False)
    NS = len(seqs)
    NT = sum(seqs)
    SM = max(seqs)
    ORDER = ["W", "KV", "A0", "B0", "C0", "D0", "B1", "D1", "END"]
    lvl = ORDER.index(upto)

    def din(name, shape):
        return nc.dram_tensor(name, shape, F32, kind="ExternalInput").ap()

    x_in = din("x", [NT, D])
    mem_in = din("mem", [NS * 256, D])
    P = {n: din(n, s) for n, s in PARAMS}
    oh_in = din("onehot", [32, RL])
    y_out = nc.dram_tensor("y", [NT, D], F32, kind="ExternalOutput").ap()
    skind = "ExternalOutput" if dbg else "Internal"

    def dscr(name, shape, dt, k=None):
        return nc.dram_tensor(name, shape, dt, kind=(k or skind)).ap()

    WB = {
        "in0": dscr("wb_in0", [1024, 7232], BF16, "Internal"), "out0": dscr("wb_out0", [3072, 1024], BF16, "Internal"),
        "in1": dscr("wb_in1", [1024, 4096], BF16, "Internal"), "out1": dscr("wb_out1", [2048, 1024], BF16, "Internal"),
        "kv": dscr("wb_kv", [2048, 2048], BF16, "Internal"), "w1": dscr("wb_w1", [2048, 4096], BF16, "Internal"),
        "w2": dscr("wb_w2", [8192, 1024], BF16, "Internal"),
    }
    WSRC = {"in0": "ssd_w_in", "out0": "ssd_w_out", "in1": "diff_w_in", "out1": "diff_w_out",
            "kv": "x_w_kv", "w1": "mlp_w1", "w2": "mlp_w2"}
    z_s = dscr("z_s", [SM, 2048], BF16)
    xbc_s = dscr("xbc_s", [4096, SM], BF16)
    dt_s = dscr("dt_s", [SM, 64], F32)
    mo_s = [dscr("mo_s%d" % l, [1024, SM], BF16) for l in range(2)]
    ct_s = dscr("ct_s", [1024, SM], BF16)
    bt_s = dscr("bt_s", [1024, SM], BF16)
    btm_s = dscr("btm_s", [SM, 1024], BF16)
    xtm_s = dscr("xtm_s", [SM, 2048], BF16)
    yf_s = dscr("yf_s", [SM, 2048], F32)
    mix_s = dscr("mix_s", [2048, SM], BF16)
    atm_s = dscr("atm_s", [SM, 1024], BF16)
    x1_s = dscr("x1_s", [SM, 1024], F32)
    q_s = dscr("q_s", [1024, SM], BF16)
    k_s = dscr("k_s", [1024, SM], BF16)
    v_s = dscr("v_s", [SM, 1024], BF16)
    vecd = dscr("vecd", [8, RL], F32)
    rep_t = nc.dram_tensor("rep", [8 * 128, RL], F32, kind=skind)
    rep = rep_t.ap()

    DBUF = {}

    def db(name, blk=0):
        k = (name, blk)
        if k not in DBUF:
            DBUF[k] = Buf(name)
        return DBUF[k]

    with ExitStack() as st:
        S = Sched(nc, st, n_dma_sems=48)
        AW = 50000
        arena_t = st.enter_context(nc.sbuf_tensor("arena", [128, AW], F32))
        AR = Arena(arena_t, AW)
        pbanks = []
        pall = st.enter_context(nc.psum_tensor("pall", [128, 4096], F32))
        for i in range(8):
            pbanks.append(T(pall[:, i * 512:(i + 1) * 512], "pb%d" % i))
            pbanks[-1].b.excl = True
        PR = Ring(pbanks[0:6])
        trs = []
        for i in (6, 7):
            t_ = T(pbanks[i].ap.bitcast(BF16)[:, 0:512], "tr%d" % i)
            t_.b = pbanks[i].b
            trs.append(t_)
        TRR = Ring(trs)

        ident_f = AR.alloc([128], F32, "ident_f")
        ident = AR.alloc([128], BF16, "ident")
        ones_b = AR.alloc([128], BF16, "ones_b")
        ones_f = AR.alloc([128], F32, "ones_f")
        tri_f = AR.alloc([128], F32, "tri_f")
        triu_f = AR.alloc([128], F32, "triu_f")
        tris_f = AR.alloc([128], F32, "tris_f")
        trisl_f = AR.alloc([128], F32, "trisl_f")
        maskf = AR.alloc([4, 128], F32, "maskf")
        maskb = AR.alloc([4, 128], F32, "maskb")
        maskf_b = AR.alloc([4, 128], BF16, "maskf_b")
        maskb_b = AR.alloc([4, 128], BF16, "maskb_b")
        ssdn = AR.alloc([2048], F32, "ssdn")
        dtb = AR.alloc([64], F32, "dtb")
        avec = AR.alloc([64], F32, "avec")
        dsk = AR.alloc([32], F32, "dsk")
        subln = AR.alloc([128], F32, "subln")
        lamt = AR.alloc([8], F32, "lamt")
        tab15 = AR.alloc([8], F32, "tab15")
        tab31 = AR.alloc([8], F32, "tab31")
        cw = AR.alloc([32, 5], F32, "cw")
        cb = AR.alloc([32], F32, "cb")
        kmT = [AR.alloc([8, 256], BF16, "kmT%d" % l) for l in range(2)]
        vm = [AR.alloc([2, 1024], BF16, "vm%d" % l) for l in range(2)]
        PERSIST = AR.p

        def A_(e, o, i, f, r=(), w=(), **kw):
            return S.emit("act", lambda h: h.activation(o, i, f, **kw), [t.b for t in r], [t.b for t in w])

        def bs(ts):
            return [t if isinstance(t, Buf) else t.b for t in ts]

        def MM(out, lhsT, rhs, start, stop, r, w):
            return S.mm(out, lhsT, rhs, start=start, stop=stop, reads=bs(r), writes=bs(w))

        def ACT(o, i, f, r, w, **kw):
            return S.emit("act", lambda h: h.activation(o, i, f, **kw), bs(r), bs(w))

        def TT(e, o, a, b, op, r, w):
            return S.tt(e, o, a, b, op, reads=bs(r), writes=bs(w))

        def TS(e, o, a, s1, s2, op0, op1, r, w):
            return S.ts(e, o, a, s1, s2, op0, op1, reads=bs(r), writes=bs(w))

        def STT(e, o, a, sc, b, op0, op1, r, w):
            return S.stt(e, o, a, sc, b, op0, op1, reads=bs(r), writes=bs(w))

        def CP(e, o, i, r, w):
            return S.copy(e, o, i, reads=bs(r), writes=bs(w))

        def DMA(o, i, r, w, eng="sp"):
            return S.dma(eng, o, i, reads=bs(r), writes=bs(w))

        def MS(e, ap, val, w):
            return S.memset(e, ap, val, writes=bs(w))

        def AFS(t, pattern, op, fill, base, cm):
            S.emit("pool", lambda h: h.affine_select(t.ap, t.ap, pattern, op, fill, base=base, channel_multiplier=cm),
                   bs([t]), bs([t]))

        MS("pool", ident_f.ap, 1.0, [ident_f])
        AFS(ident_f, [[-1, 128]], ALU.is_equal, 0.0, 0, 1)
        CP("dve", ident.ap, ident_f.ap, [ident_f], [ident])
        MS("pool", ones_f.ap, 1.0, [ones_f])
        MS("pool", ones_b.ap, 1.0, [ones_b])
        MS("pool", tri_f.ap, 1.0, [tri_f])
        AFS(tri_f, [[1, 128]], ALU.is_ge, 0.0, 0, -1)
        MS("pool", triu_f.ap, 1.0, [triu_f])
        AFS(triu_f, [[-1, 128]], ALU.is_ge, 0.0, 0, 1)
        MS("pool", tris_f.ap, 1.0, [tris_f])
        AFS(tris_f, [[-1, 128]], ALU.is_gt, 0.0, 0, 1)
        MS("pool", trisl_f.ap, 1.0, [trisl_f])
        AFS(trisl_f, [[1, 128]], ALU.is_gt, 0.0, 0, -1)
        MS("pool", maskf.ap, 0.0, [maskf])
        S.emit("pool", lambda h: h.affine_select(maskf.ap, maskf.ap, [[0, 4], [1, 128]], ALU.is_ge, NEG, base=0,
                                                 channel_multiplier=-1), bs([maskf]), bs([maskf]))
        MS("pool", maskb.ap, 0.0, [maskb])
        S.emit("pool", lambda h: h.affine_select(maskb.ap, maskb.ap, [[0, 4], [-1, 128]], ALU.is_ge, NEG, base=0,
                                                 channel_multiplier=1), bs([maskb]), bs([maskb]))
        CP("dve", maskf_b.ap, maskf.ap, [maskf], [maskf_b])
        CP("dve", maskb_b.ap, maskb.ap, [maskb], [maskb_b])

        def load_gain(nm, l):
            g = AR.alloc([1024], F32, "g_" + nm)
            DMA(g.ap, P[nm][l:l + 1, :].partition_broadcast(128), [], [g])
            return g
        DMA(ssdn.ap, P["ssd_norm"][0:1, :].partition_broadcast(128), [], [ssdn])
        DMA(dtb.ap, P["ssd_dt_bias"][0:1, :].partition_broadcast(128), [], [dtb])
        DMA(avec.ap, P["ssd_a_log"][0:1, :].partition_broadcast(128), [], [avec])
        DMA(dsk.ap, P["ssd_d"][0:1, :].partition_broadcast(128), [], [dsk])
        DMA(subln.ap, P["diff_subln"][0:1, :].partition_broadcast(128), [], [subln])
        DMA(tab15.ap, P["rel_bias_table"][15:16, :].partition_broadcast(128), [], [tab15])
        DMA(tab31.ap, P["rel_bias_table"][31:32, :].partition_broadcast(128), [], [tab31])
        ACT(avec.ap, avec.ap, AF.Exp, [avec], [avec])
        TS("dve", avec.ap, avec.ap, -1.0, None, ALU.mult, None, [avec], [avec])
        TS("dve", subln.ap, subln.ap, 1.0 - LAM_INIT1, None, ALU.mult, None, [subln], [subln])
        AR.reset(PERSIST)
        lp = AR.alloc([4, 64], F32, "lp")
        lpp = AR.alloc([2, 64], F32, "lpp")
        lps = AR.alloc([2], F32, "lps")
        DMA(lp.ap.rearrange("p a b -> p (a b)"), P["diff_lambda"][0:1, :].partition_broadcast(128), [], [lp])
        lp4 = lp.ap.rearrange("p (a c) b -> p a c b", c=2)
        TT("dve", lpp.ap, lp4[:, :, 0, :], lp4[:, :, 1, :], ALU.mult, [lp], [lpp])
        S.emit("dve", lambda h: h.tensor_reduce(lps.ap, lpp.ap, AX.X, ALU.add), bs([lpp]), bs([lps]))
        ACT(lps.ap, lps.ap, AF.Exp, [lps], [lps])
        TT("dve", lamt.ap[:, 0:1], lps.ap[:, 0:1], lps.ap[:, 1:2], ALU.subtract, [lps], [lamt])
        TS("dve", lamt.ap[:, 0:1], lamt.ap[:, 0:1], LAM_INIT1, None, ALU.add, None, [lamt], [lamt])
        TS("dve", lamt.ap[:, 1:2], lamt.ap[:, 0:1], -1.0, None, ALU.mult, None, [lamt], [lamt])
        cwr = AR.alloc([2, 128], F32, "cwr")
        cbr = AR.alloc([128], F32, "cbr")
        MS("pool", cwr.ap, 0.0, [cwr])
        MS("pool", cbr.ap, 0.0, [cbr])
        cw_rows = P["ssd_conv_w"].rearrange("k (ct p) -> (k ct) p", p=128)
        DMA(cwr.ap[:, 0, :], cw_rows[0:128, :], [], [cwr])
        DMA(cwr.ap[0:32, 1, :], cw_rows[128:160, :], [], [cwr])
        DMA(cbr.ap[0:32, :], P["ssd_conv_b"].rearrange("o (ct p) -> (o ct) p", p=128), [], [cbr])
        pb = PR.next()
        MM(pb.ap[:, 0:128], cwr.ap[:, 0, :], ident_f.ap, True, True, [cwr, ident_f], [pb])
        MM(pb.ap[:, 128:160], cwr.ap[0:32, 1, :], ident_f.ap[0:32, 0:32], True, True, [cwr, ident_f], [pb])
        MM(pb.ap[:, 160:192], cbr.ap[0:32, :], ident_f.ap[0:32, 0:32], True, True, [cbr, ident_f], [pb])
        CP("dve", cw.ap.rearrange("p ct k -> p k ct"), pb.ap[:, 0:160].rearrange("p (k ct) -> p k ct", k=5), [pb], [cw])
        CP("dve", cb.ap, pb.ap[:, 160:192], [pb], [cb])
        tabs = AR.alloc([8], F32, "tabs")
        ohs = AR.alloc([RL], F32, "ohs")
        rv = AR.alloc([RL], F32, "rv")
        DMA(tabs.ap[0:32, :], P["rel_bias_table"][:, :], [], [tabs])
        DMA(ohs.ap[0:32, :], oh_in[:, :], [], [ohs])
        for c0 in range(0, RL, 512):
            n = min(512, RL - c0)
            pb = PR.next()
            MM(pb.ap[0:8, 0:n], tabs.ap[0:32, :], ohs.ap[0:32, c0:c0 + n], True, True, [tabs, ohs], [pb])
            CP("dve", rv.ap[0:8, c0:c0 + n], pb.ap[0:8, 0:n], [pb], [rv])
        DMA(vecd[:, :], rv.ap[0:8, :], [rv], [db("vecd")])
        S.barrier()
        for h in range(8):
            DMA(rep[h * 128:(h + 1) * 128, :], vecd[h:h + 1, :].partition_broadcast(128), [db("vecd")], [db("rep", h)])

        def emit_cast(key, r0):
            DMA(WB[key][r0:r0 + 256, :], P[WSRC[key]][r0:r0 + 256, :], [], [db("w_" + key, r0)], eng="pool")

        for key in ("kv", "in0"):
            for r0 in range(0, P[WSRC[key]].shape[0], 256):
                emit_cast(key, r0)
        pending_casts = []
        for key, lo, hi in (("out0", 0, 3072), ("w1", 0, 1024), ("w2", 0, 4096), ("in1", 0, 1024),
                            ("out1", 0, 2048), ("w1", 1024, 2048), ("w2", 4096, 8192)):
            for r0 in range(lo, hi, 256):
                pending_casts.append((key, r0))

        def drip(n):
            for _ in range(min(n, len(pending_casts))):
                emit_cast(*pending_casts.pop(0))
        S.barrier()

        def new_phase():
            S.barrier()
            AR.reset(PERSIST)

        WR = [None]

        def wload(key, r0, nkt, c0, ncols):
            t = WR[0].next()
            src = WB[key][r0:r0 + nkt * 128, c0:c0 + ncols].rearrange("(kt p) c -> p kt c", p=128)
            deps = [db("w_" + key, r) for r in range(r0 - r0 % 256, r0 + nkt * 128, 256)]
            assert all(d_.w for d_ in deps), ("weight block not cast yet", key, r0)
            DMA(t.ap[:, 0:nkt, 0:ncols], src, deps, [t])
            return t

        STQ = "pool"
        evac_i = [0]

        def evac(out_ap, in_ap, r, w):
            evac_i[0] += 1
            if evac_i[0] % 2 == 0:
                ACT(out_ap, in_ap, AF.Copy, r, w)
            else:
                CP("dve", out_ap, in_ap, r, w)

        def tm_norm(src, g, dst, ss, sq, r_extra=()):
            MS("pool", ss.ap, 0.0, [ss])
            for j in range(4):
                ACT(sq.ap, src.ap[:, j, :], AF.Square, [src, ss] + list(r_extra), [sq, ss], accum_out=ss.ap[:, j:j + 1])
            ACT(ss.ap[:, 4:8], ss.ap[:, 0:4], AF.Ln, [ss], [ss], bias=EPS, scale=1.0 / D)
            ACT(ss.ap[:, 8:12], ss.ap[:, 4:8], AF.Exp, [ss], [ss], scale=-0.5)
            for j in range(4):
                STT("dve", dst.ap[:, j, :], src.ap[:, j, :], ss.ap[:, 8 + j:9 + j], g.ap, ALU.mult, ALU.mult,
                    [src, ss, g], [dst])

        def transpose_block(hb, hT, nkt=8):
            for kt in range(nkt):
                tr = TRR.next()
                for j in range(4):
                    S.tr(tr.ap[:, j * 128:(j + 1) * 128], hb.ap[:, j, kt * 128:(kt + 1) * 128], ident.ap,
                         reads=bs([hb, ident]), writes=bs([tr]))
                CP("dve", hT.ap[:, kt, :], tr.ap, [tr], [hT])

        def cross_attn(l, qT, moT, E, rden):
            for h in range(4):
                es = []
                for mt in range(2):
                    pb = PR.next()
                    for dt_ in range(2):
                        MM(pb.ap, kmT[l].ap[:, 2 * h + dt_, mt * 128:(mt + 1) * 128], qT.ap[:, 2 * h + dt_, :],
                           dt_ == 0, dt_ == 1, [kmT[l], qT], [pb])
                    e = E.next()
                    ACT(e.ap, pb.ap, AF.Exp, [pb], [e], scale=1.0 / 16.0)
                    es.append(e)
                pden = PR.next()
                for mt in range(2):
                    MM(pden.ap, ones_b.ap, es[mt].ap, mt == 0, mt == 1, [ones_b, es[mt]], [pden])
                S.emit("dve", lambda hh, o=rden.ap, i=pden.ap: hh.reciprocal(o, i), bs([pden]), bs([rden]))
                for dt_ in range(2):
                    pn = PR.next()
                    for mt in range(2):
                        MM(pn.ap, vm[l].ap[:, mt, (2 * h + dt_) * 128:(2 * h + dt_ + 1) * 128], es[mt].ap,
                           mt == 0, mt == 1, [vm[l], es[mt]], [pn])
                    TT("dve", moT.ap[:, 2 * h + dt_, :], pn.ap, rden.ap, ALU.mult, [pn, rden], [moT])

        def phase_KV(si, m0):
            new_phase()
            WR[0] = Ring([AR.alloc([8, 512], BF16, "w%d" % i) for i in range(3)])
            mt_ = AR.alloc([4, 1024], F32, "memt")
            ss = AR.alloc([12], F32, "ss")
            sq = AR.alloc([1024], F32, "sq")
            hb = AR.alloc([4, 1024], BF16, "hb")
            hT = AR.alloc([8, 512], BF16, "hT")
            gm = [load_gain("x_mem_norm", l) for l in range(2)]
            MS("pool", mt_.ap[:, 2:4, :], 0.0, [mt_])
            DMA(mt_.ap[:, 0:2, :], mem_in[m0:m0 + 256, :].rearrange("(j p) d -> p j d", p=128), [], [mt_])
            for l in range(2):
                tm_norm(mt_, gm[l], hb, ss, sq)
                transpose_block(hb, hT)
                for c in range(4):
                    w = wload("kv", l * 1024, 8, c * 512, 512)
                    if c < 2:
                        for i in range(4):
                            pb = PR.next()
                            for kt in range(8):
                                MM(pb.ap[:, 0:256], w.ap[:, kt, i * 128:(i + 1) * 128], hT.ap[:, kt, 0:256],
                                   kt == 0, kt == 7, [w, hT], [pb])
                            evac(kmT[l].ap[:, c * 4 + i, :], pb.ap[:, 0:256], [pb], [kmT[l]])
                    else:
                        for mt in range(2):
                            pb = PR.next()
                            for kt in range(8):
                                MM(pb.ap, hT.ap[:, kt, mt * 128:(mt + 1) * 128], w.ap[:, kt, :],
                                   kt == 0, kt == 7, [w, hT], [pb])
                            evac(vm[l].ap[:, mt, (c - 2) * 512:(c - 1) * 512], pb.ap, [pb], [vm[l]])

        def alloc_front(host=None, with_dt=True):
            d = {}
            d["xt"] = AR.alloc([4, 1024], F32, "xt")
            d["ss"] = AR.alloc([12], F32, "ss")
            d["sq"] = AR.alloc([1024], F32, "sq")
            d["hb"] = AR.alloc([4, 1024], BF16, "hb")
            d["hT"] = AR.alloc([8, 512], BF16, "hT")
            d["E"] = Ring([AR.alloc([512], BF16, "E%d" % i) for i in range(4)])
            d["rden"] = AR.alloc([512], F32, "rden")
            if host is None:
                d["qT"] = AR.alloc([8, 512], BF16, "qT")
                d["moT"] = AR.alloc([8, 512], BF16, "moT")
                d["fst"] = Ring([AR.alloc([4, 512], BF16, "fst%d" % i) for i in range(2)])
            else:
                sub = Arena(arena_t, AW)
                sub.reset(host[0])
                d["qT"] = sub.alloc([8, 512], BF16, "qT")
                d["moT"] = sub.alloc([8, 512], BF16, "moT")
                f0 = sub.alloc([4, 512], BF16, "fst0")
                f1 = sub.alloc([4, 512], BF16, "fst1")
                assert sub.p <= host[0] + host[1]
                for t in (d["qT"], d["moT"], f0, f1):
                    t.b = host[2]
                d["fst"] = Ring([f0, f1])
            if with_dt:
                d["dtt"] = AR.alloc([4, 4, 64], F32, "dtt")
            return d

        def front(l, t0, S_, tb, d):
            xt, hb, hT, qT, moT = d["xt"], d["hb"], d["hT"], d["qT"], d["moT"]
            c0t = tb * TB
            tm_norm(xt, d["g_pre_mix%d" % l], hb, d["ss"], d["sq"])
            transpose_block(hb, hT)
            if l == 0:
                chunks = [("z", 512 * c, 512, c) for c in range(4)] + [("xbc", 2048 + 512 * c, 512, c) for c in range(8)] \
                    + [("dt", 6144, 64, 0)] + [("q", 6208 + 512 * c, 512, c) for c in range(2)]
                key = "in0"
            else:
                chunks = [("qd", 512 * c, 512, c) for c in range(2)] + [("kd", 1024 + 512 * c, 512, c) for c in range(2)] \
                    + [("vd", 2048 + 512 * c, 512, c) for c in range(2)] + [("q", 3072 + 512 * c, 512, c) for c in range(2)]
                key = "in1"
            for kind, col0, ncols, c in chunks:
                w = wload(key, 0, 8, col0, ncols)
                if kind in ("xbc", "q", "qd", "kd"):
                    stg = None if kind == "q" else d["fst"].next()
                    for i in range(4):
                        pb = PR.next()
                        for kt in range(8):
                            MM(pb.ap, w.ap[:, kt, i * 128:(i + 1) * 128], hT.ap[:, kt, :], kt == 0, kt == 7, [w, hT], [pb])
                        if kind == "q":
                            evac(qT.ap[:, c * 4 + i, :], pb.ap, [pb], [qT])
                        else:
                            evac(stg.ap[:, i, :], pb.ap, [pb], [stg])
                    if kind != "q":
                        dst = {"xbc": xbc_s, "qd": q_s, "kd": k_s}[kind]
                        DMA(dst[c * 512:(c + 1) * 512, c0t:c0t + TB].rearrange("(i p) t -> p i t", p=128), stg.ap,
                            [stg], [db(kind, tb)], eng=STQ)
                elif kind in ("z", "vd"):
                    stg = d["fst"].next()
                    for j in range(4):
                        pb = PR.next()
                        for kt in range(8):
                            MM(pb.ap, hT.ap[:, kt, j * 128:(j + 1) * 128], w.ap[:, kt, :], kt == 0, kt == 7, [w, hT], [pb])
                        if kind == "z":
                            ACT(stg.ap[:, j, :], pb.ap, AF.Silu, [pb], [stg])
                        else:
                            evac(stg.ap[:, j, :], pb.ap, [pb], [stg])
                    dst = z_s if kind == "z" else v_s
                    DMA(dst[c0t:c0t + TB, c * 512:(c + 1) * 512].rearrange("(j p) c -> p j c", p=128), stg.ap,
                        [stg], [db(kind, tb)], eng=STQ)
                else:
                    dtt = d["dtt"]
                    pb = PR.next()
                    for j in range(4):
                        for kt in range(8):
                            MM(pb.ap[:, j * 64:(j + 1) * 64], hT.ap[:, kt, j * 128:(j + 1) * 128], w.ap[:, kt, 0:64],
                               kt == 0, kt == 7, [w, hT], [pb])
                    pv = pb.ap[:, 0:256].rearrange("p (j c) -> p j c", j=4)
                    dtb_b = dtb.ap.unsqueeze(1).to_broadcast([128, 4, 64])
                    TT("dve", dtt.ap[:, 0], pv, dtb_b, ALU.add, [pb, dtb], [dtt])
                    STT("dve", dtt.ap[:, 1], dtt.ap[:, 0], -1.0, dtt.ap[:, 0], ALU.mult, ALU.max, [dtt], [dtt])
                    ACT(dtt.ap[:, 2], dtt.ap[:, 1], AF.Exp, [dtt], [dtt], scale=-1.0)
                    ACT(dtt.ap[:, 3], dtt.ap[:, 2], AF.Ln, [dtt], [dtt], bias=1.0)
                    STT("dve", dtt.ap[:, 1], dtt.ap[:, 0], 0.0, dtt.ap[:, 3], ALU.max, ALU.add, [dtt], [dtt])
                    DMA(dt_s[c0t:c0t + TB, :].rearrange("(j p) c -> p j c", p=128), dtt.ap[:, 1], [dtt], [db("dt", tb)], eng=STQ)
            cross_attn(l, qT, moT, d["E"], d["rden"])
            DMA(mo_s[l][:, c0t:c0t + TB].rearrange("(i p) t -> p i t", p=128), moT.ap, [moT], [db("mo%d" % l, tb)], eng=STQ)

        def phase_A0(si, t0, S_):
            new_phase()
            WR[0] = Ring([AR.alloc([8, 512], BF16, "w%d" % i) for i in range(4)])
            d = alloc_front()
            d["g_pre_mix0"] = load_gain("norm_pre_mix", 0)
            for tb in range(S_ // TB):
                drip(8)
                DMA(d["xt"].ap, x_in[t0 + tb * TB:t0 + (tb + 1) * TB, :].rearrange("(j p) d -> p j d", p=128), [], [d["xt"]])
                front(0, t0, S_, tb, d)
            drip(len(pending_casts))

        def phase_B0(si, S_):
            new_phase()
            xin = Ring([AR.alloc([32, 516], BF16, "xin%d" % i) for i in range(1)])
            dgall = AR.alloc([160, 128], BF16, "dgall")
            for ct in range(32):
                for k in range(5):
                    TS("dve", dgall.ap[:, ct * 5 + k, :], ident_f.ap, cw.ap[:, ct, k:k + 1], None,
                       ALU.mult, None, [ident_f, cw], [dgall])
            pc = Ring([AR.alloc([32, 512], BF16, "pc%d" % i) for i in range(1)])
            xtm = Ring([AR.alloc([4, 2048], BF16, "xtm%d" % i) for i in range(2)])
            btm = Ring([AR.alloc([4, 1024], BF16, "btm%d" % i) for i in range(2)])
            nb = S_ // TB
            for tb in range(nb):
                xi = xin.next()
                lo = max(0, tb * TB - 2)
                hi = min(S_, tb * TB + TB + 2)
                o0 = lo - (tb * TB - 2)
                if tb == 0:
                    MS("pool", xi.ap[:, :, 0:2], 0.0, [xi])
                if tb == nb - 1:
                    MS("pool", xi.ap[:, :, 514:516], 0.0, [xi])
                rds = [db("xbc", b) for b in (tb - 1, tb, tb + 1) if 0 <= b < nb]
                for q4 in range(4):
                    DMA(xi.ap[:, q4 * 8:(q4 + 1) * 8, o0:o0 + (hi - lo)],
                        xbc_s[q4 * 1024:(q4 + 1) * 1024, lo:hi].rearrange("(ct p) t -> p ct t", p=128), rds, [xi])
                po = pc.next()
                for ct in range(32):
                    pb = PR.next()
                    for k in range(5):
                        MM(pb.ap, dgall.ap[:, ct * 5 + k, :], xi.ap[:, ct, k:k + 512], k == 0, k == 4, [dgall, xi], [pb])
                    ACT(po.ap[:, ct, :], pb.ap, AF.Silu, [pb, cb], [po], bias=cb.ap[:, ct:ct + 1])
                c0t = tb * TB
                DMA(bt_s[:, c0t:c0t + TB].rearrange("(i p) t -> p i t", p=128), po.ap[:, 16:24, :], [po], [db("bt", tb)])
                DMA(ct_s[:, c0t:c0t + TB].rearrange("(i p) t -> p i t", p=128), po.ap[:, 24:32, :], [po], [db("ct", tb)])
                xt_ = xtm.next()
                bt_ = btm.next()
                for j in range(4):
                    for ct in range(24):
                        if ct % 4 == 0:
                            tr = TRR.next()
                        S.tr(tr.ap[:, (ct % 4) * 128:(ct % 4 + 1) * 128], po.ap[:, ct, j * 128:(j + 1) * 128], ident.ap,
                             reads=bs([po, ident]), writes=bs([tr]))
                        if ct % 4 == 3:
                            c4 = ct // 4
                            if c4 < 4:
                                CP("dve", xt_.ap[:, j, c4 * 512:(c4 + 1) * 512], tr.ap, [tr], [xt_])
                            else:
                                CP("dve", bt_.ap[:, j, (c4 - 4) * 512:(c4 - 3) * 512], tr.ap, [tr], [bt_])
                DMA(xtm_s[c0t:c0t + TB, :].rearrange("(j p) c -> p j c", p=128), xt_.ap, [xt_], [db("xtm", tb)])
                DMA(btm_s[c0t:c0t + TB, :].rearrange("(j p) c -> p j c", p=128), bt_.ap, [bt_], [db("btm", tb)])

        def phase_C0(si, S_):
            new_phase()
            nb = S_ // TB
            xtm = Ring([AR.alloc([2048], BF16, "sxtm%d" % i) for i in range(2)])
            btm = Ring([AR.alloc([1024], BF16, "sbtm%d" % i) for i in range(2)])
            btf = Ring([AR.alloc([8, 512], BF16, "sbt%d" % i) for i in range(2)])
            ctf = Ring([AR.alloc([8, 512], BF16, "sct%d" % i) for i in range(2)])
            dtr = Ring([AR.alloc([4, 64], F32, "sdt%d" % i) for i in range(2)])
            zr = Ring([AR.alloc([2048], BF16, "sz%d" % i) for i in range(2)])
            yfl = Ring([AR.alloc([2048], F32, "syf%d" % i) for i in range(2)])
            xdsr = Ring([AR.alloc([2048], BF16, "xds%d" % i) for i in range(2)])
            ldtr = Ring([AR.alloc([64], F32, "ldt%d" % i) for i in range(2)])
            state = AR.alloc([2048], F32, "state")
            stateb = AR.alloc([2048], BF16, "stateb")
            dar = Ring([AR.alloc([32], F32, "da%d" % i) for i in range(2)])
            xdte = Ring([AR.alloc([2048], BF16, "xdte%d" % i) for i in range(2)])
            ex3 = Ring([AR.alloc([96], F32, "ex3_%d" % i) for i in range(2)])
            ncum = Ring([AR.alloc([32], F32, "ncum%d" % i) for i in range(2)])
            gt = Ring([AR.alloc([128], BF16, "gt%d" % i) for i in range(3)])
            exs = Ring([AR.alloc([4, 128], BF16, "exs%d" % i) for i in range(3)])
            wt = Ring([AR.alloc([4, 128], BF16, "wt%d" % i) for i in range(3)])
            yc = Ring([AR.alloc([2048], F32, "yc%d" % i) for i in range(2)])
            ytmp = Ring([AR.alloc([256], F32, "ytmp%d" % i) for i in range(3)])
            ssgr = Ring([AR.alloc([24], F32, "ssg%d" % i) for i in range(2)])
            sq = AR.alloc([2048], BF16, "sq2")
            ybr = Ring([AR.alloc([2048], BF16, "yb%d" % i) for i in range(2)])
            mixT = Ring([AR.alloc([16, 128], BF16, "mixT%d" % i) for i in range(2)])

            class Ctx:
                pass

            blk = {}

            def prologue(c):
                dirn, tb, j = c.dirn, c.tb, c.j
                c0t = tb * TB
                key = (dirn, tb)
                if key not in blk:
                    Bt = btf.next()
                    Ct = ctf.next()
                    Dt = dtr.next()
                    DMA(Bt.ap, bt_s[:, c0t:c0t + TB].rearrange("(i p) t -> p i t", p=128), [db("bt", tb)], [Bt])
                    DMA(Ct.ap, ct_s[:, c0t:c0t + TB].rearrange("(i p) t -> p i t", p=128), [db("ct", tb)], [Ct])
                    DMA(Dt.ap, dt_s[c0t:c0t + TB, :].rearrange("(j p) c -> p j c", p=128), [db("dt", tb)], [Dt])
                    blk.clear()
                    blk[key] = (Bt, Ct, Dt)
                c.Bt, c.Ct, c.Dt = blk[key]
                c.r0 = c0t + j * 128
                c.ck = tb * 4 + j
                c.X = xtm.next()
                c.Bm = btm.next()
                DMA(c.X.ap, xtm_s[c.r0:c.r0 + 128, :], [db("xtm", tb)], [c.X])
                DMA(c.Bm.ap, btm_s[c.r0:c.r0 + 128, :], [db("btm", tb)], [c.Bm])
                if dirn == 1:
                    c.Z = zr.next()
                    DMA(c.Z.ap, z_s[c.r0:c.r0 + 128, :], [db("z", tb)], [c.Z])
                c.tri_c = tri_f if dirn == 0 else triu_f
                c.tri_e = tris_f if dirn == 0 else trisl_f
                c.mask = maskf_b if dirn == 0 else maskb_b
                dtj = c.Dt.ap[:, j, dirn * 32:(dirn + 1) * 32]
                c.da = dar.next()
                TT("dve", c.da.ap, dtj, avec.ap[:, dirn * 32:(dirn + 1) * 32], ALU.mult, [c.Dt, avec], [c.da])
                pm = PR.next()
                MM(pm.ap[:, 0:32], c.tri_c.ap, c.da.ap, True, True, [c.tri_c, c.da], [pm])
                MM(pm.ap[:, 32:64], c.tri_e.ap, c.da.ap, True, True, [c.tri_e, c.da], [pm])
                MM(pm.ap[:, 64:96], ones_f.ap, c.da.ap, True, True, [ones_f, c.da], [pm])
                c.e3 = ex3.next()
                ACT(c.e3.ap, pm.ap[:, 0:96], AF.Exp, [pm], [c.e3])
                ld = ldtr.next()
                ACT(ld.ap[:, 0:32], dtj, AF.Ln, [c.Dt], [ld])
                c.nc_ = ncum.next()
                STT("dve", c.nc_.ap, pm.ap[:, 0:32], -1.0, ld.ap[:, 0:32], ALU.mult, ALU.add, [pm, ld], [c.nc_])
                TT("dve", ld.ap[:, 32:64], dtj, c.e3.ap[:, 32:64], ALU.mult, [c.Dt, c.e3], [ld])
                c.xe = xdte.next()
                TT("dve", c.xe.ap.rearrange("p (h c) -> p h c", h=32), c.X.ap.rearrange("p (h c) -> p h c", h=32),
                   ld.ap[:, 32:64].unsqueeze(2).to_broadcast([128, 32, 64]), ALU.mult, [c.X, ld], [c.xe])
                if dirn == 0:
                    c.xds = xdsr.next()
                    TT("pool", c.xds.ap.rearrange("p (h c) -> p h c", h=32), c.X.ap.rearrange("p (h c) -> p h c", h=32),
                       dsk.ap.unsqueeze(2).to_broadcast([128, 32, 64]), ALU.mult, [c.X, dsk], [c.xds])
                else:
                    c.yf = yfl.next()
                    DMA(c.yf.ap, yf_s[c.r0:c.r0 + 128, :], [db("yf", c.ck)], [c.yf])
                c.Y = yc.next()
                c.pupd = None

            def front_g(c, g):
                j = c.j
                pg = PR.next()
                MM(pg.ap[:, 0:128], c.Bt.ap[:, g, j * 128:(j + 1) * 128], c.Ct.ap[:, g, j * 128:(j + 1) * 128],
                   True, True, [c.Bt, c.Ct], [pg])
                G = gt.next()
                ACT(G.ap, pg.ap[:, 0:128], AF.Copy, [pg], [G])
                psg = PR.next()
                for hh in range(4):
                    h_ = g * 4 + hh
                    S.mm(psg.ap[:, hh * 128:(hh + 1) * 128], c.da.ap[:, h_:h_ + 1].to_broadcast([128, 128]), c.tri_c.ap,
                         start=(hh == 0), stop=False, reads=bs([c.da, c.tri_c]), writes=bs([psg]), skip_group_check=True)
                S.mm(psg.ap, ident.ap, c.mask.ap.rearrange("p h t -> p (h t)"), start=False, stop=True,
                     reads=bs([ident, c.mask]), writes=bs([psg]), skip_group_check=True)
                ex = exs.next()
                for hh in range(4):
                    ACT(ex.ap[:, hh, :], psg.ap[:, hh * 128:(hh + 1) * 128], AF.Exp, [psg, c.nc_], [ex],
                        bias=c.nc_.ap[:, g * 4 + hh:g * 4 + hh + 1])
                W = wt.next()
                TT("dve", W.ap, ex.ap, G.ap.unsqueeze(1).to_broadcast([128, 4, 128]), ALU.mult, [ex, G], [W])
                return W

            def back_g(c, g, W):
                j = c.j
                py = PR.next()
                for hh in range(4):
                    h_ = g * 4 + hh
                    if c.dirn == 0:
                        S.mm(py.ap[:, hh * 64:(hh + 1) * 64], W.ap[:, hh, :], c.X.ap[:, h_ * 64:(h_ + 1) * 64],
                             start=True, stop=False, reads=bs([W, c.X]), writes=bs([py]), skip_group_check=True)
                        S.mm(py.ap[:, hh * 64:(hh + 1) * 64], ident.ap, c.xds.ap[:, h_ * 64:(h_ + 1) * 64],
                             start=False, stop=True, reads=bs([ident, c.xds]), writes=bs([py]), skip_group_check=True)
                    else:
                        MM(py.ap[:, hh * 64:(hh + 1) * 64], W.ap[:, hh, :], c.X.ap[:, h_ * 64:(h_ + 1) * 64],
                           True, True, [W, c.X], [py])
                MM(py.ap[:, 256:512], c.Ct.ap[:, g, j * 128:(j + 1) * 128], stateb.ap[:, g * 256:(g + 1) * 256],
                   True, True, [c.Ct, stateb], [py])
                yt = ytmp.next()
                TT("dve", yt.ap.rearrange("p (h c) -> p h c", h=4), py.ap[:, 256:512].rearrange("p (h c) -> p h c", h=4),
                   c.e3.ap[:, g * 4:(g + 1) * 4].unsqueeze(2).to_broadcast([128, 4, 64]), ALU.mult, [py, c.e3], [yt])
                TT("dve", c.Y.ap[:, g * 256:(g + 1) * 256], yt.ap, py.ap[:, 0:256], ALU.add, [yt, py], [c.Y])
                if g % 2 == 0:
                    c.pupd = PR.next()
                MM(c.pupd.ap[:, (g % 2) * 256:(g % 2 + 1) * 256], c.Bm.ap[:, g * 128:(g + 1) * 128],
                   c.xe.ap[:, g * 256:(g + 1) * 256], True, True, [c.Bm, c.xe], [c.pupd])
                if g % 2 == 1:
                    g0 = g - 1
                    sl = slice(g0 * 256, (g0 + 2) * 256)
                    TT("pool", state.ap[:, sl].rearrange("p (h c) -> p h c", h=8),
                       state.ap[:, sl].rearrange("p (h c) -> p h c", h=8),
                       c.e3.ap[:, 64 + g0 * 4:64 + g0 * 4 + 8].unsqueeze(2).to_broadcast([128, 8, 64]), ALU.mult,
                       [state, c.e3], [state])
                    TT("dve", state.ap[:, sl], state.ap[:, sl], c.pupd.ap, ALU.add, [state, c.pupd], [state])
                    ACT(stateb.ap[:, sl], state.ap[:, sl], AF.Copy, [state], [stateb])

            def ep_stages(c):
                Y, r0, ck = c.Y, c.r0, c.ck
                if c.dirn == 0:
                    return [lambda: DMA(yf_s[r0:r0 + 128, :], Y.ap, [Y], [db("yf", ck)])]
                Z = c.Z
                st_ = {}

                def s1():
                    TT("pool", Y.ap, Y.ap, c.yf.ap, ALU.add, [Y, c.yf], [Y])

                def s2():
                    TT("dve", Y.ap, Y.ap, Z.ap, ALU.mult, [Y, Z], [Y])

                def s3():
                    ssg = ssgr.next()
                    st_["ssg"] = ssg
                    MS("pool", ssg.ap, 0.0, [ssg])
                    for g in range(8):
                        ACT(sq.ap[:, g * 256:(g + 1) * 256], Y.ap[:, g * 256:(g + 1) * 256], AF.Square, [Y, ssg], [sq, ssg],
                            accum_out=ssg.ap[:, g:g + 1])
                    ACT(ssg.ap[:, 8:16], ssg.ap[:, 0:8], AF.Ln, [ssg], [ssg], bias=EPS, scale=1.0 / 256.0)
                    ACT(ssg.ap[:, 16:24], ssg.ap[:, 8:16], AF.Exp, [ssg], [ssg], scale=-0.5)

                def s4():
                    ssg = st_["ssg"]
                    c.yb = ybr.next()
                    for g in range(8):
                        STT("dve", c.yb.ap[:, g * 256:(g + 1) * 256], Y.ap[:, g * 256:(g + 1) * 256], ssg.ap[:, 16 + g:17 + g],
                            ssdn.ap[:, g * 256:(g + 1) * 256], ALU.mult, ALU.mult, [Y, ssg, ssdn], [c.yb])

                def s5():
                    yb = c.yb
                    mt_ = mixT.next()
                    for c4 in range(4):
                        tr = TRR.next()
                        for i in range(4):
                            ct = c4 * 4 + i
                            S.tr(tr.ap[:, i * 128:(i + 1) * 128], yb.ap[:, ct * 128:(ct + 1) * 128], ident.ap,
                                 reads=bs([yb, ident]), writes=bs([tr]))
                        CP("dve", mt_.ap[:, c4 * 4:(c4 + 1) * 4, :], tr.ap.rearrange("p (i t) -> p i t", i=4), [tr], [mt_])
                    DMA(mix_s[:, r0:r0 + 128].rearrange("(i p) t -> p i t", p=128), mt_.ap, [mt_], [db("mix", ck)])

                return [s1, None, s2, None, s3, None, s4, None, s5]

            for dirn in range(2):
                MS("pool", state.ap, 0.0, [state])
                MS("pool", stateb.ap, 0.0, [stateb])
                blocks = list(range(nb)) if dirn == 0 else list(range(nb - 1, -1, -1))
                chunks = []
                for tb in blocks:
                    for j in (range(4) if dirn == 0 else range(3, -1, -1)):
                        c = Ctx()
                        c.dirn, c.tb, c.j = dirn, tb, j
                        chunks.append(c)
                items = [(c, g) for c in chunks for g in range(8)]
                deferred = []
                prologue(items[0][0])
                Wn = front_g(*items[0])
                for idx, (c, g) in enumerate(items):
                    Wc = Wn
                    if idx + 1 < len(items):
                        c2, g2 = items[idx + 1]
                        if g2 == 0:
                            prologue(c2)
                        Wn = front_g(c2, g2)
                    back_g(c, g, Wc)
                    for q_ in deferred:
                        if q_:
                            f_ = q_.pop(0)
                            if f_ is not None:
                                f_()
                    deferred = [q_ for q_ in deferred if q_]
                    if g == 7:
                        deferred.append(ep_stages(c))
                for q_ in deferred:
                    for f_ in q_:
                        if f_ is not None:
                            f_()

        def phase_D(l, si, t0, S_):
            new_phase()
            WR[0] = Ring([AR.alloc([8, 512], BF16, "w%d" % i) for i in range(5)])
            nk = 24 if l == 0 else 16
            hp0 = AR.p
            hid = AR.alloc([32, 512], BF16, "hid")
            sub = Arena(arena_t, AW)
            sub.reset(hp0)
            actT = sub.alloc([nk, 512], BF16, "actT")
            actT.b = hid.b
            d = alloc_front(host=(hp0, 8192, hid.b), with_dt=False)
            ot = AR.alloc([4, 1024], F32, "ot")
            rl = Ring([AR.alloc([512], F32, "rl%d" % i) for i in range(2)])
            xt, hb, hT = d["xt"], d["hb"], d["hT"]
            atm = hb
            g_post_mix = load_gain("norm_post_mix", l)
            g_pre_mlp = load_gain("norm_pre_mlp", l)
            g_post_mlp = load_gain("norm_post_mlp", l)
            if l == 0:
                d["g_pre_mix1"] = load_gain("norm_pre_mix", 1)
            nmix = nk - 8
            for tb in range(S_ // TB):
                c0t = tb * TB
                if l == 0:
                    DMA(xt.ap, x_in[t0 + c0t:t0 + c0t + TB, :].rearrange("(j p) d -> p j d", p=128), [], [xt])
                    for c4 in range(4):
                        DMA(actT.ap[:, 0:16, c4 * 128:(c4 + 1) * 128],
                            mix_s[:, c0t + c4 * 128:c0t + (c4 + 1) * 128].rearrange("(i p) t -> p i t", p=128),
                            [db("mix", tb * 4 + c4)], [actT])
                else:
                    DMA(xt.ap, x1_s[c0t:c0t + TB, :].rearrange("(j p) d -> p j d", p=128), [db("x1", tb)], [xt])
                    DMA(atm.ap, atm_s[c0t:c0t + TB, :].rearrange("(j p) c -> p j c", p=128),
                        [db("atm", (tb, hh)) for hh in range(8)], [atm])
                    transpose_block(atm, actT)
                DMA(actT.ap[:, nmix:nk, :], mo_s[l][:, c0t:c0t + TB].rearrange("(i p) t -> p i t", p=128),
                    [db("mo%d" % l, tb)], [actT])
                key = "out0" if l == 0 else "out1"
                for ch in range(2):
                    accs = [PR.next() for _ in range(4)]
                    for kc in range(nk // 8):
                        w = wload(key, kc * 1024, 8, ch * 512, 512)
                        for j in range(4):
                            for kt in range(8):
                                MM(accs[j].ap, actT.ap[:, kc * 8 + kt, j * 128:(j + 1) * 128], w.ap[:, kt, :],
                                   kc == 0 and kt == 0, kc == nk // 8 - 1 and kt == 7, [actT, w], [accs[j]])
                    for j in range(4):
                        evac(ot.ap[:, j, ch * 512:(ch + 1) * 512], accs[j].ap, [accs[j]], [ot])
                tm_norm(ot, g_post_mix, ot, d["ss"], d["sq"])
                TT("dve", xt.ap, xt.ap, ot.ap, ALU.add, [xt, ot], [xt])
                tm_norm(xt, g_pre_mlp, hb, d["ss"], d["sq"])
                transpose_block(hb, hT)
                for fc in range(8):
                    w = wload("w1", l * 1024, 8, fc * 512, 512)
                    for i in range(4):
                        pb = PR.next()
                        for kt in range(8):
                            MM(pb.ap, w.ap[:, kt, i * 128:(i + 1) * 128], hT.ap[:, kt, :], kt == 0, kt == 7, [w, hT], [pb])
                        r = rl.next()
                        ACT(r.ap, pb.ap, AF.Relu, [pb], [r])
                        TT("pool", hid.ap[:, fc * 4 + i, :], r.ap, r.ap, ALU.mult, [r], [hid])
                for ch in range(2):
                    accs = [PR.next() for _ in range(4)]
                    for fc in range(4):
                        w = wload("w2", l * 4096 + fc * 1024, 8, ch * 512, 512)
                        for j in range(4):
                            for ft in range(8):
                                MM(accs[j].ap, hid.ap[:, fc * 8 + ft, j * 128:(j + 1) * 128], w.ap[:, ft, :],
                                   fc == 0 and ft == 0, fc == 3 and ft == 7, [hid, w], [accs[j]])
                    for j in range(4):
                        evac(ot.ap[:, j, ch * 512:(ch + 1) * 512], accs[j].ap, [accs[j]], [ot])
                tm_norm(ot, g_post_mlp, ot, d["ss"], d["sq"])
                TT("dve", xt.ap, xt.ap, ot.ap, ALU.add, [xt, ot], [xt])
                if l == 0:
                    DMA(x1_s[c0t:c0t + TB, :].rearrange("(j p) d -> p j d", p=128), xt.ap, [xt], [db("x1", tb)], eng=STQ)
                    if lvl >= ORDER.index("B1"):
                        front(1, t0, S_, tb, d)
                else:
                    DMA(y_out[t0 + c0t:t0 + c0t + TB, :].rearrange("(j p) d -> p j d", p=128), xt.ap, [xt], [db("y", (si, tb))], eng=STQ)

        def phase_B1(si, S_):
            new_phase()
            nkt = S_ // 128
            nqb = S_ // TB
            qh = Ring([AR.alloc([S_], BF16, "qh%d" % i) for i in range(2)])
            kh = Ring([AR.alloc([S_], BF16, "kh%d" % i) for i in range(2)])
            vh = Ring([AR.alloc([nkt, 130], BF16, "vh%d" % i) for i in range(2)])
            bt6 = Ring([AR.alloc([6, 512], F32, "bt6_%d" % i) for i in range(2)])
            Er = Ring([AR.alloc([2, 512], BF16, "ae%d" % i) for i in range(3)])
            tmpr = Ring([AR.alloc([2, 512], F32, "atmp%d" % i) for i in range(2)])
            rr = Ring([AR.alloc([4], F32, "arr%d" % i) for i in range(8)])
            t1 = Ring([AR.alloc([128], F32, "at1_%d" % i) for i in range(4)])
            o_ = Ring([AR.alloc([128], F32, "ao%d" % i) for i in range(8)])
            sqa = AR.alloc([128], F32, "asq")
            ssa = Ring([AR.alloc([4], F32, "assa%d" % i) for i in range(8)])
            ob = Ring([AR.alloc([4, 128], BF16, "aob%d" % i) for i in range(2)])
            accs_r = Ring([AR.alloc([4, 260], F32, "accs%d" % i) for i in range(2)])
            for v in vh.items:
                MS("pool", v.ap[:, :, 128:130], 1.0, [v])
            accb = pbanks[0:4]
            prs = []
            for i in (4, 6):
                t_ = T(pall[:, i * 512:(i + 2) * 512].rearrange("p (c q) -> p c q", c=2), "pair%d" % i)
                t_.b.excl = True
                prs.append(t_)
            scr = Ring(prs)
            heads = {}

            def load_head(h):
                Q = qh.next()
                K = kh.next()
                V = vh.next()
                Bt = bt6.next()
                DMA(Q.ap, q_s[h * 128:(h + 1) * 128, 0:S_], [db("qd", b_) for b_ in range(nqb)], [Q])
                DMA(K.ap, k_s[h * 128:(h + 1) * 128, 0:S_], [db("kd", b_) for b_ in range(nqb)], [K])
                DMA(V.ap[:, :, 0:128], v_s[0:S_, h * 128:(h + 1) * 128].rearrange("(kt p) e -> p kt e", p=128),
                    [db("vd", b_) for b_ in range(nqb)], [V])
                for di in range(6):
                    delta = -128 + 128 * di
                    src = bass.AP(rep_t, h * 128 * RL + 640 - delta, [[RL - 1, 128], [1, 512]])
                    DMA(Bt.ap[:, di, :], src, [db("rep", h)], [Bt])
                heads[h] = (Q, K, V, Bt)

            def emit_scores(h, qb, kt):
                Q, K, V, Bt = heads[h]
                delta = kt * 128 - qb * TB
                pair = scr.next()
                for c in range(2):
                    MM(pair.ap[:, c, :], K.ap[c * 64:(c + 1) * 64, kt * 128:(kt + 1) * 128],
                       Q.ap[c * 64:(c + 1) * 64, qb * TB:(qb + 1) * TB], True, True, [K, Q], [pair])
                e = Er.next()
                if -128 <= delta <= 512:
                    tm = tmpr.next()
                    STT("dve", tm.ap, pair.ap, 0.125, Bt.ap[:, (delta + 128) // 128, :].unsqueeze(1).to_broadcast([128, 2, 512]),
                        ALU.mult, ALU.add, [pair, Bt], [tm])
                    ACT(e.ap, tm.ap, AF.Exp, [tm], [e])
                else:
                    cbias = tab15 if delta < 0 else tab31
                    ACT(e.ap, pair.ap, AF.Exp, [pair, cbias], [e], scale=0.125, bias=cbias.ap[:, h:h + 1])
                return e

            def emit_pv(h, qb, kt, es):
                Q, K, V, Bt = heads[h]
                for jq in range(4):
                    av = accb[jq].ap[:, 0:260].rearrange("p (c e) -> p c e", c=2)
                    for c in range(2):
                        S.mm(av[:, c, :], es.ap[:, c, jq * 128:(jq + 1) * 128], V.ap[:, kt, :],
                             start=(kt == 0 and c == 0), stop=(kt == nkt - 1 and c == 1),
                             reads=bs([es, V]), writes=bs([accb[jq]]), skip_group_check=True)

            def epilogue_stages(h, qb):
                OB = ob.next()
                AS = accs_r.next()
                avs = [AS.ap[:, jq, :].rearrange("p (c e) -> p c e", c=2) for jq in range(4)]
                rs_, oos, sas = [], [], []

                def s0():
                    for jq in range(4):
                        CP("dve", AS.ap[:, jq, :], accb[jq].ap[:, 0:260], [accb[jq]], [AS])

                def s1():
                    for jq in range(4):
                        r = rr.next()
                        S.emit("dve", lambda hh, o=r.ap[:, 0:2], i=avs[jq][:, :, 128]: hh.reciprocal(o, i), bs([AS]), bs([r]))
                        rs_.append(r)
                    for jq in range(4):
                        r = rs_[jq]
                        TT("dve", r.ap[:, 2:3], r.ap[:, 1:2], lamt.ap[:, 1:2], ALU.mult, [r, lamt], [r])

                def s2():
                    for jq in range(4):
                        r = rs_[jq]
                        t_ = t1.next()
                        TS("dve", t_.ap, avs[jq][:, 0, 0:128], r.ap[:, 0:1], None, ALU.mult, None, [AS, r], [t_])
                        oo = o_.next()
                        STT("dve", oo.ap, avs[jq][:, 1, 0:128], r.ap[:, 2:3], t_.ap, ALU.mult, ALU.add, [AS, r, t_], [oo])
                        oos.append(oo)
                        sa = ssa.next()
                        MS("pool", sa.ap, 0.0, [sa])
                        sas.append(sa)

                def s3():
                    for jq in range(4):
                        ACT(sqa.ap, oos[jq].ap, AF.Square, [oos[jq], sas[jq]], [sqa, sas[jq]], accum_out=sas[jq].ap[:, 0:1])
                    for jq in range(4):
                        ACT(sas[jq].ap[:, 1:2], sas[jq].ap[:, 0:1], AF.Ln, [sas[jq]], [sas[jq]], bias=EPS, scale=1.0 / 128.0)
                    for jq in range(4):
                        ACT(sas[jq].ap[:, 2:3], sas[jq].ap[:, 1:2], AF.Exp, [sas[jq]], [sas[jq]], scale=-0.5)

                def s4():
                    for jq in range(4):
                        STT("dve", OB.ap[:, jq, :], oos[jq].ap, sas[jq].ap[:, 2:3], subln.ap, ALU.mult, ALU.mult,
                            [oos[jq], sas[jq], subln], [OB])

                def s5():
                    DMA(atm_s[qb * TB:(qb + 1) * TB, h * 128:(h + 1) * 128].rearrange("(j p) e -> p j e", p=128), OB.ap,
                        [OB], [db("atm", (qb, h))])

                return [s0, s1, None, s2, None, s3, None, s4, None, s5]

            steps = [(h, qb, kt) for h in range(8) for qb in range(nqb) for kt in range(nkt)]
            deferred = []
            load_head(0)
            pend = emit_scores(*steps[0])
            for i, (h, qb, kt) in enumerate(steps):
                if qb == 0 and kt == 0 and h + 1 < 8:
                    load_head(h + 1)
                nxt = emit_scores(*steps[i + 1]) if i + 1 < len(steps) else None
                emit_pv(h, qb, kt, pend)
                for q_ in deferred:
                    if q_:
                        f_ = q_.pop(0)
                        if f_ is not None:
                            f_()
                deferred = [q_ for q_ in deferred if q_]
                if kt == nkt - 1:
                    st_list = epilogue_stages(h, qb)
                    st_list.pop(0)()
                    deferred.append(st_list)
                pend = nxt
            for q_ in deferred:
                for f_ in q_:
                    if f_ is not None:
                        f_()

        t0 = 0
        for si, S_ in enumerate(seqs):
            if lvl >= ORDER.index("KV"):
                phase_KV(si, si * 256)
            if lvl >= ORDER.index("A0"):
                phase_A0(si, t0, S_)
            if lvl >= ORDER.index("B0"):
                phase_B0(si, S_)
            if lvl >= ORDER.index("C0"):
                phase_C0(si, S_)
            if lvl >= ORDER.index("D0"):
                phase_D(0, si, t0, S_)
            if lvl >= ORDER.index("B1"):
                phase_B1(si, S_)
            if lvl >= ORDER.index("D1"):
                phase_D(1, si, t0, S_)
            t0 += S_
        S.finish()
        S.materialize()
        nops = {e: len(S.ops[e]) for e in S.ENGS}
    return nc, nops


def _rel_onehot():
    rel_np = np.arange(640, 640 - RL, -1, dtype=np.int32)
    try:
        import jax
        import jax.numpy as jnp
        with jax.default_device(jax.devices("cpu")[0]):
            rel = jnp.asarray(rel_np)
            nb, max_exact = 16, 8
            ret = jnp.where(rel > 0, nb, 0)
            n = jnp.abs(rel)
            nf = jnp.maximum(n, 1).astype(jnp.float32)
            large = max_exact + (jnp.log(nf / max_exact) / math.log(128 / max_exact) * (nb - max_exact)).astype(jnp.int32)
            large = jnp.minimum(large, nb - 1)
            bucket = np.asarray(ret + jnp.where(n < max_exact, n, large))
    except Exception:
        n = np.abs(rel_np)
        nf = np.maximum(n, 1).astype(np.float32)
        large = 8 + (np.log(nf / np.float32(8)) / np.float32(math.log(16)) * np.float32(8)).astype(np.int32)
        large = np.minimum(large, 15)
        bucket = np.where(rel_np > 0, 16, 0) + np.where(n < 8, n, large)
    oh = np.zeros((32, RL), np.float32)
    oh[bucket, np.arange(RL)] = 1.0
    return oh


_PROG = {}


def _param_maps(inputs):
    m = {}
    for n, shp in PARAMS:
        m[n] = np.ascontiguousarray(np.asarray(inputs[n], dtype=np.float32).reshape(shp))
    m["onehot"] = _rel_onehot()
    return m


def run_cores(seqs, xs, mems, inputs, dbg=False, upto="END"):
    key = (tuple(seqs), dbg, upto)
    if key not in _PROG:
        _PROG[key] = build(list(seqs), dbg=dbg, upto=upto)[0]
    nc = _PROG[key]
    pm = _param_maps(inputs)
    in_maps = []
    for x, mm_ in zip(xs, mems):
        d = dict(pm)
        d["x"] = np.ascontiguousarray(x, dtype=np.float32)
        d["mem"] = np.ascontiguousarray(mm_, dtype=np.float32)
        in_maps.append(d)
    res = run_bass_kernel_spmd(nc, in_maps, core_ids=list(range(len(xs))))
    return res.results


def kernel(**inputs):
    xp = np.asarray(inputs["x_prompt"], dtype=np.float32)
    xs_ = np.asarray(inputs["x_sample"], dtype=np.float32)
    mp = np.asarray(inputs["mem_prompt"], dtype=np.float32)
    ms = np.asarray(inputs["mem_sample"], dtype=np.float32)
    seqs = [4096, 2048, 2048, 2048, 2048]
    xs, mems = [], []
    for c in range(8):
        xs.append(np.concatenate([xp[c], xs_[4 * c:4 * c + 4].reshape(-1, D)], axis=0))
        mems.append(np.concatenate([mp[c], ms[4 * c:4 * c + 4].reshape(-1, D)], axis=0))
    res = run_cores(seqs, xs, mems, inputs)
    y_prompt = np.empty_like(xp)
    y_sample = np.empty_like(xs_)
    for c in range(8):
        y = np.asarray(res[c]["y"], dtype=np.float32)
        y_prompt[c] = y[0:4096]
        y_sample[4 * c:4 * c + 4] = y[4096:].reshape(4, 2048, D)
    return (y_prompt, y_sample)
```

```python
import numpy as np
import concourse.bass as bass
import concourse.mybir as mybir
from concourse.bass_utils import run_bass_kernel_spmd
from contextlib import ExitStack

F32 = mybir.dt.float32
BF16 = mybir.dt.bfloat16
AF = mybir.ActivationFunctionType
ALU = mybir.AluOpType
AX = mybir.AxisListType


class Buf:
    __slots__ = ("w", "rs", "name", "excl")

    def __init__(self, name="", excl=False):
        self.w = None
        self.rs = []
        self.name = name
        self.excl = excl


class Sched:
    ENGS = ("pe", "act", "dve", "pool", "sp")

    def __init__(self, nc, stack, n_dma_sems=40):
        self.nc = nc
        self.ops = {e: [] for e in self.ENGS}
        self.esem = {e: stack.enter_context(nc.semaphore("s_" + e)) for e in self.ENGS}
        self.dsem = [stack.enter_context(nc.semaphore("d%d" % i)) for i in range(n_dma_sems)]
        self.duse = [0] * n_dma_sems
        self.dnext = 0
        self.seen_c = {e: {} for e in self.ENGS}
        self.seen_d = {e: {} for e in self.ENGS}
        self.signal = {e: set() for e in self.ENGS}

    def _need(self, eng, tok, waits, kind):
        if tok is None:
            return
        if tok[0] == "c":
            _, te, idx = tok
            if te == eng and (kind != "raw" or eng == "pe"):
                return
            if self.seen_c[eng].get(te, -1) >= idx:
                return
            self.seen_c[eng][te] = idx
            self.signal[te].add(idx)
            waits[:] = [w for w in waits if not (w[0] == "c" and w[1] == te)]
            waits.append(tok)
        else:
            _, k, val = tok
            if self.seen_d[eng].get(k, 0) >= val:
                return
            self.seen_d[eng][k] = val
            waits[:] = [w for w in waits if not (w[0] == "d" and w[1] == k)]
            waits.append(tok)

    def emit(self, eng, fn, reads=(), writes=(), dma=False):
        waits = []
        if any(b.excl for b in reads):
            writes = list(writes) + [b for b in reads if b.excl]
            reads = [b for b in reads if not b.excl]
        for b in reads:
            for t in (b.w or ()):
                self._need(eng, t, waits, "raw")
        for b in writes:
            for t in (b.w or ()):
                self._need(eng, t, waits, "waw")
            for r in b.rs:
                self._need(eng, r, waits, "war")
        idx = len(self.ops[eng])
        if dma:
            k = self.dnext
            self.dnext = (self.dnext + 1) % len(self.dsem)
            if self.duse[k] > 0:
                self._need(eng, ("d", k, 16 * self.duse[k]), waits, "raw")
            self.duse[k] += 1
            tok = ("d", k, 16 * self.duse[k])
        else:
            tok = ("c", eng, idx)
        self.ops[eng].append((fn, waits, tok))
        for b in reads:
            b.rs.append(tok)
        for b in writes:
            if dma and b.w and not b.rs and all(t[0] == "d" for t in b.w):
                b.w = b.w + [tok]
            else:
                b.w = [tok]
            b.rs = []
        return tok

    def finish(self):
        waits = []
        for k, u in enumerate(self.duse):
            if u > 0 and self.seen_d["sp"].get(k, 0) < 16 * u:
                waits.append(("d", k, 16 * u))
        self.ops["sp"].append((None, waits, None))

    def barrier(self):
        waits = []
        for k, u in enumerate(self.duse):
            if u > 0:
                self._need("sp", ("d", k, 16 * u), waits, "raw")
        for e in ("pe", "act", "dve", "pool"):
            for i in range(len(self.ops[e]) - 1, -1, -1):
                fn, w, tok = self.ops[e][i]
                if fn is not None and tok is not None and tok[0] == "c":
                    self._need("sp", tok, waits, "raw")
                    break
        tok_sp = ("c", "sp", len(self.ops["sp"]))
        self.ops["sp"].append((lambda h: h.nop(), waits, tok_sp))
        for e in ("pe", "act", "dve", "pool"):
            w = []
            self._need(e, tok_sp, w, "raw")
            if w:
                self.ops[e].append((None, w, None))

    def materialize(self):
        nc = self.nc
        cnt = {}
        for e in self.ENGS:
            c = 0
            m = {}
            for i in sorted(self.signal[e]):
                c += 1
                m[i] = c
            cnt[e] = m
        with nc.Block() as block:
            def run(e, h):
                for i, (fn, waits, tok) in enumerate(self.ops[e]):
                    for w in waits:
                        if w[0] == "c":
                            h.wait_ge(self.esem[w[1]], cnt[w[1]][w[2]])
                        else:
                            h.wait_ge(self.dsem[w[1]], w[2])
                    if fn is None:
                        continue
                    ins = fn(h)
                    if tok[0] == "d":
                        ins.then_inc(self.dsem[tok[1]], 16)
                    elif i in cnt[e]:
                        ins.then_inc(self.esem[e], 1)

            @block.tensor
            def _(h):
                run("pe", h)

            @block.scalar
            def _(h):
                run("act", h)

            @block.vector
            def _(h):
                run("dve", h)

            @block.gpsimd
            def _(h):
                run("pool", h)

            @block.sync
            def _(h):
                run("sp", h)

    def mm(self, out, lhsT, rhs, start=True, stop=True, reads=(), writes=(), **kw):
        return self.emit("pe", lambda e: e.matmul(out, lhsT, rhs, start=start, stop=stop, **kw), reads, writes)

    def tr(self, out, in_, ident, reads=(), writes=()):
        return self.emit("pe", lambda e: e.transpose(out, in_, ident), reads, writes)

    def act(self, out, in_, func, reads=(), writes=(), **kw):
        return self.emit("act", lambda e: e.activation(out, in_, func, **kw), reads, writes)

    def dma(self, eng, out, in_, reads=(), writes=(), **kw):
        return self.emit(eng, lambda e: e.dma_start(out=out, in_=in_, **kw), reads, writes, dma=True)

    def tt(self, eng, out, in0, in1, op, reads=(), writes=()):
        return self.emit(eng, lambda e: e.tensor_tensor(out, in0, in1, op), reads, writes)

    def ts(self, eng, out, in0, s1, s2, op0, op1=None, reads=(), writes=(), **kw):
        if op1 is None:
            return self.emit(eng, lambda e: e.tensor_scalar(out, in0, s1, s2, op0, **kw), reads, writes)
        return self.emit(eng, lambda e: e.tensor_scalar(out, in0, s1, s2, op0, op1, **kw), reads, writes)

    def stt(self, eng, out, in0, scalar, in1, op0, op1, reads=(), writes=()):
        return self.emit(eng, lambda e: e.scalar_tensor_tensor(out, in0, scalar, in1, op0, op1), reads, writes)

    def copy(self, eng, out, in_, reads=(), writes=()):
        if eng == "act":
            return self.emit(eng, lambda e: e.activation(out, in_, AF.Copy), reads, writes)
        return self.emit(eng, lambda e: e.tensor_copy(out, in_), reads, writes)

    def memset(self, eng, ap, val, writes=()):
        return self.emit(eng, lambda e: e.memset(ap, val), (), writes)
import math


def _prod(s):
    r = 1
    for v in s:
        r *= v
    return r


class T:
    __slots__ = ("ap", "b")

    def __init__(self, ap, name=""):
        self.ap = ap
        self.b = Buf(name)

    def __getitem__(self, k):
        return self.ap[k]


class Arena:
    def __init__(self, tensor, width):
        self.t = tensor
        self.W = width
        self.p = 0

    def reset(self, p=0):
        self.p = p

    def alloc(self, free_shape, dtype, name=""):
        n = _prod(free_shape)
        words = n if dtype == F32 else (n + 1) // 2
        words = (words + 7) // 8 * 8
        assert self.p + words <= self.W, ("arena overflow", name, self.p, words, self.W)
        ap = self.t[:, self.p:self.p + words]
        self.p += words
        if dtype != F32:
            ap = ap.bitcast(dtype)
        ap = ap[:, 0:n]
        if len(free_shape) == 2:
            ap = ap.rearrange("p (a b) -> p a b", a=free_shape[0])
        elif len(free_shape) == 3:
            ap = ap.rearrange("p (a b c) -> p a b c", a=free_shape[0], b=free_shape[1])
        return T(ap, name)


class Ring:
    def __init__(self, items):
        self.items = items
        self.i = 0

    def next(self):
        it = self.items[self.i % len(self.items)]
        self.i += 1
        return it
D = 1024
KT = 8
TB = 512
EPS = 1e-6
LAM_INIT1 = 0.8 - 0.6 * math.exp(-0.3 * 1)
NEG = -30000.0
RL = 1280

PARAMS = [
    ("rel_bias_table", [32, 8]), ("norm_pre_mix", [2, 1024]), ("norm_post_mix", [2, 1024]),
    ("norm_pre_mlp", [2, 1024]), ("norm_post_mlp", [2, 1024]), ("ssd_w_in", [1024, 7232]),
    ("ssd_conv_w", [5, 4096]), ("ssd_conv_b", [1, 4096]), ("ssd_dt_bias", [1, 64]),
    ("ssd_a_log", [1, 64]), ("ssd_d", [1, 32]), ("ssd_norm", [1, 2048]), ("ssd_w_out", [3072, 1024]),
    ("diff_w_in", [1024, 4096]), ("diff_lambda", [1, 256]), ("diff_subln", [1, 128]),
    ("diff_w_out", [2048, 1024]), ("x_mem_norm", [2, 1024]), ("x_w_kv", [2048, 2048]),
    ("mlp_w1", [2048, 4096]), ("mlp_w2", [8192, 1024]),
]


def build(seqs, dbg=False, upto="END"):
    nc = bass.Bass("TRN2", target_bir_lowering=False)
    NS = len(seqs)
    NT = sum(seqs)
    SM = max(seqs)
    ORDER = ["W", "KV", "A0", "B0", "C0", "D0", "B1", "D1", "END"]
    lvl = ORDER.index(upto)

    def din(name, shape):
        return nc.dram_tensor(name, shape, F32, kind="ExternalInput").ap()

    x_in = din("x", [NT, D])
    mem_in = din("mem", [NS * 256, D])
    P = {n: din(n, s) for n, s in PARAMS}
    oh_in = din("onehot", [32, RL])
    y_out = nc.dram_tensor("y", [NT, D], F32, kind="ExternalOutput").ap()
    skind = "ExternalOutput" if dbg else "Internal"

    def dscr(name, shape, dt, k=None):
        return nc.dram_tensor(name, shape, dt, kind=(k or skind)).ap()

    WB = {
        "in0": dscr("wb_in0", [1024, 7232], BF16, "Internal"), "out0": dscr("wb_out0", [3072, 1024], BF16, "Internal"),
        "in1": dscr("wb_in1", [1024, 4096], BF16, "Internal"), "out1": dscr("wb_out1", [2048, 1024], BF16, "Internal"),
        "kv": dscr("wb_kv", [2048, 2048], BF16, "Internal"), "w1": dscr("wb_w1", [2048, 4096], BF16, "Internal"),
        "w2": dscr("wb_w2", [8192, 1024], BF16, "Internal"),
    }
    WSRC = {"in0": "ssd_w_in", "out0": "ssd_w_out", "in1": "diff_w_in", "out1": "diff_w_out",
            "kv": "x_w_kv", "w1": "mlp_w1", "w2": "mlp_w2"}
    z_s = dscr("z_s", [SM, 2048], BF16)
    xbc_s = dscr("xbc_s", [4096, SM], BF16)
    dt_s = dscr("dt_s", [SM, 64], F32)
    mo_s = [dscr("mo_s%d" % l, [1024, SM], BF16) for l in range(2)]
    ct_s = dscr("ct_s", [1024, SM], BF16)
    bt_s = dscr("bt_s", [1024, SM], BF16)
    btm_s = dscr("btm_s", [SM, 1024], BF16)
    xtm_s = dscr("xtm_s", [SM, 2048], BF16)
    yf_s = dscr("yf_s", [SM, 2048], F32)
    mix_s = dscr("mix_s", [2048, SM], BF16)
    atm_s = dscr("atm_s", [SM, 1024], BF16)
    x1_s = dscr("x1_s", [SM, 1024], F32)
    q_s = dscr("q_s", [1024, SM], BF16)
    k_s = dscr("k_s", [1024, SM], BF16)
    v_s = dscr("v_s", [SM, 1024], BF16)
    vecd = dscr("vecd", [8, RL], F32)
    rep_t = nc.dram_tensor("rep", [8 * 128, RL], F32, kind=skind)
    rep = rep_t.ap()

    DBUF = {}

    def db(name, blk=0):
        k = (name, blk)
        if k not in DBUF:
            DBUF[k] = Buf(name)
        return DBUF[k]

    with ExitStack() as st:
        S = Sched(nc, st, n_dma_sems=48)
        AW = 51000
        arena_t = st.enter_context(nc.sbuf_tensor("arena", [128, AW], F32))
        AR = Arena(arena_t, AW)
        pbanks = []
        pall = st.enter_context(nc.psum_tensor("pall", [128, 4096], F32))
        for i in range(8):
            pbanks.append(T(pall[:, i * 512:(i + 1) * 512], "pb%d" % i))
            pbanks[-1].b.excl = True
        PR = Ring(pbanks[0:6])
        trs = []
        for i in (6, 7):
            t_ = T(pbanks[i].ap.bitcast(BF16)[:, 0:512], "tr%d" % i)
            t_.b = pbanks[i].b
            trs.append(t_)
        TRR = Ring(trs)

        ident_f = AR.alloc([128], F32, "ident_f")
        ident = AR.alloc([128], BF16, "ident")
        ones_b = AR.alloc([128], BF16, "ones_b")
        ones_f = AR.alloc([128], F32, "ones_f")
        tri_f = AR.alloc([128], F32, "tri_f")
        triu_f = AR.alloc([128], F32, "triu_f")
        tris_f = AR.alloc([128], F32, "tris_f")
        trisl_f = AR.alloc([128], F32, "trisl_f")
        maskf = AR.alloc([4, 128], F32, "maskf")
        maskb = AR.alloc([4, 128], F32, "maskb")
        maskf_b = AR.alloc([4, 128], BF16, "maskf_b")
        maskb_b = AR.alloc([4, 128], BF16, "maskb_b")
        ssdn = AR.alloc([2048], F32, "ssdn")
        dtb = AR.alloc([64], F32, "dtb")
        avec = AR.alloc([64], F32, "avec")
        dsk = AR.alloc([32], F32, "dsk")
        subln = AR.alloc([128], F32, "subln")
        lamt = AR.alloc([8], F32, "lamt")
        tab15 = AR.alloc([8], F32, "tab15")
        tab31 = AR.alloc([8], F32, "tab31")
        cw = AR.alloc([32, 5], F32, "cw")
        cb = AR.alloc([32], F32, "cb")
        kmT = [AR.alloc([8, 256], BF16, "kmT%d" % l) for l in range(2)]
        vm = [AR.alloc([2, 1024], BF16, "vm%d" % l) for l in range(2)]
        PERSIST = AR.p

        def A_(e, o, i, f, r=(), w=(), **kw):
            return S.emit("act", lambda h: h.activation(o, i, f, **kw), [t.b for t in r], [t.b for t in w])

        def bs(ts):
            return [t if isinstance(t, Buf) else t.b for t in ts]

        def MM(out, lhsT, rhs, start, stop, r, w):
            return S.mm(out, lhsT, rhs, start=start, stop=stop, reads=bs(r), writes=bs(w))

        def ACT(o, i, f, r, w, **kw):
            return S.emit("act", lambda h: h.activation(o, i, f, **kw), bs(r), bs(w))

        def TT(e, o, a, b, op, r, w):
            return S.tt(e, o, a, b, op, reads=bs(r), writes=bs(w))

        def TS(e, o, a, s1, s2, op0, op1, r, w):
            return S.ts(e, o, a, s1, s2, op0, op1, reads=bs(r), writes=bs(w))

        def STT(e, o, a, sc, b, op0, op1, r, w):
            return S.stt(e, o, a, sc, b, op0, op1, reads=bs(r), writes=bs(w))

        def CP(e, o, i, r, w):
            return S.copy(e, o, i, reads=bs(r), writes=bs(w))

        def DMA(o, i, r, w, eng="sp"):
            return S.dma(eng, o, i, reads=bs(r), writes=bs(w))

        def MS(e, ap, val, w):
            return S.memset(e, ap, val, writes=bs(w))

        def AFS(t, pattern, op, fill, base, cm):
            S.emit("pool", lambda h: h.affine_select(t.ap, t.ap, pattern, op, fill, base=base, channel_multiplier=cm),
                   bs([t]), bs([t]))

        MS("pool", ident_f.ap, 1.0, [ident_f])
        AFS(ident_f, [[-1, 128]], ALU.is_equal, 0.0, 0, 1)
        CP("dve", ident.ap, ident_f.ap, [ident_f], [ident])
        MS("pool", ones_f.ap, 1.0, [ones_f])
        MS("pool", ones_b.ap, 1.0, [ones_b])
        MS("pool", tri_f.ap, 1.0, [tri_f])
        AFS(tri_f, [[1, 128]], ALU.is_ge, 0.0, 0, -1)
        MS("pool", triu_f.ap, 1.0, [triu_f])
        AFS(triu_f, [[-1, 128]], ALU.is_ge, 0.0, 0, 1)
        MS("pool", tris_f.ap, 1.0, [tris_f])
        AFS(tris_f, [[-1, 128]], ALU.is_gt, 0.0, 0, 1)
        MS("pool", trisl_f.ap, 1.0, [trisl_f])
        AFS(trisl_f, [[1, 128]], ALU.is_gt, 0.0, 0, -1)
        MS("pool", maskf.ap, 0.0, [maskf])
        S.emit("pool", lambda h: h.affine_select(maskf.ap, maskf.ap, [[0, 4], [1, 128]], ALU.is_ge, NEG, base=0,
                                                 channel_multiplier=-1), bs([maskf]), bs([maskf]))
        MS("pool", maskb.ap, 0.0, [maskb])
        S.emit("pool", lambda h: h.affine_select(maskb.ap, maskb.ap, [[0, 4], [-1, 128]], ALU.is_ge, NEG, base=0,
                                                 channel_multiplier=1), bs([maskb]), bs([maskb]))
        CP("dve", maskf_b.ap, maskf.ap, [maskf], [maskf_b])
        CP("dve", maskb_b.ap, maskb.ap, [maskb], [maskb_b])

        def load_gain(nm, l):
            g = AR.alloc([1024], F32, "g_" + nm)
            DMA(g.ap, P[nm][l:l + 1, :].partition_broadcast(128), [], [g])
            return g
        DMA(ssdn.ap, P["ssd_norm"][0:1, :].partition_broadcast(128), [], [ssdn])
        DMA(dtb.ap, P["ssd_dt_bias"][0:1, :].partition_broadcast(128), [], [dtb])
        DMA(avec.ap, P["ssd_a_log"][0:1, :].partition_broadcast(128), [], [avec])
        DMA(dsk.ap, P["ssd_d"][0:1, :].partition_broadcast(128), [], [dsk])
        DMA(subln.ap, P["diff_subln"][0:1, :].partition_broadcast(128), [], [subln])
        DMA(tab15.ap, P["rel_bias_table"][15:16, :].partition_broadcast(128), [], [tab15])
        DMA(tab31.ap, P["rel_bias_table"][31:32, :].partition_broadcast(128), [], [tab31])
        ACT(avec.ap, avec.ap, AF.Exp, [avec], [avec])
        TS("dve", avec.ap, avec.ap, -1.0, None, ALU.mult, None, [avec], [avec])
        TS("dve", subln.ap, subln.ap, 1.0 - LAM_INIT1, None, ALU.mult, None, [subln], [subln])
        AR.reset(PERSIST)
        lp = AR.alloc([4, 64], F32, "lp")
        lpp = AR.alloc([2, 64], F32, "lpp")
        lps = AR.alloc([2], F32, "lps")
        DMA(lp.ap.rearrange("p a b -> p (a b)"), P["diff_lambda"][0:1, :].partition_broadcast(128), [], [lp])
        lp4 = lp.ap.rearrange("p (a c) b -> p a c b", c=2)
        TT("dve", lpp.ap, lp4[:, :, 0, :], lp4[:, :, 1, :], ALU.mult, [lp], [lpp])
        S.emit("dve", lambda h: h.tensor_reduce(lps.ap, lpp.ap, AX.X, ALU.add), bs([lpp]), bs([lps]))
        ACT(lps.ap, lps.ap, AF.Exp, [lps], [lps])
        TT("dve", lamt.ap[:, 0:1], lps.ap[:, 0:1], lps.ap[:, 1:2], ALU.subtract, [lps], [lamt])
        TS("dve", lamt.ap[:, 0:1], lamt.ap[:, 0:1], LAM_INIT1, None, ALU.add, None, [lamt], [lamt])
        TS("dve", lamt.ap[:, 1:2], lamt.ap[:, 0:1], -1.0, None, ALU.mult, None, [lamt], [lamt])
        cwr = AR.alloc([2, 128], F32, "cwr")
        cbr = AR.alloc([128], F32, "cbr")
        MS("pool", cwr.ap, 0.0, [cwr])
        MS("pool", cbr.ap, 0.0, [cbr])
        cw_rows = P["ssd_conv_w"].rearrange("k (ct p) -> (k ct) p", p=128)
        DMA(cwr.ap[:, 0, :], cw_rows[0:128, :], [], [cwr])
        DMA(cwr.ap[0:32, 1, :], cw_rows[128:160, :], [], [cwr])
        DMA(cbr.ap[0:32, :], P["ssd_conv_b"].rearrange("o (ct p) -> (o ct) p", p=128), [], [cbr])
        pb = PR.next()
        MM(pb.ap[:, 0:128], cwr.ap[:, 0, :], ident_f.ap, True, True, [cwr, ident_f], [pb])
        MM(pb.ap[:, 128:160], cwr.ap[0:32, 1, :], ident_f.ap[0:32, 0:32], True, True, [cwr, ident_f], [pb])
        MM(pb.ap[:, 160:192], cbr.ap[0:32, :], ident_f.ap[0:32, 0:32], True, True, [cbr, ident_f], [pb])
        CP("dve", cw.ap.rearrange("p ct k -> p k ct"), pb.ap[:, 0:160].rearrange("p (k ct) -> p k ct", k=5), [pb], [cw])
        CP("dve", cb.ap, pb.ap[:, 160:192], [pb], [cb])
        tabs = AR.alloc([8], F32, "tabs")
        ohs = AR.alloc([RL], F32, "ohs")
        rv = AR.alloc([RL], F32, "rv")
        DMA(tabs.ap[0:32, :], P["rel_bias_table"][:, :], [], [tabs])
        DMA(ohs.ap[0:32, :], oh_in[:, :], [], [ohs])
        for c0 in range(0, RL, 512):
            n = min(512, RL - c0)
            pb = PR.next()
            MM(pb.ap[0:8, 0:n], tabs.ap[0:32, :], ohs.ap[0:32, c0:c0 + n], True, True, [tabs, ohs], [pb])
            CP("dve", rv.ap[0:8, c0:c0 + n], pb.ap[0:8, 0:n], [pb], [rv])
        DMA(vecd[:, :], rv.ap[0:8, :], [rv], [db("vecd")])
        S.barrier()
        for h in range(8):
            DMA(rep[h * 128:(h + 1) * 128, :], vecd[h:h + 1, :].partition_broadcast(128), [db("vecd")], [db("rep", h)])

        def emit_cast(key, r0):
            DMA(WB[key][r0:r0 + 256, :], P[WSRC[key]][r0:r0 + 256, :], [], [db("w_" + key, r0)], eng="pool")

        for key in ("kv", "in0"):
            for r0 in range(0, P[WSRC[key]].shape[0], 256):
                emit_cast(key, r0)
        pending_casts = []
        for key, lo, hi in (("out0", 0, 3072), ("w1", 0, 1024), ("w2", 0, 4096), ("in1", 0, 1024),
                            ("out1", 0, 2048), ("w1", 1024, 2048), ("w2", 4096, 8192)):
            for r0 in range(lo, hi, 256):
                pending_casts.append((key, r0))

        def drip(n):
            for _ in range(min(n, len(pending_casts))):
                emit_cast(*pending_casts.pop(0))
        S.barrier()

        def new_phase():
            S.barrier()
            AR.reset(PERSIST)

        WR = [None]

        def wload(key, r0, nkt, c0, ncols):
            t = WR[0].next()
            src = WB[key][r0:r0 + nkt * 128, c0:c0 + ncols].rearrange("(kt p) c -> p kt c", p=128)
            deps = [db("w_" + key, r) for r in range(r0 - r0 % 256, r0 + nkt * 128, 256)]
            assert all(d_.w for d_ in deps), ("weight block not cast yet", key, r0)
            DMA(t.ap[:, 0:nkt, 0:ncols], src, deps, [t])
            return t

        STQ = "pool"
        evac_i = [0]

        def evac(out_ap, in_ap, r, w):
            evac_i[0] += 1
            if evac_i[0] % 2 == 0:
                ACT(out_ap, in_ap, AF.Copy, r, w)
            else:
                CP("dve", out_ap, in_ap, r, w)

        def tm_norm(src, g, dst, ss, sq, r_extra=()):
            MS("pool", ss.ap, 0.0, [ss])
            for j in range(4):
                ACT(sq.ap, src.ap[:, j, :], AF.Square, [src, ss] + list(r_extra), [sq, ss], accum_out=ss.ap[:, j:j + 1])
            ACT(ss.ap[:, 4:8], ss.ap[:, 0:4], AF.Ln, [ss], [ss], bias=EPS, scale=1.0 / D)
            ACT(ss.ap[:, 8:12], ss.ap[:, 4:8], AF.Exp, [ss], [ss], scale=-0.5)
            for j in range(4):
                STT("dve", dst.ap[:, j, :], src.ap[:, j, :], ss.ap[:, 8 + j:9 + j], g.ap, ALU.mult, ALU.mult,
                    [src, ss, g], [dst])

        def transpose_block(hb, hT, nkt=8):
            for kt in range(nkt):
                tr = TRR.next()
                for j in range(4):
                    S.tr(tr.ap[:, j * 128:(j + 1) * 128], hb.ap[:, j, kt * 128:(kt + 1) * 128], ident.ap,
                         reads=bs([hb, ident]), writes=bs([tr]))
                CP("dve", hT.ap[:, kt, :], tr.ap, [tr], [hT])

        def cross_attn(l, qT, moT, E, rden):
            for h in range(4):
                es = []
                for mt in range(2):
                    pb = PR.next()
                    for dt_ in range(2):
                        MM(pb.ap, kmT[l].ap[:, 2 * h + dt_, mt * 128:(mt + 1) * 128], qT.ap[:, 2 * h + dt_, :],
                           dt_ == 0, dt_ == 1, [kmT[l], qT], [pb])
                    e = E.next()
                    ACT(e.ap, pb.ap, AF.Exp, [pb], [e], scale=1.0 / 16.0)
                    es.append(e)
                pden = PR.next()
                for mt in range(2):
                    MM(pden.ap, ones_b.ap, es[mt].ap, mt == 0, mt == 1, [ones_b, es[mt]], [pden])
                S.emit("dve", lambda hh, o=rden.ap, i=pden.ap: hh.reciprocal(o, i), bs([pden]), bs([rden]))
                for dt_ in range(2):
                    pn = PR.next()
                    for mt in range(2):
                        MM(pn.ap, vm[l].ap[:, mt, (2 * h + dt_) * 128:(2 * h + dt_ + 1) * 128], es[mt].ap,
                           mt == 0, mt == 1, [vm[l], es[mt]], [pn])
                    TT("dve", moT.ap[:, 2 * h + dt_, :], pn.ap, rden.ap, ALU.mult, [pn, rden], [moT])

        def phase_KV(si, m0):
            new_phase()
            WR[0] = Ring([AR.alloc([8, 512], BF16, "w%d" % i) for i in range(3)])
            mt_ = AR.alloc([4, 1024], F32, "memt")
            ss = AR.alloc([12], F32, "ss")
            sq = AR.alloc([1024], F32, "sq")
            hb = AR.alloc([4, 1024], BF16, "hb")
            hT = AR.alloc([8, 512], BF16, "hT")
            gm = [load_gain("x_mem_norm", l) for l in range(2)]
            MS("pool", mt_.ap[:, 2:4, :], 0.0, [mt_])
            DMA(mt_.ap[:, 0:2, :], mem_in[m0:m0 + 256, :].rearrange("(j p) d -> p j d", p=128), [], [mt_])
            for l in range(2):
                tm_norm(mt_, gm[l], hb, ss, sq)
                transpose_block(hb, hT)
                for c in range(4):
                    w = wload("kv", l * 1024, 8, c * 512, 512)
                    if c < 2:
                        for i in range(4):
                            pb = PR.next()
                            for kt in range(8):
                                MM(pb.ap[:, 0:256], w.ap[:, kt, i * 128:(i + 1) * 128], hT.ap[:, kt, 0:256],
                                   kt == 0, kt == 7, [w, hT], [pb])
                            evac(kmT[l].ap[:, c * 4 + i, :], pb.ap[:, 0:256], [pb], [kmT[l]])
                    else:
                        for mt in range(2):
                            pb = PR.next()
                            for kt in range(8):
                                MM(pb.ap, hT.ap[:, kt, mt * 128:(mt + 1) * 128], w.ap[:, kt, :],
                                   kt == 0, kt == 7, [w, hT], [pb])
                            evac(vm[l].ap[:, mt, (c - 2) * 512:(c - 1) * 512], pb.ap, [pb], [vm[l]])

        def alloc_front(host=None, with_dt=True):
            d = {}
            d["xt"] = AR.alloc([4, 1024], F32, "xt")
            d["ss"] = AR.alloc([12], F32, "ss")
            d["sq"] = AR.alloc([1024], F32, "sq")
            d["hb"] = AR.alloc([4, 1024], BF16, "hb")
            d["hT"] = AR.alloc([8, 512], BF16, "hT")
            d["E"] = Ring([AR.alloc([512], BF16, "E%d" % i) for i in range(4)])
            d["rden"] = AR.alloc([512], F32, "rden")
            if host is None:
                d["qT"] = AR.alloc([8, 512], BF16, "qT")
                d["moT"] = AR.alloc([8, 512], BF16, "moT")
                d["fst"] = Ring([AR.alloc([4, 512], BF16, "fst%d" % i) for i in range(2)])
            else:
                sub = Arena(arena_t, AW)
                sub.reset(host[0])
                d["qT"] = sub.alloc([8, 512], BF16, "qT")
                d["moT"] = sub.alloc([8, 512], BF16, "moT")
                f0 = sub.alloc([4, 512], BF16, "fst0")
                f1 = sub.alloc([4, 512], BF16, "fst1")
                assert sub.p <= host[0] + host[1]
                for t in (d["qT"], d["moT"], f0, f1):
                    t.b = host[2]
                d["fst"] = Ring([f0, f1])
            if with_dt:
                d["dtt"] = AR.alloc([4, 4, 64], F32, "dtt")
            return d

        def front(l, t0, S_, tb, d):
            xt, hb, hT, qT, moT = d["xt"], d["hb"], d["hT"], d["qT"], d["moT"]
            c0t = tb * TB
            tm_norm(xt, d["g_pre_mix%d" % l], hb, d["ss"], d["sq"])
            transpose_block(hb, hT)
            if l == 0:
                chunks = [("z", 512 * c, 512, c) for c in range(4)] + [("xbc", 2048 + 512 * c, 512, c) for c in range(8)] \
                    + [("dt", 6144, 64, 0)] + [("q", 6208 + 512 * c, 512, c) for c in range(2)]
                key = "in0"
            else:
                chunks = [("qd", 512 * c, 512, c) for c in range(2)] + [("kd", 1024 + 512 * c, 512, c) for c in range(2)] \
                    + [("vd", 2048 + 512 * c, 512, c) for c in range(2)] + [("q", 3072 + 512 * c, 512, c) for c in range(2)]
                key = "in1"
            for kind, col0, ncols, c in chunks:
                w = wload(key, 0, 8, col0, ncols)
                if kind in ("xbc", "q", "qd", "kd"):
                    stg = None if kind == "q" else d["fst"].next()
                    for i in range(4):
                        pb = PR.next()
                        for kt in range(8):
                            MM(pb.ap, w.ap[:, kt, i * 128:(i + 1) * 128], hT.ap[:, kt, :], kt == 0, kt == 7, [w, hT], [pb])
                        if kind == "q":
                            evac(qT.ap[:, c * 4 + i, :], pb.ap, [pb], [qT])
                        else:
                            evac(stg.ap[:, i, :], pb.ap, [pb], [stg])
                    if kind != "q":
                        dst = {"xbc": xbc_s, "qd": q_s, "kd": k_s}[kind]
                        DMA(dst[c * 512:(c + 1) * 512, c0t:c0t + TB].rearrange("(i p) t -> p i t", p=128), stg.ap,
                            [stg], [db(kind, tb)], eng=STQ)
                elif kind in ("z", "vd"):
                    stg = d["fst"].next()
                    for j in range(4):
                        pb = PR.next()
                        for kt in range(8):
                            MM(pb.ap, hT.ap[:, kt, j * 128:(j + 1) * 128], w.ap[:, kt, :], kt == 0, kt == 7, [w, hT], [pb])
                        if kind == "z":
                            ACT(stg.ap[:, j, :], pb.ap, AF.Silu, [pb], [stg])
                        else:
                            evac(stg.ap[:, j, :], pb.ap, [pb], [stg])
                    dst = z_s if kind == "z" else v_s
                    DMA(dst[c0t:c0t + TB, c * 512:(c + 1) * 512].rearrange("(j p) c -> p j c", p=128), stg.ap,
                        [stg], [db(kind, tb)], eng=STQ)
                else:
                    dtt = d["dtt"]
                    pb = PR.next()
                    for j in range(4):
                        for kt in range(8):
                            MM(pb.ap[:, j * 64:(j + 1) * 64], hT.ap[:, kt, j * 128:(j + 1) * 128], w.ap[:, kt, 0:64],
                               kt == 0, kt == 7, [w, hT], [pb])
                    pv = pb.ap[:, 0:256].rearrange("p (j c) -> p j c", j=4)
                    dtb_b = dtb.ap.unsqueeze(1).to_broadcast([128, 4, 64])
                    TT("dve", dtt.ap[:, 0], pv, dtb_b, ALU.add, [pb, dtb], [dtt])
                    STT("dve", dtt.ap[:, 1], dtt.ap[:, 0], -1.0, dtt.ap[:, 0], ALU.mult, ALU.max, [dtt], [dtt])
                    ACT(dtt.ap[:, 2], dtt.ap[:, 1], AF.Exp, [dtt], [dtt], scale=-1.0)
                    ACT(dtt.ap[:, 3], dtt.ap[:, 2], AF.Ln, [dtt], [dtt], bias=1.0)
                    STT("dve", dtt.ap[:, 1], dtt.ap[:, 0], 0.0, dtt.ap[:, 3], ALU.max, ALU.add, [dtt], [dtt])
                    DMA(dt_s[c0t:c0t + TB, :].rearrange("(j p) c -> p j c", p=128), dtt.ap[:, 1], [dtt], [db("dt", tb)], eng=STQ)
            cross_attn(l, qT, moT, d["E"], d["rden"])
            DMA(mo_s[l][:, c0t:c0t + TB].rearrange("(i p) t -> p i t", p=128), moT.ap, [moT], [db("mo%d" % l, tb)], eng=STQ)

        def phase_A0(si, t0, S_):
            new_phase()
            WR[0] = Ring([AR.alloc([8, 512], BF16, "w%d" % i) for i in range(4)])
            d = alloc_front()
            d["g_pre_mix0"] = load_gain("norm_pre_mix", 0)
            for tb in range(S_ // TB):
                drip(8)
                DMA(d["xt"].ap, x_in[t0 + tb * TB:t0 + (tb + 1) * TB, :].rearrange("(j p) d -> p j d", p=128), [], [d["xt"]])
                front(0, t0, S_, tb, d)
            drip(len(pending_casts))

        def phase_B0(si, S_):
            new_phase()
            xin = Ring([AR.alloc([32, 516], BF16, "xin%d" % i) for i in range(1)])
            dgall = AR.alloc([160, 128], BF16, "dgall")
            for ct in range(32):
                for k in range(5):
                    TS("dve", dgall.ap[:, ct * 5 + k, :], ident_f.ap, cw.ap[:, ct, k:k + 1], None,
                       ALU.mult, None, [ident_f, cw], [dgall])
            pc = Ring([AR.alloc([32, 512], BF16, "pc%d" % i) for i in range(1)])
            xtm = Ring([AR.alloc([4, 2048], BF16, "xtm%d" % i) for i in range(2)])
            btm = Ring([AR.alloc([4, 1024], BF16, "btm%d" % i) for i in range(2)])
            nb = S_ // TB
            for tb in range(nb):
                xi = xin.next()
                lo = max(0, tb * TB - 2)
                hi = min(S_, tb * TB + TB + 2)
                o0 = lo - (tb * TB - 2)
                if tb == 0:
                    MS("pool", xi.ap[:, :, 0:2], 0.0, [xi])
                if tb == nb - 1:
                    MS("pool", xi.ap[:, :, 514:516], 0.0, [xi])
                rds = [db("xbc", b) for b in (tb - 1, tb, tb + 1) if 0 <= b < nb]
                for q4 in range(4):
                    DMA(xi.ap[:, q4 * 8:(q4 + 1) * 8, o0:o0 + (hi - lo)],
                        xbc_s[q4 * 1024:(q4 + 1) * 1024, lo:hi].rearrange("(ct p) t -> p ct t", p=128), rds, [xi])
                po = pc.next()
                for ct in range(32):
                    pb = PR.next()
                    for k in range(5):
                        MM(pb.ap, dgall.ap[:, ct * 5 + k, :], xi.ap[:, ct, k:k + 512], k == 0, k == 4, [dgall, xi], [pb])
                    ACT(po.ap[:, ct, :], pb.ap, AF.Silu, [pb, cb], [po], bias=cb.ap[:, ct:ct + 1])
                c0t = tb * TB
                DMA(bt_s[:, c0t:c0t + TB].rearrange("(i p) t -> p i t", p=128), po.ap[:, 16:24, :], [po], [db("bt", tb)])
                DMA(ct_s[:, c0t:c0t + TB].rearrange("(i p) t -> p i t", p=128), po.ap[:, 24:32, :], [po], [db("ct", tb)])
                xt_ = xtm.next()
                bt_ = btm.next()
                for j in range(4):
                    for ct in range(24):
                        if ct % 4 == 0:
                            tr = TRR.next()
                        S.tr(tr.ap[:, (ct % 4) * 128:(ct % 4 + 1) * 128], po.ap[:, ct, j * 128:(j + 1) * 128], ident.ap,
                             reads=bs([po, ident]), writes=bs([tr]))
                        if ct % 4 == 3:
                            c4 = ct // 4
                            if c4 < 4:
                                CP("dve", xt_.ap[:, j, c4 * 512:(c4 + 1) * 512], tr.ap, [tr], [xt_])
                            else:
                                CP("dve", bt_.ap[:, j, (c4 - 4) * 512:(c4 - 3) * 512], tr.ap, [tr], [bt_])
                DMA(xtm_s[c0t:c0t + TB, :].rearrange("(j p) c -> p j c", p=128), xt_.ap, [xt_], [db("xtm", tb)])
                DMA(btm_s[c0t:c0t + TB, :].rearrange("(j p) c -> p j c", p=128), bt_.ap, [bt_], [db("btm", tb)])

        def phase_C0(si, S_):
            new_phase()
            nb = S_ // TB
            xtm = Ring([AR.alloc([2048], BF16, "sxtm%d" % i) for i in range(2)])
            btm = Ring([AR.alloc([1024], BF16, "sbtm%d" % i) for i in range(2)])
            btf = Ring([AR.alloc([8, 512], BF16, "sbt%d" % i) for i in range(2)])
            ctf = Ring([AR.alloc([8, 512], BF16, "sct%d" % i) for i in range(2)])
            dtr = Ring([AR.alloc([4, 64], F32, "sdt%d" % i) for i in range(2)])
            zr = Ring([AR.alloc([2048], BF16, "sz%d" % i) for i in range(2)])
            yfl = Ring([AR.alloc([2048], F32, "syf%d" % i) for i in range(2)])
            xdsr = Ring([AR.alloc([2048], BF16, "xds%d" % i) for i in range(2)])
            ldtr = Ring([AR.alloc([64], F32, "ldt%d" % i) for i in range(2)])
            state = AR.alloc([2048], F32, "state")
            stateb = AR.alloc([2048], BF16, "stateb")
            dar = Ring([AR.alloc([32], F32, "da%d" % i) for i in range(2)])
            xdte = Ring([AR.alloc([2048], BF16, "xdte%d" % i) for i in range(2)])
            ex3 = Ring([AR.alloc([96], F32, "ex3_%d" % i) for i in range(2)])
            ncum = Ring([AR.alloc([32], F32, "ncum%d" % i) for i in range(2)])
            gt = Ring([AR.alloc([128], BF16, "gt%d" % i) for i in range(3)])
            exs = Ring([AR.alloc([4, 128], BF16, "exs%d" % i) for i in range(3)])
            wt = Ring([AR.alloc([4, 128], BF16, "wt%d" % i) for i in range(3)])
            yc = Ring([AR.alloc([2048], F32, "yc%d" % i) for i in range(2)])
            ytmp = Ring([AR.alloc([256], F32, "ytmp%d" % i) for i in range(3)])
            ssgr = Ring([AR.alloc([24], F32, "ssg%d" % i) for i in range(2)])
            sq = AR.alloc([2048], BF16, "sq2")
            ybr = Ring([AR.alloc([2048], BF16, "yb%d" % i) for i in range(2)])
            mixT = Ring([AR.alloc([16, 128], BF16, "mixT%d" % i) for i in range(2)])

            class Ctx:
                pass

            blk = {}

            def prologue(c):
                dirn, tb, j = c.dirn, c.tb, c.j
                c0t = tb * TB
                key = (dirn, tb)
                if key not in blk:
                    Bt = btf.next()
                    Ct = ctf.next()
                    Dt = dtr.next()
                    DMA(Bt.ap, bt_s[:, c0t:c0t + TB].rearrange("(i p) t -> p i t", p=128), [db("bt", tb)], [Bt])
                    DMA(Ct.ap, ct_s[:, c0t:c0t + TB].rearrange("(i p) t -> p i t", p=128), [db("ct", tb)], [Ct])
                    DMA(Dt.ap, dt_s[c0t:c0t + TB, :].rearrange("(j p) c -> p j c", p=128), [db("dt", tb)], [Dt])
                    blk.clear()
                    blk[key] = (Bt, Ct, Dt)
                c.Bt, c.Ct, c.Dt = blk[key]
                c.r0 = c0t + j * 128
                c.ck = tb * 4 + j
                c.X = xtm.next()
                c.Bm = btm.next()
                DMA(c.X.ap, xtm_s[c.r0:c.r0 + 128, :], [db("xtm", tb)], [c.X])
                DMA(c.Bm.ap, btm_s[c.r0:c.r0 + 128, :], [db("btm", tb)], [c.Bm])
                if dirn == 1:
                    c.Z = zr.next()
                    DMA(c.Z.ap, z_s[c.r0:c.r0 + 128, :], [db("z", tb)], [c.Z])
                c.tri_c = tri_f if dirn == 0 else triu_f
                c.tri_e = tris_f if dirn == 0 else trisl_f
                c.mask = maskf_b if dirn == 0 else maskb_b
                dtj = c.Dt.ap[:, j, dirn * 32:(dirn + 1) * 32]
                c.da = dar.next()
                TT("dve", c.da.ap, dtj, avec.ap[:, dirn * 32:(dirn + 1) * 32], ALU.mult, [c.Dt, avec], [c.da])
                pm = PR.next()
                MM(pm.ap[:, 0:32], c.tri_c.ap, c.da.ap, True, True, [c.tri_c, c.da], [pm])
                MM(pm.ap[:, 32:64], c.tri_e.ap, c.da.ap, True, True, [c.tri_e, c.da], [pm])
                MM(pm.ap[:, 64:96], ones_f.ap, c.da.ap, True, True, [ones_f, c.da], [pm])
                c.e3 = ex3.next()
                ACT(c.e3.ap, pm.ap[:, 0:96], AF.Exp, [pm], [c.e3])
                ld = ldtr.next()
                ACT(ld.ap[:, 0:32], dtj, AF.Ln, [c.Dt], [ld])
                c.nc_ = ncum.next()
                STT("dve", c.nc_.ap, pm.ap[:, 0:32], -1.0, ld.ap[:, 0:32], ALU.mult, ALU.add, [pm, ld], [c.nc_])
                TT("dve", ld.ap[:, 32:64], dtj, c.e3.ap[:, 32:64], ALU.mult, [c.Dt, c.e3], [ld])
                c.xe = xdte.next()
                TT("dve", c.xe.ap.rearrange("p (h c) -> p h c", h=32), c.X.ap.rearrange("p (h c) -> p h c", h=32),
                   ld.ap[:, 32:64].unsqueeze(2).to_broadcast([128, 32, 64]), ALU.mult, [c.X, ld], [c.xe])
                if dirn == 0:
                    c.xds = xdsr.next()
                    TT("pool", c.xds.ap.rearrange("p (h c) -> p h c", h=32), c.X.ap.rearrange("p (h c) -> p h c", h=32),
                       dsk.ap.unsqueeze(2).to_broadcast([128, 32, 64]), ALU.mult, [c.X, dsk], [c.xds])
                else:
                    c.yf = yfl.next()
                    DMA(c.yf.ap, yf_s[c.r0:c.r0 + 128, :], [db("yf", c.ck)], [c.yf])
                c.Y = yc.next()
                c.pupd = None

            def front_g(c, g):
                j = c.j
                pg = PR.next()
                MM(pg.ap[:, 0:128], c.Bt.ap[:, g, j * 128:(j + 1) * 128], c.Ct.ap[:, g, j * 128:(j + 1) * 128],
                   True, True, [c.Bt, c.Ct], [pg])
                G = gt.next()
                ACT(G.ap, pg.ap[:, 0:128], AF.Copy, [pg], [G])
                psg = PR.next()
                for hh in range(4):
                    h_ = g * 4 + hh
                    S.mm(psg.ap[:, hh * 128:(hh + 1) * 128], c.da.ap[:, h_:h_ + 1].to_broadcast([128, 128]), c.tri_c.ap,
                         start=(hh == 0), stop=False, reads=bs([c.da, c.tri_c]), writes=bs([psg]), skip_group_check=True)
                S.mm(psg.ap, ident.ap, c.mask.ap.rearrange("p h t -> p (h t)"), start=False, stop=True,
                     reads=bs([ident, c.mask]), writes=bs([psg]), skip_group_check=True)
                ex = exs.next()
                for hh in range(4):
                    ACT(ex.ap[:, hh, :], psg.ap[:, hh * 128:(hh + 1) * 128], AF.Exp, [psg, c.nc_], [ex],
                        bias=c.nc_.ap[:, g * 4 + hh:g * 4 + hh + 1])
                W = wt.next()
                TT("dve", W.ap, ex.ap, G.ap.unsqueeze(1).to_broadcast([128, 4, 128]), ALU.mult, [ex, G], [W])
                return W

            def back_g(c, g, W):
                j = c.j
                py = PR.next()
                for hh in range(4):
                    h_ = g * 4 + hh
                    if c.dirn == 0:
                        S.mm(py.ap[:, hh * 64:(hh + 1) * 64], W.ap[:, hh, :], c.X.ap[:, h_ * 64:(h_ + 1) * 64],
                             start=True, stop=False, reads=bs([W, c.X]), writes=bs([py]), skip_group_check=True)
                        S.mm(py.ap[:, hh * 64:(hh + 1) * 64], ident.ap, c.xds.ap[:, h_ * 64:(h_ + 1) * 64],
                             start=False, stop=True, reads=bs([ident, c.xds]), writes=bs([py]), skip_group_check=True)
                    else:
                        MM(py.ap[:, hh * 64:(hh + 1) * 64], W.ap[:, hh, :], c.X.ap[:, h_ * 64:(h_ + 1) * 64],
                           True, True, [W, c.X], [py])
                MM(py.ap[:, 256:512], c.Ct.ap[:, g, j * 128:(j + 1) * 128], stateb.ap[:, g * 256:(g + 1) * 256],
                   True, True, [c.Ct, stateb], [py])
                yt = ytmp.next()
                TT("dve", yt.ap.rearrange("p (h c) -> p h c", h=4), py.ap[:, 256:512].rearrange("p (h c) -> p h c", h=4),
                   c.e3.ap[:, g * 4:(g + 1) * 4].unsqueeze(2).to_broadcast([128, 4, 64]), ALU.mult, [py, c.e3], [yt])
                TT("dve", c.Y.ap[:, g * 256:(g + 1) * 256], yt.ap, py.ap[:, 0:256], ALU.add, [yt, py], [c.Y])
                if g % 2 == 0:
                    c.pupd = PR.next()
                MM(c.pupd.ap[:, (g % 2) * 256:(g % 2 + 1) * 256], c.Bm.ap[:, g * 128:(g + 1) * 128],
                   c.xe.ap[:, g * 256:(g + 1) * 256], True, True, [c.Bm, c.xe], [c.pupd])
                if g % 2 == 1:
                    g0 = g - 1
                    sl = slice(g0 * 256, (g0 + 2) * 256)
                    TT("pool", state.ap[:, sl].rearrange("p (h c) -> p h c", h=8),
                       state.ap[:, sl].rearrange("p (h c) -> p h c", h=8),
                       c.e3.ap[:, 64 + g0 * 4:64 + g0 * 4 + 8].unsqueeze(2).to_broadcast([128, 8, 64]), ALU.mult,
                       [state, c.e3], [state])
                    TT("dve", state.ap[:, sl], state.ap[:, sl], c.pupd.ap, ALU.add, [state, c.pupd], [state])
                    ACT(stateb.ap[:, sl], state.ap[:, sl], AF.Copy, [state], [stateb])

            def ep_stages(c):
                Y, r0, ck = c.Y, c.r0, c.ck
                if c.dirn == 0:
                    return [lambda: DMA(yf_s[r0:r0 + 128, :], Y.ap, [Y], [db("yf", ck)])]
                Z = c.Z
                st_ = {}

                def s1():
                    TT("pool", Y.ap, Y.ap, c.yf.ap, ALU.add, [Y, c.yf], [Y])

                def s2():
                    TT("dve", Y.ap, Y.ap, Z.ap, ALU.mult, [Y, Z], [Y])

                def s3():
                    ssg = ssgr.next()
                    st_["ssg"] = ssg
                    MS("pool", ssg.ap, 0.0, [ssg])
                    for g in range(8):
                        ACT(sq.ap[:, g * 256:(g + 1) * 256], Y.ap[:, g * 256:(g + 1) * 256], AF.Square, [Y, ssg], [sq, ssg],
                            accum_out=ssg.ap[:, g:g + 1])
                    ACT(ssg.ap[:, 8:16], ssg.ap[:, 0:8], AF.Ln, [ssg], [ssg], bias=EPS, scale=1.0 / 256.0)
                    ACT(ssg.ap[:, 16:24], ssg.ap[:, 8:16], AF.Exp, [ssg], [ssg], scale=-0.5)

                def s4():
                    ssg = st_["ssg"]
                    c.yb = ybr.next()
                    for g in range(8):
                        STT("dve", c.yb.ap[:, g * 256:(g + 1) * 256], Y.ap[:, g * 256:(g + 1) * 256], ssg.ap[:, 16 + g:17 + g],
                            ssdn.ap[:, g * 256:(g + 1) * 256], ALU.mult, ALU.mult, [Y, ssg, ssdn], [c.yb])

                def s5():
                    yb = c.yb
                    mt_ = mixT.next()
                    for c4 in range(4):
                        tr = TRR.next()
                        for i in range(4):
                            ct = c4 * 4 + i
                            S.tr(tr.ap[:, i * 128:(i + 1) * 128], yb.ap[:, ct * 128:(ct + 1) * 128], ident.ap,
                                 reads=bs([yb, ident]), writes=bs([tr]))
                        CP("dve", mt_.ap[:, c4 * 4:(c4 + 1) * 4, :], tr.ap.rearrange("p (i t) -> p i t", i=4), [tr], [mt_])
                    DMA(mix_s[:, r0:r0 + 128].rearrange("(i p) t -> p i t", p=128), mt_.ap, [mt_], [db("mix", ck)])

                return [s1, None, s2, None, s3, None, s4, None, s5]

            for dirn in range(2):
                MS("pool", state.ap, 0.0, [state])
                MS("pool", stateb.ap, 0.0, [stateb])
                blocks = list(range(nb)) if dirn == 0 else list(range(nb - 1, -1, -1))
                chunks = []
                for tb in blocks:
                    for j in (range(4) if dirn == 0 else range(3, -1, -1)):
                        c = Ctx()
                        c.dirn, c.tb, c.j = dirn, tb, j
                        chunks.append(c)
                items = [(c, g) for c in chunks for g in range(8)]
                deferred = []
                prologue(items[0][0])
                Wn = front_g(*items[0])
                for idx, (c, g) in enumerate(items):
                    Wc = Wn
                    if idx + 1 < len(items):
                        c2, g2 = items[idx + 1]
                        if g2 == 0:
                            prologue(c2)
                        Wn = front_g(c2, g2)
                    back_g(c, g, Wc)
                    for q_ in deferred:
                        if q_:
                            f_ = q_.pop(0)
                            if f_ is not None:
                                f_()
                    deferred = [q_ for q_ in deferred if q_]
                    if g == 7:
                        deferred.append(ep_stages(c))
                for q_ in deferred:
                    for f_ in q_:
                        if f_ is not None:
                            f_()

        def phase_D(l, si, t0, S_):
            new_phase()
            WR[0] = Ring([AR.alloc([8, 512], BF16, "w%d" % i) for i in range(4)])
            nk = 24 if l == 0 else 16
            hp0 = AR.p
            hid = AR.alloc([32, 512], BF16, "hid")
            sub = Arena(arena_t, AW)
            sub.reset(hp0)
            actT = sub.alloc([nk, 512], BF16, "actT")
            actT.b = hid.b
            d = alloc_front(host=(hp0, 8192, hid.b), with_dt=False)
            ot = AR.alloc([4, 1024], F32, "ot")
            rl = Ring([AR.alloc([512], F32, "rl%d" % i) for i in range(2)])
            xt, hb, hT = d["xt"], d["hb"], d["hT"]
            atm = hb
            g_post_mix = load_gain("norm_post_mix", l)
            g_pre_mlp = load_gain("norm_pre_mlp", l)
            g_post_mlp = load_gain("norm_post_mlp", l)
            if l == 0:
                d["g_pre_mix1"] = load_gain("norm_pre_mix", 1)
            nmix = nk - 8
            ot2 = AR.alloc([4, 1024], F32, "ot2")
            ots = [ot, ot2]
            key = "out0" if l == 0 else "out1"
            nb_ = S_ // TB

            def out_proj(tb, ot):
                c0t = tb * TB
                if l == 0:
                    for c4 in range(4):
                        DMA(actT.ap[:, 0:16, c4 * 128:(c4 + 1) * 128],
                            mix_s[:, c0t + c4 * 128:c0t + (c4 + 1) * 128].rearrange("(i p) t -> p i t", p=128),
                            [db("mix", tb * 4 + c4)], [actT])
                else:
                    DMA(atm.ap, atm_s[c0t:c0t + TB, :].rearrange("(j p) c -> p j c", p=128),
                        [db("atm", (tb, hh)) for hh in range(8)], [atm])
                    transpose_block(atm, actT)
                DMA(actT.ap[:, nmix:nk, :], mo_s[l][:, c0t:c0t + TB].rearrange("(i p) t -> p i t", p=128),
                    [db("mo%d" % l, tb)], [actT])
                for ch in range(2):
                    accs = [PR.next() for _ in range(4)]
                    for kc in range(nk // 8):
                        w = wload(key, kc * 1024, 8, ch * 512, 512)
                        for j in range(4):
                            for kt in range(8):
                                MM(accs[j].ap, actT.ap[:, kc * 8 + kt, j * 128:(j + 1) * 128], w.ap[:, kt, :],
                                   kc == 0 and kt == 0, kc == nk // 8 - 1 and kt == 7, [actT, w], [accs[j]])
                    for j in range(4):
                        evac(ot.ap[:, j, ch * 512:(ch + 1) * 512], accs[j].ap, [accs[j]], [ot])

            out_proj(0, ots[0])
            for tb in range(nb_):
                c0t = tb * TB
                ot = ots[tb % 2]
                if l == 0:
                    DMA(xt.ap, x_in[t0 + c0t:t0 + c0t + TB, :].rearrange("(j p) d -> p j d", p=128), [], [xt])
                else:
                    DMA(xt.ap, x1_s[c0t:c0t + TB, :].rearrange("(j p) d -> p j d", p=128), [db("x1", tb)], [xt])
                tm_norm(ot, g_post_mix, ot, d["ss"], d["sq"])
                TT("dve", xt.ap, xt.ap, ot.ap, ALU.add, [xt, ot], [xt])
                tm_norm(xt, g_pre_mlp, hb, d["ss"], d["sq"])
                transpose_block(hb, hT)
                for fc in range(8):
                    w = wload("w1", l * 1024, 8, fc * 512, 512)
                    for i in range(4):
                        pb = PR.next()
                        for kt in range(8):
                            MM(pb.ap, w.ap[:, kt, i * 128:(i + 1) * 128], hT.ap[:, kt, :], kt == 0, kt == 7, [w, hT], [pb])
                        r = rl.next()
                        ACT(r.ap, pb.ap, AF.Relu, [pb], [r])
                        TT("pool", hid.ap[:, fc * 4 + i, :], r.ap, r.ap, ALU.mult, [r], [hid])
                for ch in range(2):
                    accs = [PR.next() for _ in range(4)]
                    for fc in range(4):
                        w = wload("w2", l * 4096 + fc * 1024, 8, ch * 512, 512)
                        for j in range(4):
                            for ft in range(8):
                                MM(accs[j].ap, hid.ap[:, fc * 8 + ft, j * 128:(j + 1) * 128], w.ap[:, ft, :],
                                   fc == 0 and ft == 0, fc == 3 and ft == 7, [hid, w], [accs[j]])
                    for j in range(4):
                        evac(ot.ap[:, j, ch * 512:(ch + 1) * 512], accs[j].ap, [accs[j]], [ot])
                if tb + 1 < nb_:
                    out_proj(tb + 1, ots[(tb + 1) % 2])
                tm_norm(ot, g_post_mlp, ot, d["ss"], d["sq"])
                TT("dve", xt.ap, xt.ap, ot.ap, ALU.add, [xt, ot], [xt])
                if l == 0:
                    DMA(x1_s[c0t:c0t + TB, :].rearrange("(j p) d -> p j d", p=128), xt.ap, [xt], [db("x1", tb)], eng=STQ)
                    if lvl >= ORDER.index("B1"):
                        front(1, t0, S_, tb, d)
                else:
                    DMA(y_out[t0 + c0t:t0 + c0t + TB, :].rearrange("(j p) d -> p j d", p=128), xt.ap, [xt], [db("y", (si, tb))], eng=STQ)

        def phase_B1(si, S_):
            new_phase()
            nkt = S_ // 128
            nqb = S_ // TB
            qh = Ring([AR.alloc([S_], BF16, "qh%d" % i) for i in range(2)])
            kh = Ring([AR.alloc([S_], BF16, "kh%d" % i) for i in range(2)])
            vh = Ring([AR.alloc([nkt, 130], BF16, "vh%d" % i) for i in range(2)])
            bt6 = Ring([AR.alloc([6, 512], F32, "bt6_%d" % i) for i in range(2)])
            Er = Ring([AR.alloc([2, 512], BF16, "ae%d" % i) for i in range(3)])
            tmpr = Ring([AR.alloc([2, 512], F32, "atmp%d" % i) for i in range(2)])
            rr = Ring([AR.alloc([4], F32, "arr%d" % i) for i in range(8)])
            t1 = Ring([AR.alloc([128], F32, "at1_%d" % i) for i in range(4)])
            o_ = Ring([AR.alloc([128], F32, "ao%d" % i) for i in range(8)])
            sqa = AR.alloc([128], F32, "asq")
            ssa = Ring([AR.alloc([4], F32, "assa%d" % i) for i in range(8)])
            ob = Ring([AR.alloc([4, 128], BF16, "aob%d" % i) for i in range(2)])
            accs_r = Ring([AR.alloc([4, 260], F32, "accs%d" % i) for i in range(2)])
            for v in vh.items:
                MS("pool", v.ap[:, :, 128:130], 1.0, [v])
            accb = pbanks[0:4]
            prs = []
            for i in (4, 6):
                t_ = T(pall[:, i * 512:(i + 2) * 512].rearrange("p (c q) -> p c q", c=2), "pair%d" % i)
                t_.b.excl = True
                prs.append(t_)
            scr = Ring(prs)
            heads = {}

            def load_head(h):
                Q = qh.next()
                K = kh.next()
                V = vh.next()
                Bt = bt6.next()
                DMA(Q.ap, q_s[h * 128:(h + 1) * 128, 0:S_], [db("qd", b_) for b_ in range(nqb)], [Q])
                DMA(K.ap, k_s[h * 128:(h + 1) * 128, 0:S_], [db("kd", b_) for b_ in range(nqb)], [K])
                DMA(V.ap[:, :, 0:128], v_s[0:S_, h * 128:(h + 1) * 128].rearrange("(kt p) e -> p kt e", p=128),
                    [db("vd", b_) for b_ in range(nqb)], [V])
                for di in range(6):
                    delta = -128 + 128 * di
                    src = bass.AP(rep_t, h * 128 * RL + 640 - delta, [[RL - 1, 128], [1, 512]])
                    DMA(Bt.ap[:, di, :], src, [db("rep", h)], [Bt])
                heads[h] = (Q, K, V, Bt)

            def emit_scores(h, qb, kt):
                Q, K, V, Bt = heads[h]
                delta = kt * 128 - qb * TB
                pair = scr.next()
                for c in range(2):
                    MM(pair.ap[:, c, :], K.ap[c * 64:(c + 1) * 64, kt * 128:(kt + 1) * 128],
                       Q.ap[c * 64:(c + 1) * 64, qb * TB:(qb + 1) * TB], True, True, [K, Q], [pair])
                e = Er.next()
                if -128 <= delta <= 512:
                    tm = tmpr.next()
                    STT("dve", tm.ap, pair.ap, 0.125, Bt.ap[:, (delta + 128) // 128, :].unsqueeze(1).to_broadcast([128, 2, 512]),
                        ALU.mult, ALU.add, [pair, Bt], [tm])
                    ACT(e.ap, tm.ap, AF.Exp, [tm], [e])
                else:
                    cbias = tab15 if delta < 0 else tab31
                    ACT(e.ap, pair.ap, AF.Exp, [pair, cbias], [e], scale=0.125, bias=cbias.ap[:, h:h + 1])
                return e

            def emit_pv(h, qb, kt, es):
                Q, K, V, Bt = heads[h]
                for jq in range(4):
                    av = accb[jq].ap[:, 0:260].rearrange("p (c e) -> p c e", c=2)
                    for c in range(2):
                        S.mm(av[:, c, :], es.ap[:, c, jq * 128:(jq + 1) * 128], V.ap[:, kt, :],
                             start=(kt == 0 and c == 0), stop=(kt == nkt - 1 and c == 1),
                             reads=bs([es, V]), writes=bs([accb[jq]]), skip_group_check=True)

            def epilogue_stages(h, qb):
                OB = ob.next()
                AS = accs_r.next()
                avs = [AS.ap[:, jq, :].rearrange("p (c e) -> p c e", c=2) for jq in range(4)]
                rs_, oos, sas = [], [], []

                def s0():
                    for jq in range(4):
                        CP("dve", AS.ap[:, jq, :], accb[jq].ap[:, 0:260], [accb[jq]], [AS])

                def s1():
                    for jq in range(4):
                        r = rr.next()
                        S.emit("dve", lambda hh, o=r.ap[:, 0:2], i=avs[jq][:, :, 128]: hh.reciprocal(o, i), bs([AS]), bs([r]))
                        rs_.append(r)
                    for jq in range(4):
                        r = rs_[jq]
                        TT("dve", r.ap[:, 2:3], r.ap[:, 1:2], lamt.ap[:, 1:2], ALU.mult, [r, lamt], [r])

                def s2():
                    for jq in range(4):
                        r = rs_[jq]
                        t_ = t1.next()
                        TS("dve", t_.ap, avs[jq][:, 0, 0:128], r.ap[:, 0:1], None, ALU.mult, None, [AS, r], [t_])
                        oo = o_.next()
                        STT("dve", oo.ap, avs[jq][:, 1, 0:128], r.ap[:, 2:3], t_.ap, ALU.mult, ALU.add, [AS, r, t_], [oo])
                        oos.append(oo)
                        sa = ssa.next()
                        MS("pool", sa.ap, 0.0, [sa])
                        sas.append(sa)

                def s3():
                    for jq in range(4):
                        ACT(sqa.ap, oos[jq].ap, AF.Square, [oos[jq], sas[jq]], [sqa, sas[jq]], accum_out=sas[jq].ap[:, 0:1])
                    for jq in range(4):
                        ACT(sas[jq].ap[:, 1:2], sas[jq].ap[:, 0:1], AF.Ln, [sas[jq]], [sas[jq]], bias=EPS, scale=1.0 / 128.0)
                    for jq in range(4):
                        ACT(sas[jq].ap[:, 2:3], sas[jq].ap[:, 1:2], AF.Exp, [sas[jq]], [sas[jq]], scale=-0.5)

                def s4():
                    for jq in range(4):
                        STT("dve", OB.ap[:, jq, :], oos[jq].ap, sas[jq].ap[:, 2:3], subln.ap, ALU.mult, ALU.mult,
                            [oos[jq], sas[jq], subln], [OB])

                def s5():
                    DMA(atm_s[qb * TB:(qb + 1) * TB, h * 128:(h + 1) * 128].rearrange("(j p) e -> p j e", p=128), OB.ap,
                        [OB], [db("atm", (qb, h))])

                return [s0, s1, None, s2, None, s3, None, s4, None, s5]

            steps = [(h, qb, kt) for h in range(8) for qb in range(nqb) for kt in range(nkt)]
            deferred = []
            load_head(0)
            pend = emit_scores(*steps[0])
            for i, (h, qb, kt) in enumerate(steps):
                if qb == 0 and kt == 0 and h + 1 < 8:
                    load_head(h + 1)
                nxt = emit_scores(*steps[i + 1]) if i + 1 < len(steps) else None
                emit_pv(h, qb, kt, pend)
                for q_ in deferred:
                    if q_:
                        f_ = q_.pop(0)
                        if f_ is not None:
                            f_()
                deferred = [q_ for q_ in deferred if q_]
                if kt == nkt - 1:
                    st_list = epilogue_stages(h, qb)
                    st_list.pop(0)()
                    deferred.append(st_list)
                pend = nxt
            for q_ in deferred:
                for f_ in q_:
                    if f_ is not None:
                        f_()

        t0 = 0
        for si, S_ in enumerate(seqs):
            if lvl >= ORDER.index("KV"):
                phase_KV(si, si * 256)
            if lvl >= ORDER.index("A0"):
                phase_A0(si, t0, S_)
            if lvl >= ORDER.index("B0"):
                phase_B0(si, S_)
            if lvl >= ORDER.index("C0"):
                phase_C0(si, S_)
            if lvl >= ORDER.index("D0"):
                phase_D(0, si, t0, S_)
            if lvl >= ORDER.index("B1"):
                phase_B1(si, S_)
            if lvl >= ORDER.index("D1"):
                phase_D(1, si, t0, S_)
            t0 += S_
        S.finish()
        S.materialize()
        nops = {e: len(S.ops[e]) for e in S.ENGS}
    return nc, nops


def _rel_onehot():
    rel_np = np.arange(640, 640 - RL, -1, dtype=np.int32)
    try:
        import jax
        import jax.numpy as jnp
        with jax.default_device(jax.devices("cpu")[0]):
            rel = jnp.asarray(rel_np)
            nb, max_exact = 16, 8
            ret = jnp.where(rel > 0, nb, 0)
            n = jnp.abs(rel)
            nf = jnp.maximum(n, 1).astype(jnp.float32)
            large = max_exact + (jnp.log(nf / max_exact) / math.log(128 / max_exact) * (nb - max_exact)).astype(jnp.int32)
            large = jnp.minimum(large, nb - 1)
            bucket = np.asarray(ret + jnp.where(n < max_exact, n, large))
    except Exception:
        n = np.abs(rel_np)
        nf = np.maximum(n, 1).astype(np.float32)
        large = 8 + (np.log(nf / np.float32(8)) / np.float32(math.log(16)) * np.float32(8)).astype(np.int32)
        large = np.minimum(large, 15)
        bucket = np.where(rel_np > 0, 16, 0) + np.where(n < 8, n, large)
    oh = np.zeros((32, RL), np.float32)
    oh[bucket, np.arange(RL)] = 1.0
    return oh


_PROG = {}


def _param_maps(inputs):
    m = {}
    for n, shp in PARAMS:
        m[n] = np.ascontiguousarray(np.asarray(inputs[n], dtype=np.float32).reshape(shp))
    m["onehot"] = _rel_onehot()
    return m


def run_cores(seqs, xs, mems, inputs, dbg=False, upto="END"):
    key = (tuple(seqs), dbg, upto)
    if key not in _PROG:
        _PROG[key] = build(list(seqs), dbg=dbg, upto=upto)[0]
    nc = _PROG[key]
    pm = _param_maps(inputs)
    in_maps = []
    for x, mm_ in zip(xs, mems):
        d = dict(pm)
        d["x"] = np.ascontiguousarray(x, dtype=np.float32)
        d["mem"] = np.ascontiguousarray(mm_, dtype=np.float32)
        in_maps.append(d)
    res = run_bass_kernel_spmd(nc, in_maps, core_ids=list(range(len(xs))))
    return res.results


def kernel(**inputs):
    xp = np.asarray(inputs["x_prompt"], dtype=np.float32)
    xs_ = np.asarray(inputs["x_sample"], dtype=np.float32)
    mp = np.asarray(inputs["mem_prompt"], dtype=np.float32)
    ms = np.asarray(inputs["mem_sample"], dtype=np.float32)
    seqs = [4096, 2048, 2048, 2048, 2048]
    xs, mems = [], []
    for c in range(8):
        xs.append(np.concatenate([xp[c], xs_[4 * c:4 * c + 4].reshape(-1, D)], axis=0))
        mems.append(np.concatenate([mp[c], ms[4 * c:4 * c + 4].reshape(-1, D)], axis=0))
    res = run_cores(seqs, xs, mems, inputs)
    y_prompt = np.empty_like(xp)
    y_sample = np.empty_like(xs_)
    for c in range(8):
        y = np.asarray(res[c]["y"], dtype=np.float32)
        y_prompt[c] = y[0:4096]
        y_sample[4 * c:4 * c + 4] = y[4096:].reshape(4, 2048, D)
    return (y_prompt, y_sample)
```

```python
import numpy as np
import concourse.bass as bass
import concourse.mybir as mybir
from concourse.bass_utils import run_bass_kernel_spmd
from contextlib import ExitStack

F32 = mybir.dt.float32
BF16 = mybir.dt.bfloat16
AF = mybir.ActivationFunctionType
ALU = mybir.AluOpType
AX = mybir.AxisListType


class Buf:
    __slots__ = ("w", "rs", "name", "excl")

    def __init__(self, name="", excl=False):
        self.w = None
        self.rs = []
        self.name = name
        self.excl = excl


class Sched:
    ENGS = ("pe", "act", "dve", "pool", "sp")

    def __init__(self, nc, stack, n_dma_sems=40):
        self.nc = nc
        self.ops = {e: [] for e in self.ENGS}
        self.esem = {e: stack.enter_context(nc.semaphore("s_" + e)) for e in self.ENGS}
        self.dsem = [stack.enter_context(nc.semaphore("d%d" % i)) for i in range(n_dma_sems)]
        self.duse = [0] * n_dma_sems
        self.dnext = 0
        self.seen_c = {e: {} for e in self.ENGS}
        self.seen_d = {e: {} for e in self.ENGS}
        self.signal = {e: set() for e in self.ENGS}

    def _need(self, eng, tok, waits, kind):
        if tok is None:
            return
        if tok[0] == "c":
            _, te, idx = tok
            if te == eng and (kind != "raw" or eng == "pe"):
                return
            if self.seen_c[eng].get(te, -1) >= idx:
                return
            self.seen_c[eng][te] = idx
            self.signal[te].add(idx)
            waits[:] = [w for w in waits if not (w[0] == "c" and w[1] == te)]
            waits.append(tok)
        else:
            _, k, val = tok
            if self.seen_d[eng].get(k, 0) >= val:
                return
            self.seen_d[eng][k] = val
            waits[:] = [w for w in waits if not (w[0] == "d" and w[1] == k)]
            waits.append(tok)

    def emit(self, eng, fn, reads=(), writes=(), dma=False):
        waits = []
        if any(b.excl for b in reads):
            writes = list(writes) + [b for b in reads if b.excl]
            reads = [b for b in reads if not b.excl]
        for b in reads:
            for t in (b.w or ()):
                self._need(eng, t, waits, "raw")
        for b in writes:
            for t in (b.w or ()):
                self._need(eng, t, waits, "waw")
            for r in b.rs:
                self._need(eng, r, waits, "war")
        idx = len(self.ops[eng])
        if dma:
            k = self.dnext
            self.dnext = (self.dnext + 1) % len(self.dsem)
            if self.duse[k] > 0:
                self._need(eng, ("d", k, 16 * self.duse[k]), waits, "raw")
            self.duse[k] += 1
            tok = ("d", k, 16 * self.duse[k])
        else:
            tok = ("c", eng, idx)
        self.ops[eng].append((fn, waits, tok))
        for b in reads:
            b.rs.append(tok)
        for b in writes:
            if dma and b.w and not b.rs and all(t[0] == "d" for t in b.w):
                b.w = b.w + [tok]
            else:
                b.w = [tok]
            b.rs = []
        return tok

    def finish(self):
        waits = []
        for k, u in enumerate(self.duse):
            if u > 0 and self.seen_d["sp"].get(k, 0) < 16 * u:
                waits.append(("d", k, 16 * u))
        self.ops["sp"].append((None, waits, None))

    def barrier(self):
        waits = []
        for k, u in enumerate(self.duse):
            if u > 0:
                self._need("sp", ("d", k, 16 * u), waits, "raw")
        for e in ("pe", "act", "dve", "pool"):
            for i in range(len(self.ops[e]) - 1, -1, -1):
                fn, w, tok = self.ops[e][i]
                if fn is not None and tok is not None and tok[0] == "c":
                    self._need("sp", tok, waits, "raw")
                    break
        tok_sp = ("c", "sp", len(self.ops["sp"]))
        self.ops["sp"].append((lambda h: h.nop(), waits, tok_sp))
        for e in ("pe", "act", "dve", "pool"):
            w = []
            self._need(e, tok_sp, w, "raw")
            if w:
                self.ops[e].append((None, w, None))

    def materialize(self):
        nc = self.nc
        cnt = {}
        for e in self.ENGS:
            c = 0
            m = {}
            for i in sorted(self.signal[e]):
                c += 1
                m[i] = c
            cnt[e] = m
        with nc.Block() as block:
            def run(e, h):
                for i, (fn, waits, tok) in enumerate(self.ops[e]):
                    for w in waits:
                        if w[0] == "c":
                            h.wait_ge(self.esem[w[1]], cnt[w[1]][w[2]])
                        else:
                            h.wait_ge(self.dsem[w[1]], w[2])
                    if fn is None:
                        continue
                    ins = fn(h)
                    if tok[0] == "d":
                        ins.then_inc(self.dsem[tok[1]], 16)
                    elif i in cnt[e]:
                        ins.then_inc(self.esem[e], 1)

            @block.tensor
            def _(h):
                run("pe", h)

            @block.scalar
            def _(h):
                run("act", h)

            @block.vector
            def _(h):
                run("dve", h)

            @block.gpsimd
            def _(h):
                run("pool", h)

            @block.sync
            def _(h):
                run("sp", h)

    def mm(self, out, lhsT, rhs, start=True, stop=True, reads=(), writes=(), **kw):
        return self.emit("pe", lambda e: e.matmul(out, lhsT, rhs, start=start, stop=stop, **kw), reads, writes)

    def tr(self, out, in_, ident, reads=(), writes=()):
        return self.emit("pe", lambda e: e.transpose(out, in_, ident), reads, writes)

    def act(self, out, in_, func, reads=(), writes=(), **kw):
        return self.emit("act", lambda e: e.activation(out, in_, func, **kw), reads, writes)

    def dma(self, eng, out, in_, reads=(), writes=(), **kw):
        return self.emit(eng, lambda e: e.dma_start(out=out, in_=in_, **kw), reads, writes, dma=True)

    def tt(self, eng, out, in0, in1, op, reads=(), writes=()):
        return self.emit(eng, lambda e: e.tensor_tensor(out, in0, in1, op), reads, writes)

    def ts(self, eng, out, in0, s1, s2, op0, op1=None, reads=(), writes=(), **kw):
        if op1 is None:
            return self.emit(eng, lambda e: e.tensor_scalar(out, in0, s1, s2, op0, **kw), reads, writes)
        return self.emit(eng, lambda e: e.tensor_scalar(out, in0, s1, s2, op0, op1, **kw), reads, writes)

    def stt(self, eng, out, in0, scalar, in1, op0, op1, reads=(), writes=()):
        return self.emit(eng, lambda e: e.scalar_tensor_tensor(out, in0, scalar, in1, op0, op1), reads, writes)

    def copy(self, eng, out, in_, reads=(), writes=()):
        if eng == "act":
            return self.emit(eng, lambda e: e.activation(out, in_, AF.Copy), reads, writes)
        return self.emit(eng, lambda e: e.tensor_copy(out, in_), reads, writes)

    def memset(self, eng, ap, val, writes=()):
        return self.emit(eng, lambda e: e.memset(ap, val), (), writes)
import math


def _prod(s):
    r = 1
    for v in s:
        r *= v
    return r


class T:
    __slots__ = ("ap", "b")

    def __init__(self, ap, name=""):
        self.ap = ap
        self.b = Buf(name)

    def __getitem__(self, k):
        return self.ap[k]


class Arena:
    def __init__(self, tensor, width):
        self.t = tensor
        self.W = width
        self.p = 0

    def reset(self, p=0):
        self.p = p

    def alloc(self, free_shape, dtype, name=""):
        n = _prod(free_shape)
        words = n if dtype == F32 else (n + 1) // 2
        words = (words + 7) // 8 * 8
        assert self.p + words <= self.W, ("arena overflow", name, self.p, words, self.W)
        ap = self.t[:, self.p:self.p + words]
        self.p += words
        if dtype != F32:
            ap = ap.bitcast(dtype)
        ap = ap[:, 0:n]
        if len(free_shape) == 2:
            ap = ap.rearrange("p (a b) -> p a b", a=free_shape[0])
        elif len(free_shape) == 3:
            ap = ap.rearrange("p (a b c) -> p a b c", a=free_shape[0], b=free_shape[1])
        return T(ap, name)


class Ring:
    def __init__(self, items):
        self.items = items
        self.i = 0

    def next(self):
        it = self.items[self.i % len(self.items)]
        self.i += 1
        return it
D = 1024
KT = 8
TB = 512
EPS = 1e-6
LAM_INIT1 = 0.8 - 0.6 * math.exp(-0.3 * 1)
NEG = -30000.0
RL = 1280

PARAMS = [
    ("rel_bias_table", [32, 8]), ("norm_pre_mix", [2, 1024]), ("norm_post_mix", [2, 1024]),
    ("norm_pre_mlp", [2, 1024]), ("norm_post_mlp", [2, 1024]), ("ssd_w_in", [1024, 7232]),
    ("ssd_conv_w", [5, 4096]), ("ssd_conv_b", [1, 4096]), ("ssd_dt_bias", [1, 64]),
    ("ssd_a_log", [1, 64]), ("ssd_d", [1, 32]), ("ssd_norm", [1, 2048]), ("ssd_w_out", [3072, 1024]),
    ("diff_w_in", [1024, 4096]), ("diff_lambda", [1, 256]), ("diff_subln", [1, 128]),
    ("diff_w_out", [2048, 1024]), ("x_mem_norm", [2, 1024]), ("x_w_kv", [2048, 2048]),
    ("mlp_w1", [2048, 4096]), ("mlp_w2", [8192, 1024]),
]


def build(seqs, dbg=False, upto="END"):
    nc = bass.Bass("TRN2", target_bir_lowering=False)
    NS = len(seqs)
    NT = sum(seqs)
    SM = max(seqs)
    ORDER = ["W", "KV", "A0", "B0", "C0", "D0", "B1", "D1", "END"]
    lvl = ORDER.index(upto)

    def din(name, shape):
        return nc.dram_tensor(name, shape, F32, kind="ExternalInput").ap()

    x_in = din("x", [NT, D])
    mem_in = din("mem", [NS * 256, D])
    P = {n: din(n, s) for n, s in PARAMS}
    oh_in = din("onehot", [32, RL])
    y_out = nc.dram_tensor("y", [NT, D], F32, kind="ExternalOutput").ap()
    skind = "ExternalOutput" if dbg else "Internal"

    def dscr(name, shape, dt, k=None):
        return nc.dram_tensor(name, shape, dt, kind=(k or skind)).ap()

    WB = {
        "in0": dscr("wb_in0", [1024, 7232], BF16, "Internal"), "out0": dscr("wb_out0", [3072, 1024], BF16, "Internal"),
        "in1": dscr("wb_in1", [1024, 4096], BF16, "Internal"), "out1": dscr("wb_out1", [2048, 1024], BF16, "Internal"),
        "kv": dscr("wb_kv", [2048, 2048], BF16, "Internal"), "w1": dscr("wb_w1", [2048, 4096], BF16, "Internal"),
        "w2": dscr("wb_w2", [8192, 1024], BF16, "Internal"),
    }
    WSRC = {"in0": "ssd_w_in", "out0": "ssd_w_out", "in1": "diff_w_in", "out1": "diff_w_out",
            "kv": "x_w_kv", "w1": "mlp_w1", "w2": "mlp_w2"}
    z_s = dscr("z_s", [SM, 2048], BF16)
    xbc_s = dscr("xbc_s", [4096, SM], BF16)
    dt_s = dscr("dt_s", [SM, 64], F32)
    mo_s = [dscr("mo_s%d" % l, [1024, SM], BF16) for l in range(2)]
    ct_s = dscr("ct_s", [1024, SM], BF16)
    bt_s = dscr("bt_s", [1024, SM], BF16)
    btm_s = dscr("btm_s", [SM, 1024], BF16)
    xtm_s = dscr("xtm_s", [SM, 2048], BF16)
    yf_s = dscr("yf_s", [SM, 2048], F32)
    mix_s = dscr("mix_s", [2048, SM], BF16)
    atm_s = dscr("atm_s", [SM, 1024], BF16)
    x1_s = dscr("x1_s", [SM, 1024], F32)
    q_s = dscr("q_s", [1024, SM], BF16)
    k_s = dscr("k_s", [1024, SM], BF16)
    v_s = dscr("v_s", [SM, 1024], BF16)
    vecd = dscr("vecd", [8, RL], F32)
    rep_t = nc.dram_tensor("rep", [8 * 128, RL], F32, kind=skind)
    rep = rep_t.ap()

    DBUF = {}

    def db(name, blk=0):
        k = (name, blk)
        if k not in DBUF:
            DBUF[k] = Buf(name)
        return DBUF[k]

    with ExitStack() as st:
        S = Sched(nc, st, n_dma_sems=48)
        AW = 51700
        arena_t = st.enter_context(nc.sbuf_tensor("arena", [128, AW], F32))
        AR = Arena(arena_t, AW)
        pbanks = []
        pall = st.enter_context(nc.psum_tensor("pall", [128, 4096], F32))
        for i in range(8):
            pbanks.append(T(pall[:, i * 512:(i + 1) * 512], "pb%d" % i))
            pbanks[-1].b.excl = True
        PR = Ring(pbanks[0:6])
        trs = []
        for i in (6, 7):
            t_ = T(pbanks[i].ap.bitcast(BF16)[:, 0:512], "tr%d" % i)
            t_.b = pbanks[i].b
            trs.append(t_)
        TRR = Ring(trs)

        ident_f = AR.alloc([128], F32, "ident_f")
        ident = AR.alloc([128], BF16, "ident")
        ones_b = AR.alloc([128], BF16, "ones_b")
        ones_f = AR.alloc([128], F32, "ones_f")
        tri_f = AR.alloc([128], F32, "tri_f")
        triu_f = AR.alloc([128], F32, "triu_f")
        tris_f = AR.alloc([128], F32, "tris_f")
        trisl_f = AR.alloc([128], F32, "trisl_f")
        maskf = AR.alloc([4, 128], F32, "maskf")
        maskb = AR.alloc([4, 128], F32, "maskb")
        maskf_b = AR.alloc([4, 128], BF16, "maskf_b")
        maskb_b = AR.alloc([4, 128], BF16, "maskb_b")
        ssdn = AR.alloc([2048], F32, "ssdn")
        dtb = AR.alloc([64], F32, "dtb")
        avec = AR.alloc([64], F32, "avec")
        dsk = AR.alloc([32], F32, "dsk")
        subln = AR.alloc([128], F32, "subln")
        lamt = AR.alloc([8], F32, "lamt")
        tab15 = AR.alloc([8], F32, "tab15")
        tab31 = AR.alloc([8], F32, "tab31")
        cw = AR.alloc([32, 5], F32, "cw")
        cb = AR.alloc([32], F32, "cb")
        kmT = [AR.alloc([8, 256], BF16, "kmT%d" % l) for l in range(2)]
        vm = [AR.alloc([2, 1024], BF16, "vm%d" % l) for l in range(2)]
        PERSIST = AR.p

        def A_(e, o, i, f, r=(), w=(), **kw):
            return S.emit("act", lambda h: h.activation(o, i, f, **kw), [t.b for t in r], [t.b for t in w])

        def bs(ts):
            return [t if isinstance(t, Buf) else t.b for t in ts]

        def MM(out, lhsT, rhs, start, stop, r, w):
            return S.mm(out, lhsT, rhs, start=start, stop=stop, reads=bs(r), writes=bs(w))

        def ACT(o, i, f, r, w, **kw):
            return S.emit("act", lambda h: h.activation(o, i, f, **kw), bs(r), bs(w))

        def TT(e, o, a, b, op, r, w):
            return S.tt(e, o, a, b, op, reads=bs(r), writes=bs(w))

        def TS(e, o, a, s1, s2, op0, op1, r, w):
            return S.ts(e, o, a, s1, s2, op0, op1, reads=bs(r), writes=bs(w))

        def STT(e, o, a, sc, b, op0, op1, r, w):
            return S.stt(e, o, a, sc, b, op0, op1, reads=bs(r), writes=bs(w))

        def CP(e, o, i, r, w):
            return S.copy(e, o, i, reads=bs(r), writes=bs(w))

        def DMA(o, i, r, w, eng="sp"):
            return S.dma(eng, o, i, reads=bs(r), writes=bs(w))

        def MS(e, ap, val, w):
            return S.memset(e, ap, val, writes=bs(w))

        def AFS(t, pattern, op, fill, base, cm):
            S.emit("pool", lambda h: h.affine_select(t.ap, t.ap, pattern, op, fill, base=base, channel_multiplier=cm),
                   bs([t]), bs([t]))

        MS("pool", ident_f.ap, 1.0, [ident_f])
        AFS(ident_f, [[-1, 128]], ALU.is_equal, 0.0, 0, 1)
        CP("dve", ident.ap, ident_f.ap, [ident_f], [ident])
        MS("pool", ones_f.ap, 1.0, [ones_f])
        MS("pool", ones_b.ap, 1.0, [ones_b])
        MS("pool", tri_f.ap, 1.0, [tri_f])
        AFS(tri_f, [[1, 128]], ALU.is_ge, 0.0, 0, -1)
        MS("pool", triu_f.ap, 1.0, [triu_f])
        AFS(triu_f, [[-1, 128]], ALU.is_ge, 0.0, 0, 1)
        MS("pool", tris_f.ap, 1.0, [tris_f])
        AFS(tris_f, [[-1, 128]], ALU.is_gt, 0.0, 0, 1)
        MS("pool", trisl_f.ap, 1.0, [trisl_f])
        AFS(trisl_f, [[1, 128]], ALU.is_gt, 0.0, 0, -1)
        MS("pool", maskf.ap, 0.0, [maskf])
        S.emit("pool", lambda h: h.affine_select(maskf.ap, maskf.ap, [[0, 4], [1, 128]], ALU.is_ge, NEG, base=0,
                                                 channel_multiplier=-1), bs([maskf]), bs([maskf]))
        MS("pool", maskb.ap, 0.0, [maskb])
        S.emit("pool", lambda h: h.affine_select(maskb.ap, maskb.ap, [[0, 4], [-1, 128]], ALU.is_ge, NEG, base=0,
                                                 channel_multiplier=1), bs([maskb]), bs([maskb]))
        CP("dve", maskf_b.ap, maskf.ap, [maskf], [maskf_b])
        CP("dve", maskb_b.ap, maskb.ap, [maskb], [maskb_b])

        def load_gain(nm, l):
            g = AR.alloc([1024], F32, "g_" + nm)
            DMA(g.ap, P[nm][l:l + 1, :].partition_broadcast(128), [], [g])
            return g
        DMA(ssdn.ap, P["ssd_norm"][0:1, :].partition_broadcast(128), [], [ssdn])
        DMA(dtb.ap, P["ssd_dt_bias"][0:1, :].partition_broadcast(128), [], [dtb])
        DMA(avec.ap, P["ssd_a_log"][0:1, :].partition_broadcast(128), [], [avec])
        DMA(dsk.ap, P["ssd_d"][0:1, :].partition_broadcast(128), [], [dsk])
        DMA(subln.ap, P["diff_subln"][0:1, :].partition_broadcast(128), [], [subln])
        DMA(tab15.ap, P["rel_bias_table"][15:16, :].partition_broadcast(128), [], [tab15])
        DMA(tab31.ap, P["rel_bias_table"][31:32, :].partition_broadcast(128), [], [tab31])
        ACT(avec.ap, avec.ap, AF.Exp, [avec], [avec])
        TS("dve", avec.ap, avec.ap, -1.0, None, ALU.mult, None, [avec], [avec])
        TS("dve", subln.ap, subln.ap, 1.0 - LAM_INIT1, None, ALU.mult, None, [subln], [subln])
        AR.reset(PERSIST)
        lp = AR.alloc([4, 64], F32, "lp")
        lpp = AR.alloc([2, 64], F32, "lpp")
        lps = AR.alloc([2], F32, "lps")
        DMA(lp.ap.rearrange("p a b -> p (a b)"), P["diff_lambda"][0:1, :].partition_broadcast(128), [], [lp])
        lp4 = lp.ap.rearrange("p (a c) b -> p a c b", c=2)
        TT("dve", lpp.ap, lp4[:, :, 0, :], lp4[:, :, 1, :], ALU.mult, [lp], [lpp])
        S.emit("dve", lambda h: h.tensor_reduce(lps.ap, lpp.ap, AX.X, ALU.add), bs([lpp]), bs([lps]))
        ACT(lps.ap, lps.ap, AF.Exp, [lps], [lps])
        TT("dve", lamt.ap[:, 0:1], lps.ap[:, 0:1], lps.ap[:, 1:2], ALU.subtract, [lps], [lamt])
        TS("dve", lamt.ap[:, 0:1], lamt.ap[:, 0:1], LAM_INIT1, None, ALU.add, None, [lamt], [lamt])
        TS("dve", lamt.ap[:, 1:2], lamt.ap[:, 0:1], -1.0, None, ALU.mult, None, [lamt], [lamt])
        cwr = AR.alloc([2, 128], F32, "cwr")
        cbr = AR.alloc([128], F32, "cbr")
        MS("pool", cwr.ap, 0.0, [cwr])
        MS("pool", cbr.ap, 0.0, [cbr])
        cw_rows = P["ssd_conv_w"].rearrange("k (ct p) -> (k ct) p", p=128)
        DMA(cwr.ap[:, 0, :], cw_rows[0:128, :], [], [cwr])
        DMA(cwr.ap[0:32, 1, :], cw_rows[128:160, :], [], [cwr])
        DMA(cbr.ap[0:32, :], P["ssd_conv_b"].rearrange("o (ct p) -> (o ct) p", p=128), [], [cbr])
        pb = PR.next()
        MM(pb.ap[:, 0:128], cwr.ap[:, 0, :], ident_f.ap, True, True, [cwr, ident_f], [pb])
        MM(pb.ap[:, 128:160], cwr.ap[0:32, 1, :], ident_f.ap[0:32, 0:32], True, True, [cwr, ident_f], [pb])
        MM(pb.ap[:, 160:192], cbr.ap[0:32, :], ident_f.ap[0:32, 0:32], True, True, [cbr, ident_f], [pb])
        CP("dve", cw.ap.rearrange("p ct k -> p k ct"), pb.ap[:, 0:160].rearrange("p (k ct) -> p k ct", k=5), [pb], [cw])
        CP("dve", cb.ap, pb.ap[:, 160:192], [pb], [cb])
        tabs = AR.alloc([8], F32, "tabs")
        ohs = AR.alloc([RL], F32, "ohs")
        rv = AR.alloc([RL], F32, "rv")
        DMA(tabs.ap[0:32, :], P["rel_bias_table"][:, :], [], [tabs])
        DMA(ohs.ap[0:32, :], oh_in[:, :], [], [ohs])
        for c0 in range(0, RL, 512):
            n = min(512, RL - c0)
            pb = PR.next()
            MM(pb.ap[0:8, 0:n], tabs.ap[0:32, :], ohs.ap[0:32, c0:c0 + n], True, True, [tabs, ohs], [pb])
            CP("dve", rv.ap[0:8, c0:c0 + n], pb.ap[0:8, 0:n], [pb], [rv])
        DMA(vecd[:, :], rv.ap[0:8, :], [rv], [db("vecd")])
        S.barrier()
        for h in range(8):
            DMA(rep[h * 128:(h + 1) * 128, :], vecd[h:h + 1, :].partition_broadcast(128), [db("vecd")], [db("rep", h)])

        def emit_cast(key, r0):
            DMA(WB[key][r0:r0 + 256, :], P[WSRC[key]][r0:r0 + 256, :], [], [db("w_" + key, r0)], eng="pool")

        for key in ("kv", "in0"):
            for r0 in range(0, P[WSRC[key]].shape[0], 256):
                emit_cast(key, r0)
        pending_casts = []
        for key, lo, hi in (("out0", 0, 3072), ("w1", 0, 1024), ("w2", 0, 4096), ("in1", 0, 1024),
                            ("out1", 0, 2048), ("w1", 1024, 2048), ("w2", 4096, 8192)):
            for r0 in range(lo, hi, 256):
                pending_casts.append((key, r0))

        def drip(n):
            for _ in range(min(n, len(pending_casts))):
                emit_cast(*pending_casts.pop(0))
        S.barrier()

        def new_phase():
            S.barrier()
            AR.reset(PERSIST)

        WR = [None]

        def wload(key, r0, nkt, c0, ncols):
            t = WR[0].next()
            src = WB[key][r0:r0 + nkt * 128, c0:c0 + ncols].rearrange("(kt p) c -> p kt c", p=128)
            deps = [db("w_" + key, r) for r in range(r0 - r0 % 256, r0 + nkt * 128, 256)]
            assert all(d_.w for d_ in deps), ("weight block not cast yet", key, r0)
            DMA(t.ap[:, 0:nkt, 0:ncols], src, deps, [t])
            return t

        STQ = "pool"
        evac_i = [0]

        def evac(out_ap, in_ap, r, w):
            evac_i[0] += 1
            if evac_i[0] % 2 == 0:
                ACT(out_ap, in_ap, AF.Copy, r, w)
            else:
                CP("dve", out_ap, in_ap, r, w)

        def tm_norm(src, g, dst, ss, sq, r_extra=()):
            MS("pool", ss.ap, 0.0, [ss])
            for j in range(4):
                ACT(sq.ap, src.ap[:, j, :], AF.Square, [src, ss] + list(r_extra), [sq, ss], accum_out=ss.ap[:, j:j + 1])
            ACT(ss.ap[:, 4:8], ss.ap[:, 0:4], AF.Ln, [ss], [ss], bias=EPS, scale=1.0 / D)
            ACT(ss.ap[:, 8:12], ss.ap[:, 4:8], AF.Exp, [ss], [ss], scale=-0.5)
            for j in range(4):
                STT("dve", dst.ap[:, j, :], src.ap[:, j, :], ss.ap[:, 8 + j:9 + j], g.ap, ALU.mult, ALU.mult,
                    [src, ss, g], [dst])

        def transpose_block(hb, hT, nkt=8):
            for kt in range(nkt):
                tr = TRR.next()
                for j in range(4):
                    S.tr(tr.ap[:, j * 128:(j + 1) * 128], hb.ap[:, j, kt * 128:(kt + 1) * 128], ident.ap,
                         reads=bs([hb, ident]), writes=bs([tr]))
                CP("dve", hT.ap[:, kt, :], tr.ap, [tr], [hT])

        def cross_attn(l, qT, moT, E, rden):
            for h in range(4):
                es = []
                for mt in range(2):
                    pb = PR.next()
                    for dt_ in range(2):
                        MM(pb.ap, kmT[l].ap[:, 2 * h + dt_, mt * 128:(mt + 1) * 128], qT.ap[:, 2 * h + dt_, :],
                           dt_ == 0, dt_ == 1, [kmT[l], qT], [pb])
                    e = E.next()
                    ACT(e.ap, pb.ap, AF.Exp, [pb], [e], scale=1.0 / 16.0)
                    es.append(e)
                pden = PR.next()
                for mt in range(2):
                    MM(pden.ap, ones_b.ap, es[mt].ap, mt == 0, mt == 1, [ones_b, es[mt]], [pden])
                S.emit("dve", lambda hh, o=rden.ap, i=pden.ap: hh.reciprocal(o, i), bs([pden]), bs([rden]))
                for dt_ in range(2):
                    pn = PR.next()
                    for mt in range(2):
                        MM(pn.ap, vm[l].ap[:, mt, (2 * h + dt_) * 128:(2 * h + dt_ + 1) * 128], es[mt].ap,
                           mt == 0, mt == 1, [vm[l], es[mt]], [pn])
                    TT("dve", moT.ap[:, 2 * h + dt_, :], pn.ap, rden.ap, ALU.mult, [pn, rden], [moT])

        def phase_KV(si, m0):
            new_phase()
            WR[0] = Ring([AR.alloc([8, 512], BF16, "w%d" % i) for i in range(3)])
            mt_ = AR.alloc([4, 1024], F32, "memt")
            ss = AR.alloc([12], F32, "ss")
            sq = AR.alloc([1024], F32, "sq")
            hb = AR.alloc([4, 1024], BF16, "hb")
            hT = AR.alloc([8, 512], BF16, "hT")
            gm = [load_gain("x_mem_norm", l) for l in range(2)]
            MS("pool", mt_.ap[:, 2:4, :], 0.0, [mt_])
            DMA(mt_.ap[:, 0:2, :], mem_in[m0:m0 + 256, :].rearrange("(j p) d -> p j d", p=128), [], [mt_])
            for l in range(2):
                tm_norm(mt_, gm[l], hb, ss, sq)
                transpose_block(hb, hT)
                for c in range(4):
                    w = wload("kv", l * 1024, 8, c * 512, 512)
                    if c < 2:
                        for i in range(4):
                            pb = PR.next()
                            for kt in range(8):
                                MM(pb.ap[:, 0:256], w.ap[:, kt, i * 128:(i + 1) * 128], hT.ap[:, kt, 0:256],
                                   kt == 0, kt == 7, [w, hT], [pb])
                            evac(kmT[l].ap[:, c * 4 + i, :], pb.ap[:, 0:256], [pb], [kmT[l]])
                    else:
                        for mt in range(2):
                            pb = PR.next()
                            for kt in range(8):
                                MM(pb.ap, hT.ap[:, kt, mt * 128:(mt + 1) * 128], w.ap[:, kt, :],
                                   kt == 0, kt == 7, [w, hT], [pb])
                            evac(vm[l].ap[:, mt, (c - 2) * 512:(c - 1) * 512], pb.ap, [pb], [vm[l]])

        def alloc_front(host=None, with_dt=True):
            d = {}
            d["xt"] = AR.alloc([4, 1024], F32, "xt")
            d["ss"] = AR.alloc([12], F32, "ss")
            d["sq"] = AR.alloc([1024], F32, "sq")
            d["hb"] = AR.alloc([4, 1024], BF16, "hb")
            d["hT"] = AR.alloc([8, 512], BF16, "hT")
            d["E"] = Ring([AR.alloc([512], BF16, "E%d" % i) for i in range(4)])
            d["rden"] = AR.alloc([512], F32, "rden")
            if host is None:
                d["qT"] = AR.alloc([8, 512], BF16, "qT")
                d["moT"] = AR.alloc([8, 512], BF16, "moT")
                d["fst"] = Ring([AR.alloc([4, 512], BF16, "fst%d" % i) for i in range(2)])
            else:
                sub = Arena(arena_t, AW)
                sub.reset(host[0])
                d["qT"] = sub.alloc([8, 512], BF16, "qT")
                d["moT"] = sub.alloc([8, 512], BF16, "moT")
                f0 = sub.alloc([4, 512], BF16, "fst0")
                f1 = sub.alloc([4, 512], BF16, "fst1")
                assert sub.p <= host[0] + host[1]
                for t in (d["qT"], d["moT"], f0, f1):
                    t.b = host[2]
                d["fst"] = Ring([f0, f1])
            if with_dt:
                d["dtt"] = AR.alloc([4, 4, 64], F32, "dtt")
            return d

        def front(l, t0, S_, tb, d, stage="ab"):
            xt, hb, hT, qT, moT = d["xt"], d["hb"], d["hT"], d["qT"], d["moT"]
            c0t = tb * TB
            if "a" in stage:
                tm_norm(xt, d["g_pre_mix%d" % l], hb, d["ss"], d["sq"])
            if "b" not in stage:
                return
            transpose_block(hb, hT)
            if l == 0:
                chunks = [("z", 512 * c, 512, c) for c in range(4)] + [("xbc", 2048 + 512 * c, 512, c) for c in range(8)] \
                    + [("dt", 6144, 64, 0)] + [("q", 6208 + 512 * c, 512, c) for c in range(2)]
                key = "in0"
            else:
                chunks = [("qd", 512 * c, 512, c) for c in range(2)] + [("kd", 1024 + 512 * c, 512, c) for c in range(2)] \
                    + [("vd", 2048 + 512 * c, 512, c) for c in range(2)] + [("q", 3072 + 512 * c, 512, c) for c in range(2)]
                key = "in1"
            for kind, col0, ncols, c in chunks:
                w = wload(key, 0, 8, col0, ncols)
                if kind in ("xbc", "q", "qd", "kd"):
                    stg = None if kind == "q" else d["fst"].next()
                    for i in range(4):
                        pb = PR.next()
                        for kt in range(8):
                            MM(pb.ap, w.ap[:, kt, i * 128:(i + 1) * 128], hT.ap[:, kt, :], kt == 0, kt == 7, [w, hT], [pb])
                        if kind == "q":
                            evac(qT.ap[:, c * 4 + i, :], pb.ap, [pb], [qT])
                        else:
                            evac(stg.ap[:, i, :], pb.ap, [pb], [stg])
                    if kind != "q":
                        dst = {"xbc": xbc_s, "qd": q_s, "kd": k_s}[kind]
                        DMA(dst[c * 512:(c + 1) * 512, c0t:c0t + TB].rearrange("(i p) t -> p i t", p=128), stg.ap,
                            [stg], [db(kind, tb)], eng=STQ)
                elif kind in ("z", "vd"):
                    stg = d["fst"].next()
                    for j in range(4):
                        pb = PR.next()
                        for kt in range(8):
                            MM(pb.ap, hT.ap[:, kt, j * 128:(j + 1) * 128], w.ap[:, kt, :], kt == 0, kt == 7, [w, hT], [pb])
                        if kind == "z":
                            ACT(stg.ap[:, j, :], pb.ap, AF.Silu, [pb], [stg])
                        else:
                            evac(stg.ap[:, j, :], pb.ap, [pb], [stg])
                    dst = z_s if kind == "z" else v_s
                    DMA(dst[c0t:c0t + TB, c * 512:(c + 1) * 512].rearrange("(j p) c -> p j c", p=128), stg.ap,
                        [stg], [db(kind, tb)], eng=STQ)
                else:
                    dtt = d["dtt"]
                    pb = PR.next()
                    for j in range(4):
                        for kt in range(8):
                            MM(pb.ap[:, j * 64:(j + 1) * 64], hT.ap[:, kt, j * 128:(j + 1) * 128], w.ap[:, kt, 0:64],
                               kt == 0, kt == 7, [w, hT], [pb])
                    pv = pb.ap[:, 0:256].rearrange("p (j c) -> p j c", j=4)
                    dtb_b = dtb.ap.unsqueeze(1).to_broadcast([128, 4, 64])
                    TT("dve", dtt.ap[:, 0], pv, dtb_b, ALU.add, [pb, dtb], [dtt])
                    STT("dve", dtt.ap[:, 1], dtt.ap[:, 0], -1.0, dtt.ap[:, 0], ALU.mult, ALU.max, [dtt], [dtt])
                    ACT(dtt.ap[:, 2], dtt.ap[:, 1], AF.Exp, [dtt], [dtt], scale=-1.0)
                    ACT(dtt.ap[:, 3], dtt.ap[:, 2], AF.Ln, [dtt], [dtt], bias=1.0)
                    STT("dve", dtt.ap[:, 1], dtt.ap[:, 0], 0.0, dtt.ap[:, 3], ALU.max, ALU.add, [dtt], [dtt])
                    DMA(dt_s[c0t:c0t + TB, :].rearrange("(j p) c -> p j c", p=128), dtt.ap[:, 1], [dtt], [db("dt", tb)], eng=STQ)
            cross_attn(l, qT, moT, d["E"], d["rden"])
            DMA(mo_s[l][:, c0t:c0t + TB].rearrange("(i p) t -> p i t", p=128), moT.ap, [moT], [db("mo%d" % l, tb)], eng=STQ)

        def phase_A0(si, t0, S_):
            new_phase()
            WR[0] = Ring([AR.alloc([8, 512], BF16, "w%d" % i) for i in range(4)])
            d = alloc_front()
            d["g_pre_mix0"] = load_gain("norm_pre_mix", 0)
            for tb in range(S_ // TB):
                drip(8)
                DMA(d["xt"].ap, x_in[t0 + tb * TB:t0 + (tb + 1) * TB, :].rearrange("(j p) d -> p j d", p=128), [], [d["xt"]])
                front(0, t0, S_, tb, d)
            drip(len(pending_casts))

        def phase_B0(si, S_):
            new_phase()
            xin = Ring([AR.alloc([32, 516], BF16, "xin%d" % i) for i in range(1)])
            dgall = AR.alloc([160, 128], BF16, "dgall")
            for ct in range(32):
                for k in range(5):
                    TS("dve", dgall.ap[:, ct * 5 + k, :], ident_f.ap, cw.ap[:, ct, k:k + 1], None,
                       ALU.mult, None, [ident_f, cw], [dgall])
            pc = Ring([AR.alloc([32, 512], BF16, "pc%d" % i) for i in range(1)])
            xtm = Ring([AR.alloc([4, 2048], BF16, "xtm%d" % i) for i in range(2)])
            btm = Ring([AR.alloc([4, 1024], BF16, "btm%d" % i) for i in range(2)])
            nb = S_ // TB
            for tb in range(nb):
                xi = xin.next()
                lo = max(0, tb * TB - 2)
                hi = min(S_, tb * TB + TB + 2)
                o0 = lo - (tb * TB - 2)
                if tb == 0:
                    MS("pool", xi.ap[:, :, 0:2], 0.0, [xi])
                if tb == nb - 1:
                    MS("pool", xi.ap[:, :, 514:516], 0.0, [xi])
                rds = [db("xbc", b) for b in (tb - 1, tb, tb + 1) if 0 <= b < nb]
                for q4 in range(4):
                    DMA(xi.ap[:, q4 * 8:(q4 + 1) * 8, o0:o0 + (hi - lo)],
                        xbc_s[q4 * 1024:(q4 + 1) * 1024, lo:hi].rearrange("(ct p) t -> p ct t", p=128), rds, [xi])
                po = pc.next()
                for ct in range(32):
                    pb = PR.next()
                    for k in range(5):
                        MM(pb.ap, dgall.ap[:, ct * 5 + k, :], xi.ap[:, ct, k:k + 512], k == 0, k == 4, [dgall, xi], [pb])
                    ACT(po.ap[:, ct, :], pb.ap, AF.Silu, [pb, cb], [po], bias=cb.ap[:, ct:ct + 1])
                c0t = tb * TB
                DMA(bt_s[:, c0t:c0t + TB].rearrange("(i p) t -> p i t", p=128), po.ap[:, 16:24, :], [po], [db("bt", tb)])
                DMA(ct_s[:, c0t:c0t + TB].rearrange("(i p) t -> p i t", p=128), po.ap[:, 24:32, :], [po], [db("ct", tb)])
                xt_ = xtm.next()
                bt_ = btm.next()
                for j in range(4):
                    for ct in range(24):
                        if ct % 4 == 0:
                            tr = TRR.next()
                        S.tr(tr.ap[:, (ct % 4) * 128:(ct % 4 + 1) * 128], po.ap[:, ct, j * 128:(j + 1) * 128], ident.ap,
                             reads=bs([po, ident]), writes=bs([tr]))
                        if ct % 4 == 3:
                            c4 = ct // 4
                            if c4 < 4:
                                CP("dve", xt_.ap[:, j, c4 * 512:(c4 + 1) * 512], tr.ap, [tr], [xt_])
                            else:
                                CP("dve", bt_.ap[:, j, (c4 - 4) * 512:(c4 - 3) * 512], tr.ap, [tr], [bt_])
                DMA(xtm_s[c0t:c0t + TB, :].rearrange("(j p) c -> p j c", p=128), xt_.ap, [xt_], [db("xtm", tb)])
                DMA(btm_s[c0t:c0t + TB, :].rearrange("(j p) c -> p j c", p=128), bt_.ap, [bt_], [db("btm", tb)])

        def phase_C0(si, S_):
            new_phase()
            nb = S_ // TB
            xtm = Ring([AR.alloc([2048], BF16, "sxtm%d" % i) for i in range(2)])
            btm = Ring([AR.alloc([1024], BF16, "sbtm%d" % i) for i in range(2)])
            btf = Ring([AR.alloc([8, 512], BF16, "sbt%d" % i) for i in range(2)])
            ctf = Ring([AR.alloc([8, 512], BF16, "sct%d" % i) for i in range(2)])
            dtr = Ring([AR.alloc([4, 64], F32, "sdt%d" % i) for i in range(2)])
            zr = Ring([AR.alloc([2048], BF16, "sz%d" % i) for i in range(2)])
            yfl = Ring([AR.alloc([2048], F32, "syf%d" % i) for i in range(2)])
            xdsr = Ring([AR.alloc([2048], BF16, "xds%d" % i) for i in range(2)])
            ldtr = Ring([AR.alloc([64], F32, "ldt%d" % i) for i in range(2)])
            state = AR.alloc([2048], F32, "state")
            stateb = AR.alloc([2048], BF16, "stateb")
            dar = Ring([AR.alloc([32], F32, "da%d" % i) for i in range(2)])
            xdte = Ring([AR.alloc([2048], BF16, "xdte%d" % i) for i in range(2)])
            ex3 = Ring([AR.alloc([96], F32, "ex3_%d" % i) for i in range(2)])
            ncum = Ring([AR.alloc([32], F32, "ncum%d" % i) for i in range(2)])
            gt = Ring([AR.alloc([128], BF16, "gt%d" % i) for i in range(3)])
            exs = Ring([AR.alloc([4, 128], BF16, "exs%d" % i) for i in range(3)])
            wt = Ring([AR.alloc([4, 128], BF16, "wt%d" % i) for i in range(3)])
            yc = Ring([AR.alloc([2048], F32, "yc%d" % i) for i in range(2)])
            ytmp = Ring([AR.alloc([256], F32, "ytmp%d" % i) for i in range(3)])
            ssgr = Ring([AR.alloc([24], F32, "ssg%d" % i) for i in range(2)])
            sq = AR.alloc([2048], BF16, "sq2")
            ybr = Ring([AR.alloc([2048], BF16, "yb%d" % i) for i in range(2)])
            mixT = Ring([AR.alloc([16, 128], BF16, "mixT%d" % i) for i in range(2)])

            class Ctx:
                pass

            blk = {}

            def prologue(c):
                dirn, tb, j = c.dirn, c.tb, c.j
                c0t = tb * TB
                key = (dirn, tb)
                if key not in blk:
                    Bt = btf.next()
                    Ct = ctf.next()
                    Dt = dtr.next()
                    DMA(Bt.ap, bt_s[:, c0t:c0t + TB].rearrange("(i p) t -> p i t", p=128), [db("bt", tb)], [Bt])
                    DMA(Ct.ap, ct_s[:, c0t:c0t + TB].rearrange("(i p) t -> p i t", p=128), [db("ct", tb)], [Ct])
                    DMA(Dt.ap, dt_s[c0t:c0t + TB, :].rearrange("(j p) c -> p j c", p=128), [db("dt", tb)], [Dt])
                    blk.clear()
                    blk[key] = (Bt, Ct, Dt)
                c.Bt, c.Ct, c.Dt = blk[key]
                c.r0 = c0t + j * 128
                c.ck = tb * 4 + j
                c.X = xtm.next()
                c.Bm = btm.next()
                DMA(c.X.ap, xtm_s[c.r0:c.r0 + 128, :], [db("xtm", tb)], [c.X])
                DMA(c.Bm.ap, btm_s[c.r0:c.r0 + 128, :], [db("btm", tb)], [c.Bm])
                if dirn == 1:
                    c.Z = zr.next()
                    DMA(c.Z.ap, z_s[c.r0:c.r0 + 128, :], [db("z", tb)], [c.Z])
                c.tri_c = tri_f if dirn == 0 else triu_f
                c.tri_e = tris_f if dirn == 0 else trisl_f
                c.mask = maskf_b if dirn == 0 else maskb_b
                dtj = c.Dt.ap[:, j, dirn * 32:(dirn + 1) * 32]
                c.da = dar.next()
                TT("dve", c.da.ap, dtj, avec.ap[:, dirn * 32:(dirn + 1) * 32], ALU.mult, [c.Dt, avec], [c.da])
                pm = PR.next()
                MM(pm.ap[:, 0:32], c.tri_c.ap, c.da.ap, True, True, [c.tri_c, c.da], [pm])
                MM(pm.ap[:, 32:64], c.tri_e.ap, c.da.ap, True, True, [c.tri_e, c.da], [pm])
                MM(pm.ap[:, 64:96], ones_f.ap, c.da.ap, True, True, [ones_f, c.da], [pm])
                c.e3 = ex3.next()
                ACT(c.e3.ap, pm.ap[:, 0:96], AF.Exp, [pm], [c.e3])
                ld = ldtr.next()
                ACT(ld.ap[:, 0:32], dtj, AF.Ln, [c.Dt], [ld])
                c.nc_ = ncum.next()
                STT("dve", c.nc_.ap, pm.ap[:, 0:32], -1.0, ld.ap[:, 0:32], ALU.mult, ALU.add, [pm, ld], [c.nc_])
                TT("dve", ld.ap[:, 32:64], dtj, c.e3.ap[:, 32:64], ALU.mult, [c.Dt, c.e3], [ld])
                c.xe = xdte.next()
                TT("dve", c.xe.ap.rearrange("p (h c) -> p h c", h=32), c.X.ap.rearrange("p (h c) -> p h c", h=32),
                   ld.ap[:, 32:64].unsqueeze(2).to_broadcast([128, 32, 64]), ALU.mult, [c.X, ld], [c.xe])
                if dirn == 0:
                    c.xds = xdsr.next()
                    TT("pool", c.xds.ap.rearrange("p (h c) -> p h c", h=32), c.X.ap.rearrange("p (h c) -> p h c", h=32),
                       dsk.ap.unsqueeze(2).to_broadcast([128, 32, 64]), ALU.mult, [c.X, dsk], [c.xds])
                else:
                    c.yf = yfl.next()
                    DMA(c.yf.ap, yf_s[c.r0:c.r0 + 128, :], [db("yf", c.ck)], [c.yf])
                c.Y = yc.next()
                c.pupd = None

            def front_g(c, g):
                j = c.j
                pg = PR.next()
                MM(pg.ap[:, 0:128], c.Bt.ap[:, g, j * 128:(j + 1) * 128], c.Ct.ap[:, g, j * 128:(j + 1) * 128],
                   True, True, [c.Bt, c.Ct], [pg])
                G = gt.next()
                ACT(G.ap, pg.ap[:, 0:128], AF.Copy, [pg], [G])
                psg = PR.next()
                for hh in range(4):
                    h_ = g * 4 + hh
                    S.mm(psg.ap[:, hh * 128:(hh + 1) * 128], c.da.ap[:, h_:h_ + 1].to_broadcast([128, 128]), c.tri_c.ap,
                         start=(hh == 0), stop=False, reads=bs([c.da, c.tri_c]), writes=bs([psg]), skip_group_check=True)
                S.mm(psg.ap, ident.ap, c.mask.ap.rearrange("p h t -> p (h t)"), start=False, stop=True,
                     reads=bs([ident, c.mask]), writes=bs([psg]), skip_group_check=True)
                ex = exs.next()
                for hh in range(4):
                    ACT(ex.ap[:, hh, :], psg.ap[:, hh * 128:(hh + 1) * 128], AF.Exp, [psg, c.nc_], [ex],
                        bias=c.nc_.ap[:, g * 4 + hh:g * 4 + hh + 1])
                W = wt.next()
                TT("dve", W.ap, ex.ap, G.ap.unsqueeze(1).to_broadcast([128, 4, 128]), ALU.mult, [ex, G], [W])
                return W

            def back_g(c, g, W):
                j = c.j
                py = PR.next()
                for hh in range(4):
                    h_ = g * 4 + hh
                    if c.dirn == 0:
                        S.mm(py.ap[:, hh * 64:(hh + 1) * 64], W.ap[:, hh, :], c.X.ap[:, h_ * 64:(h_ + 1) * 64],
                             start=True, stop=False, reads=bs([W, c.X]), writes=bs([py]), skip_group_check=True)
                        S.mm(py.ap[:, hh * 64:(hh + 1) * 64], ident.ap, c.xds.ap[:, h_ * 64:(h_ + 1) * 64],
                             start=False, stop=True, reads=bs([ident, c.xds]), writes=bs([py]), skip_group_check=True)
                    else:
                        MM(py.ap[:, hh * 64:(hh + 1) * 64], W.ap[:, hh, :], c.X.ap[:, h_ * 64:(h_ + 1) * 64],
                           True, True, [W, c.X], [py])
                MM(py.ap[:, 256:512], c.Ct.ap[:, g, j * 128:(j + 1) * 128], stateb.ap[:, g * 256:(g + 1) * 256],
                   True, True, [c.Ct, stateb], [py])
                yt = ytmp.next()
                TT("dve", yt.ap.rearrange("p (h c) -> p h c", h=4), py.ap[:, 256:512].rearrange("p (h c) -> p h c", h=4),
                   c.e3.ap[:, g * 4:(g + 1) * 4].unsqueeze(2).to_broadcast([128, 4, 64]), ALU.mult, [py, c.e3], [yt])
                TT("dve", c.Y.ap[:, g * 256:(g + 1) * 256], yt.ap, py.ap[:, 0:256], ALU.add, [yt, py], [c.Y])
                if g % 2 == 0:
                    c.pupd = PR.next()
                MM(c.pupd.ap[:, (g % 2) * 256:(g % 2 + 1) * 256], c.Bm.ap[:, g * 128:(g + 1) * 128],
                   c.xe.ap[:, g * 256:(g + 1) * 256], True, True, [c.Bm, c.xe], [c.pupd])
                if g % 2 == 1:
                    g0 = g - 1
                    sl = slice(g0 * 256, (g0 + 2) * 256)
                    TT("pool", state.ap[:, sl].rearrange("p (h c) -> p h c", h=8),
                       state.ap[:, sl].rearrange("p (h c) -> p h c", h=8),
                       c.e3.ap[:, 64 + g0 * 4:64 + g0 * 4 + 8].unsqueeze(2).to_broadcast([128, 8, 64]), ALU.mult,
                       [state, c.e3], [state])
                    TT("dve", state.ap[:, sl], state.ap[:, sl], c.pupd.ap, ALU.add, [state, c.pupd], [state])
                    ACT(stateb.ap[:, sl], state.ap[:, sl], AF.Copy, [state], [stateb])

            def ep_stages(c):
                Y, r0, ck = c.Y, c.r0, c.ck
                if c.dirn == 0:
                    return [lambda: DMA(yf_s[r0:r0 + 128, :], Y.ap, [Y], [db("yf", ck)])]
                Z = c.Z
                st_ = {}

                def s1():
                    TT("pool", Y.ap, Y.ap, c.yf.ap, ALU.add, [Y, c.yf], [Y])

                def s2():
                    TT("dve", Y.ap, Y.ap, Z.ap, ALU.mult, [Y, Z], [Y])

                def s3():
                    ssg = ssgr.next()
                    st_["ssg"] = ssg
                    MS("pool", ssg.ap, 0.0, [ssg])
                    for g in range(8):
                        ACT(sq.ap[:, g * 256:(g + 1) * 256], Y.ap[:, g * 256:(g + 1) * 256], AF.Square, [Y, ssg], [sq, ssg],
                            accum_out=ssg.ap[:, g:g + 1])
                    ACT(ssg.ap[:, 8:16], ssg.ap[:, 0:8], AF.Ln, [ssg], [ssg], bias=EPS, scale=1.0 / 256.0)
                    ACT(ssg.ap[:, 16:24], ssg.ap[:, 8:16], AF.Exp, [ssg], [ssg], scale=-0.5)

                def s4():
                    ssg = st_["ssg"]
                    c.yb = ybr.next()
                    for g in range(8):
                        STT("dve", c.yb.ap[:, g * 256:(g + 1) * 256], Y.ap[:, g * 256:(g + 1) * 256], ssg.ap[:, 16 + g:17 + g],
                            ssdn.ap[:, g * 256:(g + 1) * 256], ALU.mult, ALU.mult, [Y, ssg, ssdn], [c.yb])

                def s5():
                    yb = c.yb
                    mt_ = mixT.next()
                    for c4 in range(4):
                        tr = TRR.next()
                        for i in range(4):
                            ct = c4 * 4 + i
                            S.tr(tr.ap[:, i * 128:(i + 1) * 128], yb.ap[:, ct * 128:(ct + 1) * 128], ident.ap,
                                 reads=bs([yb, ident]), writes=bs([tr]))
                        CP("dve", mt_.ap[:, c4 * 4:(c4 + 1) * 4, :], tr.ap.rearrange("p (i t) -> p i t", i=4), [tr], [mt_])
                    DMA(mix_s[:, r0:r0 + 128].rearrange("(i p) t -> p i t", p=128), mt_.ap, [mt_], [db("mix", ck)])

                return [s1, None, s2, None, s3, None, s4, None, s5]

            for dirn in range(2):
                MS("pool", state.ap, 0.0, [state])
                MS("pool", stateb.ap, 0.0, [stateb])
                blocks = list(range(nb)) if dirn == 0 else list(range(nb - 1, -1, -1))
                chunks = []
                for tb in blocks:
                    for j in (range(4) if dirn == 0 else range(3, -1, -1)):
                        c = Ctx()
                        c.dirn, c.tb, c.j = dirn, tb, j
                        chunks.append(c)
                items = [(c, g) for c in chunks for g in range(8)]
                deferred = []
                prologue(items[0][0])
                Wn = front_g(*items[0])
                for idx, (c, g) in enumerate(items):
                    Wc = Wn
                    if idx + 1 < len(items):
                        c2, g2 = items[idx + 1]
                        if g2 == 0:
                            prologue(c2)
                        Wn = front_g(c2, g2)
                    back_g(c, g, Wc)
                    for q_ in deferred:
                        if q_:
                            f_ = q_.pop(0)
                            if f_ is not None:
                                f_()
                    deferred = [q_ for q_ in deferred if q_]
                    if g == 7:
                        deferred.append(ep_stages(c))
                for q_ in deferred:
                    for f_ in q_:
                        if f_ is not None:
                            f_()

        def phase_D(l, si, t0, S_):
            new_phase()
            WR[0] = Ring([AR.alloc([8, 512], BF16, "w%d" % i) for i in range(5)])
            nk = 24 if l == 0 else 16
            hp0 = AR.p
            hid = AR.alloc([32, 512], BF16, "hid")
            sub = Arena(arena_t, AW)
            sub.reset(hp0)
            actT = sub.alloc([nk, 512], BF16, "actT")
            actT.b = hid.b
            d = alloc_front(host=(hp0, 8192, hid.b), with_dt=False)
            ot = AR.alloc([4, 1024], F32, "ot")
            rl = Ring([AR.alloc([512], F32, "rl%d" % i) for i in range(2)])
            xt, hb, hT = d["xt"], d["hb"], d["hT"]
            atm = hb
            g_post_mix = load_gain("norm_post_mix", l)
            g_pre_mlp = load_gain("norm_pre_mlp", l)
            g_post_mlp = load_gain("norm_post_mlp", l)
            if l == 0:
                d["g_pre_mix1"] = load_gain("norm_pre_mix", 1)
            nmix = nk - 8
            ot2 = AR.alloc([4, 1024], F32, "ot2")
            ots = [ot, ot2]
            key = "out0" if l == 0 else "out1"
            nb_ = S_ // TB

            def out_proj(tb, ot):
                c0t = tb * TB
                if l == 0:
                    for c4 in range(4):
                        DMA(actT.ap[:, 0:16, c4 * 128:(c4 + 1) * 128],
                            mix_s[:, c0t + c4 * 128:c0t + (c4 + 1) * 128].rearrange("(i p) t -> p i t", p=128),
                            [db("mix", tb * 4 + c4)], [actT])
                else:
                    DMA(atm.ap, atm_s[c0t:c0t + TB, :].rearrange("(j p) c -> p j c", p=128),
                        [db("atm", (tb, hh)) for hh in range(8)], [atm])
                    transpose_block(atm, actT)
                DMA(actT.ap[:, nmix:nk, :], mo_s[l][:, c0t:c0t + TB].rearrange("(i p) t -> p i t", p=128),
                    [db("mo%d" % l, tb)], [actT])
                for ch in range(2):
                    accs = [PR.next() for _ in range(4)]
                    for kc in range(nk // 8):
                        w = wload(key, kc * 1024, 8, ch * 512, 512)
                        for j in range(4):
                            for kt in range(8):
                                MM(accs[j].ap, actT.ap[:, kc * 8 + kt, j * 128:(j + 1) * 128], w.ap[:, kt, :],
                                   kc == 0 and kt == 0, kc == nk // 8 - 1 and kt == 7, [actT, w], [accs[j]])
                    for j in range(4):
                        evac(ot.ap[:, j, ch * 512:(ch + 1) * 512], accs[j].ap, [accs[j]], [ot])

            out_proj(0, ots[0])
            for tb in range(nb_):
                c0t = tb * TB
                ot = ots[tb % 2]
                if l == 0:
                    DMA(xt.ap, x_in[t0 + c0t:t0 + c0t + TB, :].rearrange("(j p) d -> p j d", p=128), [], [xt])
                else:
                    DMA(xt.ap, x1_s[c0t:c0t + TB, :].rearrange("(j p) d -> p j d", p=128), [db("x1", tb)], [xt])
                tm_norm(ot, g_post_mix, ot, d["ss"], d["sq"])
                TT("dve", xt.ap, xt.ap, ot.ap, ALU.add, [xt, ot], [xt])
                tm_norm(xt, g_pre_mlp, hb, d["ss"], d["sq"])
                transpose_block(hb, hT)
                for fc in range(8):
                    w = wload("w1", l * 1024, 8, fc * 512, 512)
                    for i in range(4):
                        pb = PR.next()
                        for kt in range(8):
                            MM(pb.ap, w.ap[:, kt, i * 128:(i + 1) * 128], hT.ap[:, kt, :], kt == 0, kt == 7, [w, hT], [pb])
                        r = rl.next()
                        ACT(r.ap, pb.ap, AF.Relu, [pb], [r])
                        TT("pool", hid.ap[:, fc * 4 + i, :], r.ap, r.ap, ALU.mult, [r], [hid])
                for ch in range(2):
                    accs = [PR.next() for _ in range(4)]
                    for fc in range(4):
                        w = wload("w2", l * 4096 + fc * 1024, 8, ch * 512, 512)
                        for j in range(4):
                            for ft in range(8):
                                MM(accs[j].ap, hid.ap[:, fc * 8 + ft, j * 128:(j + 1) * 128], w.ap[:, ft, :],
                                   fc == 0 and ft == 0, fc == 3 and ft == 7, [hid, w], [accs[j]])
                    for j in range(4):
                        evac(ot.ap[:, j, ch * 512:(ch + 1) * 512], accs[j].ap, [accs[j]], [ot])
                tm_norm(ot, g_post_mlp, ot, d["ss"], d["sq"])
                TT("dve", xt.ap, xt.ap, ot.ap, ALU.add, [xt, ot], [xt])
                if l == 0:
                    DMA(x1_s[c0t:c0t + TB, :].rearrange("(j p) d -> p j d", p=128), xt.ap, [xt], [db("x1", tb)], eng=STQ)
                    if lvl >= ORDER.index("B1"):
                        front(1, t0, S_, tb, d, stage="a")
                else:
                    DMA(y_out[t0 + c0t:t0 + c0t + TB, :].rearrange("(j p) d -> p j d", p=128), xt.ap, [xt], [db("y", (si, tb))], eng=STQ)
                if tb + 1 < nb_:
                    out_proj(tb + 1, ots[(tb + 1) % 2])
                if l == 0 and lvl >= ORDER.index("B1"):
                    front(1, t0, S_, tb, d, stage="b")

        def phase_B1(si, S_):
            new_phase()
            nkt = S_ // 128
            nqb = S_ // TB
            qh = Ring([AR.alloc([S_], BF16, "qh%d" % i) for i in range(2)])
            kh = Ring([AR.alloc([S_], BF16, "kh%d" % i) for i in range(2)])
            vh = Ring([AR.alloc([nkt, 130], BF16, "vh%d" % i) for i in range(2)])
            bt6 = Ring([AR.alloc([6, 512], F32, "bt6_%d" % i) for i in range(2)])
            Er = Ring([AR.alloc([2, 512], BF16, "ae%d" % i) for i in range(3)])
            tmpr = Ring([AR.alloc([2, 512], F32, "atmp%d" % i) for i in range(2)])
            rr = Ring([AR.alloc([4], F32, "arr%d" % i) for i in range(8)])
            t1 = Ring([AR.alloc([128], F32, "at1_%d" % i) for i in range(4)])
            o_ = Ring([AR.alloc([128], F32, "ao%d" % i) for i in range(8)])
            sqa = AR.alloc([128], F32, "asq")
            ssa = Ring([AR.alloc([4], F32, "assa%d" % i) for i in range(8)])
            ob = Ring([AR.alloc([4, 128], BF16, "aob%d" % i) for i in range(2)])
            accs_r = Ring([AR.alloc([4, 260], F32, "accs%d" % i) for i in range(2)])
            for v in vh.items:
                MS("pool", v.ap[:, :, 128:130], 1.0, [v])
            accb = pbanks[0:4]
            prs = []
            for i in (4, 6):
                t_ = T(pall[:, i * 512:(i + 2) * 512].rearrange("p (c q) -> p c q", c=2), "pair%d" % i)
                t_.b.excl = True
                prs.append(t_)
            scr = Ring(prs)
            heads = {}

            def load_head(h):
                Q = qh.next()
                K = kh.next()
                V = vh.next()
                Bt = bt6.next()
                DMA(Q.ap, q_s[h * 128:(h + 1) * 128, 0:S_], [db("qd", b_) for b_ in range(nqb)], [Q])
                DMA(K.ap, k_s[h * 128:(h + 1) * 128, 0:S_], [db("kd", b_) for b_ in range(nqb)], [K])
                DMA(V.ap[:, :, 0:128], v_s[0:S_, h * 128:(h + 1) * 128].rearrange("(kt p) e -> p kt e", p=128),
                    [db("vd", b_) for b_ in range(nqb)], [V])
                for di in range(6):
                    delta = -128 + 128 * di
                    src = bass.AP(rep_t, h * 128 * RL + 640 - delta, [[RL - 1, 128], [1, 512]])
                    DMA(Bt.ap[:, di, :], src, [db("rep", h)], [Bt])
                heads[h] = (Q, K, V, Bt)

            def emit_scores(h, qb, kt):
                Q, K, V, Bt = heads[h]
                delta = kt * 128 - qb * TB
                pair = scr.next()
                for c in range(2):
                    MM(pair.ap[:, c, :], K.ap[c * 64:(c + 1) * 64, kt * 128:(kt + 1) * 128],
                       Q.ap[c * 64:(c + 1) * 64, qb * TB:(qb + 1) * TB], True, True, [K, Q], [pair])
                e = Er.next()
                if -128 <= delta <= 512:
                    tm = tmpr.next()
                    STT("dve", tm.ap, pair.ap, 0.125, Bt.ap[:, (delta + 128) // 128, :].unsqueeze(1).to_broadcast([128, 2, 512]),
                        ALU.mult, ALU.add, [pair, Bt], [tm])
                    ACT(e.ap, tm.ap, AF.Exp, [tm], [e])
                else:
                    cbias = tab15 if delta < 0 else tab31
                    ACT(e.ap, pair.ap, AF.Exp, [pair, cbias], [e], scale=0.125, bias=cbias.ap[:, h:h + 1])
                return e

            def emit_pv(h, qb, kt, es):
                Q, K, V, Bt = heads[h]
                for jq in range(4):
                    av = accb[jq].ap[:, 0:260].rearrange("p (c e) -> p c e", c=2)
                    for c in range(2):
                        S.mm(av[:, c, :], es.ap[:, c, jq * 128:(jq + 1) * 128], V.ap[:, kt, :],
                             start=(kt == 0 and c == 0), stop=(kt == nkt - 1 and c == 1),
                             reads=bs([es, V]), writes=bs([accb[jq]]), skip_group_check=True)

            def epilogue_stages(h, qb):
                OB = ob.next()
                AS = accs_r.next()
                avs = [AS.ap[:, jq, :].rearrange("p (c e) -> p c e", c=2) for jq in range(4)]
                rs_, oos, sas = [], [], []

                def s0():
                    for jq in range(4):
                        CP("dve", AS.ap[:, jq, :], accb[jq].ap[:, 0:260], [accb[jq]], [AS])

                def s1():
                    for jq in range(4):
                        r = rr.next()
                        S.emit("dve", lambda hh, o=r.ap[:, 0:2], i=avs[jq][:, :, 128]: hh.reciprocal(o, i), bs([AS]), bs([r]))
                        rs_.append(r)
                    for jq in range(4):
                        r = rs_[jq]
                        TT("dve", r.ap[:, 2:3], r.ap[:, 1:2], lamt.ap[:, 1:2], ALU.mult, [r, lamt], [r])

                def s2():
                    for jq in range(4):
                        r = rs_[jq]
                        t_ = t1.next()
                        TS("dve", t_.ap, avs[jq][:, 0, 0:128], r.ap[:, 0:1], None, ALU.mult, None, [AS, r], [t_])
                        oo = o_.next()
                        STT("dve", oo.ap, avs[jq][:, 1, 0:128], r.ap[:, 2:3], t_.ap, ALU.mult, ALU.add, [AS, r, t_], [oo])
                        oos.append(oo)
                        sa = ssa.next()
                        MS("pool", sa.ap, 0.0, [sa])
                        sas.append(sa)

                def s3():
                    for jq in range(4):
                        ACT(sqa.ap, oos[jq].ap, AF.Square, [oos[jq], sas[jq]], [sqa, sas[jq]], accum_out=sas[jq].ap[:, 0:1])
                    for jq in range(4):
                        ACT(sas[jq].ap[:, 1:2], sas[jq].ap[:, 0:1], AF.Ln, [sas[jq]], [sas[jq]], bias=EPS, scale=1.0 / 128.0)
                    for jq in range(4):
                        ACT(sas[jq].ap[:, 2:3], sas[jq].ap[:, 1:2], AF.Exp, [sas[jq]], [sas[jq]], scale=-0.5)

                def s4():
                    for jq in range(4):
                        STT("dve", OB.ap[:, jq, :], oos[jq].ap, sas[jq].ap[:, 2:3], subln.ap, ALU.mult, ALU.mult,
                            [oos[jq], sas[jq], subln], [OB])

                def s5():
                    DMA(atm_s[qb * TB:(qb + 1) * TB, h * 128:(h + 1) * 128].rearrange("(j p) e -> p j e", p=128), OB.ap,
                        [OB], [db("atm", (qb, h))])

                return [s0, s1, None, s2, None, s3, None, s4, None, s5]

            steps = [(h, qb, kt) for h in range(8) for qb in range(nqb) for kt in range(nkt)]
            deferred = []
            load_head(0)
            pend = emit_scores(*steps[0])
            for i, (h, qb, kt) in enumerate(steps):
                if qb == 0 and kt == 0 and h + 1 < 8:
                    load_head(h + 1)
                nxt = emit_scores(*steps[i + 1]) if i + 1 < len(steps) else None
                emit_pv(h, qb, kt, pend)
                for q_ in deferred:
                    if q_:
                        f_ = q_.pop(0)
                        if f_ is not None:
                            f_()
                deferred = [q_ for q_ in deferred if q_]
                if kt == nkt - 1:
                    st_list = epilogue_stages(h, qb)
                    st_list.pop(0)()
                    deferred.append(st_list)
                pend = nxt
            for q_ in deferred:
                for f_ in q_:
                    if f_ is not None:
                        f_()

        t0 = 0
        for si, S_ in enumerate(seqs):
            if lvl >= ORDER.index("KV"):
                phase_KV(si, si * 256)
            if lvl >= ORDER.index("A0"):
                phase_A0(si, t0, S_)
            if lvl >= ORDER.index("B0"):
                phase_B0(si, S_)
            if lvl >= ORDER.index("C0"):
                phase_C0(si, S_)
            if lvl >= ORDER.index("D0"):
                phase_D(0, si, t0, S_)
            if lvl >= ORDER.index("B1"):
                phase_B1(si, S_)
            if lvl >= ORDER.index("D1"):
                phase_D(1, si, t0, S_)
            t0 += S_
        S.finish()
        S.materialize()
        nops = {e: len(S.ops[e]) for e in S.ENGS}
    return nc, nops


def _rel_onehot():
    rel_np = np.arange(640, 640 - RL, -1, dtype=np.int32)
    try:
        import jax
        import jax.numpy as jnp
        with jax.default_device(jax.devices("cpu")[0]):
            rel = jnp.asarray(rel_np)
            nb, max_exact = 16, 8
            ret = jnp.where(rel > 0, nb, 0)
            n = jnp.abs(rel)
            nf = jnp.maximum(n, 1).astype(jnp.float32)
            large = max_exact + (jnp.log(nf / max_exact) / math.log(128 / max_exact) * (nb - max_exact)).astype(jnp.int32)
            large = jnp.minimum(large, nb - 1)
            bucket = np.asarray(ret + jnp.where(n < max_exact, n, large))
    except Exception:
        n = np.abs(rel_np)
        nf = np.maximum(n, 1).astype(np.float32)
        large = 8 + (np.log(nf / np.float32(8)) / np.float32(math.log(16)) * np.float32(8)).astype(np.int32)
        large = np.minimum(large, 15)
        bucket = np.where(rel_np > 0, 16, 0) + np.where(n < 8, n, large)
    oh = np.zeros((32, RL), np.float32)
    oh[bucket, np.arange(RL)] = 1.0
    return oh


_PROG = {}


def _param_maps(inputs):
    m = {}
    for n, shp in PARAMS:
        m[n] = np.ascontiguousarray(np.asarray(inputs[n], dtype=np.float32).reshape(shp))
    m["onehot"] = _rel_onehot()
    return m


def run_cores(seqs, xs, mems, inputs, dbg=False, upto="END"):
    key = (tuple(seqs), dbg, upto)
    if key not in _PROG:
        _PROG[key] = build(list(seqs), dbg=dbg, upto=upto)[0]
    nc = _PROG[key]
    pm = _param_maps(inputs)
    in_maps = []
    for x, mm_ in zip(xs, mems):
        d = dict(pm)
        d["x"] = np.ascontiguousarray(x, dtype=np.float32)
        d["mem"] = np.ascontiguousarray(mm_, dtype=np.float32)
        in_maps.append(d)
    res = run_bass_kernel_spmd(nc, in_maps, core_ids=list(range(len(xs))))
    return res.results


def kernel(**inputs):
    xp = np.asarray(inputs["x_prompt"], dtype=np.float32)
    xs_ = np.asarray(inputs["x_sample"], dtype=np.float32)
    mp = np.asarray(inputs["mem_prompt"], dtype=np.float32)
    ms = np.asarray(inputs["mem_sample"], dtype=np.float32)
    seqs = [4096, 2048, 2048, 2048, 2048]
    xs, mems = [], []
    for c in range(8):
        xs.append(np.concatenate([xp[c], xs_[4 * c:4 * c + 4].reshape(-1, D)], axis=0))
        mems.append(np.concatenate([mp[c], ms[4 * c:4 * c + 4].reshape(-1, D)], axis=0))
    res = run_cores(seqs, xs, mems, inputs)
    y_prompt = np.empty_like(xp)
    y_sample = np.empty_like(xs_)
    for c in range(8):
        y = np.asarray(res[c]["y"], dtype=np.float32)
        y_prompt[c] = y[0:4096]
        y_sample[4 * c:4 * c + 4] = y[4096:].reshape(4, 2048, D)
    return (y_prompt, y_sample)
```
